# Optimizing a Trainium2 kernel written in Bass

```python
import math
import jax, jax.numpy as jnp
from jax import lax
import numpy as np

D_MODEL = 1024
BATCH = 2
SEQ = 8192
DEPTH = 2
DEC_BATCH = 128
DEC_SEQ = 4
PAST_LEN = 2048
PAGE_SIZE = 128

N_EVEN = (DEPTH + 1) // 2
N_ODD = DEPTH // 2
PLE_DIM = 256
D_FF = 4 * D_MODEL
EPS = 1e-6
ROPE_THETA = 10000.0

POOL_WINDOWS = (2, 4, 8, 16)
POOL_GROUP = D_MODEL // 8
POOL_WIDTH = POOL_GROUP * len(POOL_WINDOWS)
POOL_STATE = max(POOL_WINDOWS) - 1
DIL_CFG = ((128, 1), (512, 4), (2048, 16))
B_HEADS = 8
B_HEAD_DIM = 64
B_WIDTH = B_HEADS * B_HEAD_DIM
ATT_BLOCK = 128
CHUNK = 128
C_GROUPS = 4
C_WIDTH = D_MODEL // 2
C_GROUP_DIM = C_WIDTH // C_GROUPS
CONV_W = 3
D_WIDTH = D_MODEL // 2

EV_IN = POOL_WIDTH + 3 * len(DIL_CFG) * B_WIDTH
EV_OUT = POOL_WIDTH + B_WIDTH
OD_IN = 2 * C_WIDTH + 3 * D_WIDTH
OD_OUT = C_WIDTH + D_WIDTH

kernel_name = "hybrid_pool_dilattn_gmlp_shortconv_step"


def rmsnorm(x, g):
    xf = x.astype(jnp.float32)
    y = xf * lax.rsqrt(jnp.mean(xf * xf, axis=-1, keepdims=True) + EPS)
    return (y * g.astype(jnp.float32)).astype(x.dtype)


def rope(x, pos):
    half = x.shape[-1] // 2
    inv = jnp.power(jnp.float32(ROPE_THETA), -jnp.arange(half, dtype=jnp.float32) / half)
    ang = pos.astype(jnp.float32)[:, None] * inv[None, :]
    cos = jnp.cos(ang)[None, :, None, :]
    sin = jnp.sin(ang)[None, :, None, :]
    xf = x.astype(jnp.float32)
    x1, x2 = xf[..., :half], xf[..., half:]
    return jnp.concatenate([x1 * cos - x2 * sin, x2 * cos + x1 * sin], axis=-1).astype(x.dtype)


def causal_pool_mix(ext, n_ctx, pos, pool_w, pool_scale):
    N, Le, _ = ext.shape
    L = Le - n_ctx
    xf = ext.astype(jnp.float32)
    csum = jnp.concatenate([jnp.zeros((N, 1, POOL_WIDTH), jnp.float32), jnp.cumsum(xf, axis=1)], axis=1)
    end = n_ctx + 1 + jnp.arange(L)
    outs = []
    for gi, w in enumerate(POOL_WINDOWS):
        sl = slice(gi * POOL_GROUP, (gi + 1) * POOL_GROUP)
        cg = csum[..., sl]
        wsum = cg[:, end] - cg[:, jnp.maximum(end - w, 0)]
        cnt = jnp.minimum(w, pos + 1).astype(jnp.float32)
        outs.append(wsum / cnt[None, :, None] - xf[:, n_ctx:, sl])
    pooled = jnp.stack(outs, axis=2)
    mixed = jnp.einsum('nlgc,gcd->nlgd', pooled, pool_w.astype(jnp.float32))
    return (mixed.reshape(N, L, POOL_WIDTH) * pool_scale.astype(jnp.float32)).astype(ext.dtype)


def dilated_attn_prompt(q, k, v, window, dil):
    N, S, H, Dh = q.shape
    span = window // dil
    unit = dil * ATT_BLOCK
    Sp = -(-S // unit) * unit
    M = Sp // dil
    nb = M // ATT_BLOCK

    def to_blocks(t):
        t = jnp.pad(t, ((0, 0), (0, Sp - S), (0, 0), (0, 0)))
        t = t.reshape(N, M, dil, H, Dh).transpose(0, 2, 1, 3, 4)
        return t.reshape(N, dil, nb, ATT_BLOCK, H, Dh)

    def with_prev(t):
        prev = jnp.pad(t, ((0, 0), (0, 0), (1, 0), (0, 0), (0, 0), (0, 0)))[:, :, :-1]
        return jnp.concatenate([prev, t], axis=3)

    qb = to_blocks(q)
    kk = with_prev(to_blocks(k))
    vv = with_prev(to_blocks(v))
    s = jnp.einsum('nrbqhd,nrbkhd->nrbhqk', qb, kk, preferred_element_type=jnp.float32) * (Dh ** -0.5)
    qi = jnp.arange(ATT_BLOCK)[:, None]
    ki = jnp.arange(2 * ATT_BLOCK)[None, :]
    rel = qi + ATT_BLOCK - ki
    blk = jnp.arange(nb)[:, None, None]
    valid = (rel >= 0) & (rel <= span) & (blk * ATT_BLOCK + ki - ATT_BLOCK >= 0)
    s = jnp.where(valid[:, None], s, -jnp.inf)
    m = jnp.max(s, axis=-1, keepdims=True)
    e = jnp.exp(s - m)
    l = jnp.sum(e, axis=-1, keepdims=True)
    o = jnp.einsum('nrbhqk,nrbkhd->nrbqhd', e / l, vv.astype(jnp.float32))
    lse = (m + jnp.log(l))[..., 0]
    o = o.reshape(N, dil, M, H, Dh).transpose(0, 2, 1, 3, 4).reshape(N, Sp, H, Dh)[:, :S]
    lse = lse.transpose(0, 1, 2, 4, 3).reshape(N, dil, M, H).transpose(0, 2, 1, 3).reshape(N, Sp, H)[:, :S]
    return o, lse


def dilated_attn_sample(q, k_new, v_new, kv_buf, window, dil):
    N, T, H, Dh = q.shape
    Lb = kv_buf.shape[1]
    span = window // dil
    kc = jnp.concatenate([kv_buf[:, :, 0].astype(k_new.dtype), k_new], axis=1)
    vc = jnp.concatenate([kv_buf[:, :, 1].astype(v_new.dtype), v_new], axis=1)
    idx = Lb + jnp.arange(T)[:, None] - dil * jnp.arange(span + 1)[None, :]
    valid = idx >= 0
    idx_c = jnp.maximum(idx, 0)
    kg = kc[:, idx_c]
    vg = vc[:, idx_c]
    s = jnp.einsum('nthd,ntjhd->nthj', q, kg, preferred_element_type=jnp.float32) * (Dh ** -0.5)
    s = jnp.where(valid[:, None, :], s, -jnp.inf)
    m = jnp.max(s, axis=-1, keepdims=True)
    e = jnp.exp(s - m)
    l = jnp.sum(e, axis=-1, keepdims=True)
    o = jnp.einsum('nthj,ntjhd->nthd', e / l, vg.astype(jnp.float32))
    return o, (m + jnp.log(l))[..., 0]


def even_mixer(h, pos, pool_ctx, kv_bufs, w_in, pool_w, pool_scale, w_out):
    N, L, _ = h.shape
    z = jnp.einsum('nld,de->nle', h, w_in)
    a = z[..., :POOL_WIDTH]
    qkv = z[..., POOL_WIDTH:].reshape(N, L, len(DIL_CFG), 3, B_HEADS, B_HEAD_DIM)
    if pool_ctx is None:
        ext, n_ctx = a, 0
    else:
        ext, n_ctx = jnp.concatenate([pool_ctx.astype(a.dtype), a], axis=1), pool_ctx.shape[1]
    ya = causal_pool_mix(ext, n_ctx, pos, pool_w, pool_scale)
    outs, lses, new_kv = [], [], []
    for g, (win, dil) in enumerate(DIL_CFG):
        q = rope(qkv[:, :, g, 0], pos)
        k = rope(qkv[:, :, g, 1], pos)
        v = qkv[:, :, g, 2]
        if kv_bufs is None:
            o, lse = dilated_attn_prompt(q, k, v, win, dil)
            keep = min(win, L)
            new_kv.append(jnp.stack([k[:, L - keep:], v[:, L - keep:]], axis=2))
        else:
            o, lse = dilated_attn_sample(q, k, v, kv_bufs[g], win, dil)
            new_kv.append(jnp.stack([k, v], axis=2))
        outs.append(o)
        lses.append(lse)
    wts = jax.nn.softmax(jnp.stack(lses, axis=0), axis=0)
    yb = jnp.einsum('gnlh,gnlhd->nlhd', wts, jnp.stack(outs, axis=0)).reshape(N, L, B_WIDTH).astype(h.dtype)
    y = jnp.einsum('nle,ed->nld', jnp.concatenate([ya, yb], axis=-1), w_out)
    return y, new_kv, ext[:, -POOL_STATE:]


def odd_mixer(h, pos, conv_ctx, is_prompt, w_in, ln_g, ln_b, ws, bs, conv_w, w_out):
    N, L, _ = h.shape
    z = jnp.einsum('nld,de->nle', h, w_in)
    u = jax.nn.gelu(z[..., :C_WIDTH])
    v = jax.nn.gelu(z[..., C_WIDTH:2 * C_WIDTH])
    vg = v.reshape(N, L, C_GROUPS, C_GROUP_DIM).astype(jnp.float32)
    mu = jnp.mean(vg, axis=-1, keepdims=True)
    var = jnp.mean(jnp.square(vg - mu), axis=-1, keepdims=True)
    vn = ((vg - mu) * lax.rsqrt(var + EPS)).reshape(N, L, C_WIDTH) * ln_g.astype(jnp.float32) + ln_b.astype(jnp.float32)
    vn = vn.astype(h.dtype)
    vn_g = vn.reshape(N, L, C_GROUPS, C_GROUP_DIM)
    w_mask = ws * jnp.tril(jnp.ones((CHUNK, CHUNK), ws.dtype))
    if is_prompt:
        vc = vn_g.reshape(N, L // CHUNK, CHUNK, C_GROUPS, C_GROUP_DIM)
        sp = jnp.einsum('gts,nksgc->nktgc', w_mask, vc) + bs.T[None, None, :, :, None]
        sp = sp.reshape(N, L, C_WIDTH)
    else:
        local = pos % CHUNK
        same = (pos[:, None] // CHUNK) == (pos[None, :] // CHUNK)
        mix = w_mask[:, local[:, None], local[None, :]] * same.astype(ws.dtype)
        sp = jnp.einsum('gts,nsgc->ntgc', mix, vn_g) + bs[:, local].T[None, :, :, None]
        sp = sp.reshape(N, L, C_WIDTH)
    yc = u * sp.astype(u.dtype)
    o0 = 2 * C_WIDTH
    g_out = z[..., o0:o0 + D_WIDTH]
    g_in = z[..., o0 + D_WIDTH:o0 + 2 * D_WIDTH]
    xin = z[..., o0 + 2 * D_WIDTH:]
    hd = g_in * xin
    ctx = jnp.zeros((N, CONV_W - 1, D_WIDTH), hd.dtype) if conv_ctx is None else conv_ctx.astype(hd.dtype)
    ext = jnp.concatenate([ctx, hd], axis=1)
    conv = sum(conv_w[j] * ext[:, j:j + L] for j in range(CONV_W))
    yd = g_out * conv
    y = jnp.einsum('nle,ed->nld', jnp.concatenate([yc, yd], axis=-1), w_out)
    return y, ext[:, -(CONV_W - 1):], vn


def setup_inputs(seed: int = 0) -> dict:
    key = jax.random.key(seed)
    ks = jax.random.split(key, 32)

    def nrm(i, shape, scale=1.0):
        return jax.random.normal(ks[i], shape, jnp.float32) * scale

    d = D_MODEL
    buf = [min(w, PAST_LEN) for w, _ in DIL_CFG]
    return {
        "x_prompt": nrm(0, (BATCH, SEQ, d)),
        "x_sample": nrm(1, (DEC_BATCH, DEC_SEQ, d)),
        "cache_kv_w128": nrm(2, (N_EVEN, DEC_BATCH, buf[0], 2, B_HEADS, B_HEAD_DIM)),
        "cache_kv_w512": nrm(3, (N_EVEN, DEC_BATCH, buf[1], 2, B_HEADS, B_HEAD_DIM)),
        "cache_kv_w2048": nrm(4, (N_EVEN, DEC_BATCH, buf[2], 2, B_HEADS, B_HEAD_DIM)),
        "state_pool": nrm(5, (N_EVEN, DEC_BATCH, POOL_STATE, POOL_WIDTH)),
        "state_conv": nrm(6, (N_ODD, DEC_BATCH, CONV_W - 1, D_WIDTH)),
        "p_prompt": nrm(7, (DEPTH, BATCH, SEQ, PLE_DIM)),
        "p_sample": nrm(8, (DEPTH, DEC_BATCH, DEC_SEQ, PLE_DIM)),
        "ev_w_in": nrm(9, (N_EVEN, d, EV_IN), d ** -0.5),
        "ev_pool_w": nrm(10, (N_EVEN, len(POOL_WINDOWS), POOL_GROUP, POOL_GROUP), POOL_GROUP ** -0.5),
        "ev_pool_scale": 1.0 + nrm(11, (N_EVEN, POOL_WIDTH), 0.01),
        "ev_w_out": nrm(12, (N_EVEN, EV_OUT, d), EV_OUT ** -0.5),
        "od_w_in": nrm(13, (N_ODD, d, OD_IN), d ** -0.5),
        "od_ln_g": 1.0 + nrm(14, (N_ODD, C_WIDTH), 0.01),
        "od_ln_b": nrm(15, (N_ODD, C_WIDTH), 0.01),
        "od_ws": nrm(16, (N_ODD, C_GROUPS, CHUNK, CHUNK), CHUNK ** -0.5),
        "od_bs": 1.0 + nrm(17, (N_ODD, C_GROUPS, CHUNK), 0.01),
        "od_conv_w": nrm(18, (N_ODD, CONV_W, D_WIDTH), CONV_W ** -0.5),
        "od_w_out": nrm(19, (N_ODD, OD_OUT, d), OD_OUT ** -0.5),
        "norm_mix": 1.0 + nrm(20, (DEPTH, d), 0.01),
        "norm_ffn": 1.0 + nrm(21, (DEPTH, d), 0.01),
        "norm_ple": 1.0 + nrm(22, (DEPTH, d), 0.01),
        "ffn_w1": nrm(23, (DEPTH, d, D_FF), d ** -0.5),
        "ffn_w2": nrm(24, (DEPTH, D_FF, d), D_FF ** -0.5),
        "ple_w_proj": nrm(25, (DEPTH, PLE_DIM, d), PLE_DIM ** -0.5),
        "ple_w_gate": nrm(26, (DEPTH, d, d), d ** -0.5),
        "norm_final": 1.0 + nrm(27, (d,), 0.01),
    }


def reference(x_prompt, x_sample, cache_kv_w128, cache_kv_w512, cache_kv_w2048, state_pool, state_conv,
              p_prompt, p_sample, ev_w_in, ev_pool_w, ev_pool_scale, ev_w_out, od_w_in, od_ln_g, od_ln_b,
              od_ws, od_bs, od_conv_w, od_w_out, norm_mix, norm_ffn, norm_ple, ffn_w1, ffn_w2,
              ple_w_proj, ple_w_gate, norm_final):
    def run(x, p, pos, cache_kv, st_pool, st_conv):
        is_prompt = cache_kv is None
        r = x
        new_kv = [[] for _ in DIL_CFG]
        new_pool, new_conv, new_cv = [], [], []
        for i in range(DEPTH):
            h = rmsnorm(r, norm_mix[i])
            if i % 2 == 0:
                e = i // 2
                bufs = None if is_prompt else tuple(c[e] for c in cache_kv)
                pctx = None if is_prompt else st_pool[e]
                y, kvs, pst = even_mixer(h, pos, pctx, bufs, ev_w_in[e], ev_pool_w[e], ev_pool_scale[e], ev_w_out[e])
                for g in range(len(DIL_CFG)):
                    new_kv[g].append(kvs[g])
                new_pool.append(pst)
            else:
                o = i // 2
                cctx = None if is_prompt else st_conv[o]
                y, cst, cv = odd_mixer(h, pos, cctx, is_prompt, od_w_in[o], od_ln_g[o], od_ln_b[o],
                                       od_ws[o], od_bs[o], od_conv_w[o], od_w_out[o])
                new_conv.append(cst)
                new_cv.append(cv)
            r = r + y
            h = rmsnorm(r, norm_ffn[i])
            r = r + jnp.einsum('nlf,fd->nld', jnp.square(jax.nn.relu(jnp.einsum('nld,df->nlf', h, ffn_w1[i]))), ffn_w2[i])
            h = rmsnorm(r, norm_ple[i])
            gate = jax.nn.sigmoid(jnp.einsum('nld,de->nle', h, ple_w_gate[i]))
            r = r + gate * jnp.einsum('nlp,pd->nld', p[i], ple_w_proj[i])
        out = rmsnorm(r, norm_final)
        kv_out = [jnp.stack(lst, axis=0) for lst in new_kv]
        return out, kv_out, jnp.stack(new_pool, axis=0), jnp.stack(new_conv, axis=0), jnp.stack(new_cv, axis=0)

    pos_p = jnp.arange(SEQ)
    pos_s = PAST_LEN + jnp.arange(DEC_SEQ)
    y_prompt, kv_p, pool_p, conv_p, _ = run(x_prompt, p_prompt, pos_p, None, None, None)
    y_sample, kv_s, pool_s, conv_s, cv_s = run(x_sample, p_sample, pos_s,
                                               (cache_kv_w128, cache_kv_w512, cache_kv_w2048),
                                               state_pool, state_conv)
    return (y_prompt, y_sample, kv_p[0], kv_p[1], kv_p[2], kv_s[0], kv_s[1], kv_s[2],
            pool_p, pool_s, conv_p, conv_s, cv_s)
```

```python
import numpy as np
import concourse.bass as bass
import concourse.mybir as mybir
from concourse.bass_utils import run_bass_kernel_spmd

F32 = mybir.dt.float32
BF16 = mybir.dt.bfloat16
ALU = mybir.AluOpType
AF = mybir.ActivationFunctionType

NCORES = 8
SEQ = 8192
SEG = 2048
TT = 512
NT = 9
NS = 64
EPS = 1e-6
MASKW = 3072


class Buf:
    __slots__ = ("name", "w", "r")

    def __init__(self, name=""):
        self.name = name
        self.w = None
        self.r = {}


class Eng:
    def __init__(self, key, sem):
        self.key = key
        self.sem = sem
        self.count = 0
        self.waited = {}
        self.prog = []


class Ctx:
    COMPUTE = ("pe", "act", "dve", "pool")

    def __init__(self, nc, n_dma_sems=8):
        self.nc = nc
        self.sems = {}
        self.engs = {}
        self._stack = []
        for k in ("pe", "act", "dve", "pool", "sp"):
            self._sem("e_" + k)
            self.engs[k] = Eng(k, "e_" + k)
        self.dma_pool = {}
        self.dma_rr = {}
        self.dma_val = {}
        for q in ("sp", "act", "pool"):
            names = []
            for i in range(n_dma_sems):
                nm = "d_%s_%d" % (q, i)
                self._sem(nm)
                names.append(nm)
                self.dma_val[nm] = 0
            self.dma_pool[q] = names
            self.dma_rr[q] = 0
        self.out_events = []
        self.n_wait = 0
        self.n_op = 0

    def _sem(self, name):
        cm = self.nc.semaphore(name)
        h = cm.__enter__()
        self._stack.append(cm)
        self.sems[name] = h
        return h

    def sb(self, name, shape, dtype):
        cm = self.nc.sbuf_tensor("sb_" + name, shape, dtype)
        t = cm.__enter__()
        self._stack.append(cm)
        return t

    def ps(self, name, shape, dtype):
        cm = self.nc.psum_tensor("ps_" + name, shape, dtype)
        t = cm.__enter__()
        self._stack.append(cm)
        return t

    def _deps(self, reads, writes):
        deps = {}
        for b in reads:
            if b.w is not None:
                k, v = b.w
                if deps.get(k, 0) < v:
                    deps[k] = v
        for b in writes:
            if b.w is not None:
                k, v = b.w
                if deps.get(k, 0) < v:
                    deps[k] = v
            for k, v in b.r.items():
                if deps.get(k, 0) < v:
                    deps[k] = v
        return deps

    def _emit_waits(self, e, deps, skip_own=False):
        for k, v in deps.items():
            if skip_own and k == e.sem:
                continue
            if e.waited.get(k, 0) >= v:
                continue
            e.waited[k] = v
            e.prog.append(("wait", self.sems[k], v))
            self.n_wait += 1

    def _record(self, ev, reads, writes):
        k, v = ev
        for b in reads:
            if b.r.get(k, 0) < v:
                b.r[k] = v
        for b in writes:
            b.w = ev
            b.r = {}

    def op(self, eng, fn, reads=(), writes=(), inc=True):
        e = self.engs[eng]
        deps = self._deps(reads, writes)
        self._emit_waits(e, deps, skip_own=(eng == "pe"))
        if inc:
            e.count += 1
            ev = (e.sem, e.count)
            e.prog.append(("op", fn, self.sems[e.sem], 1))
        else:
            ev = (e.sem, e.count + 1)
            e.prog.append(("op", fn, None, 0))
        self._record(ev, reads, writes)
        self.n_op += 1
        return ev

    def dma(self, q, out, in_, reads=(), writes=(), is_output=False):
        e = self.engs[q]
        deps = self._deps(reads, writes)
        names = self.dma_pool[q]
        nm = names[self.dma_rr[q] % len(names)]
        self.dma_rr[q] += 1
        prev = self.dma_val[nm]
        if prev > 0 and deps.get(nm, 0) < prev:
            deps[nm] = prev
        self._emit_waits(e, deps)
        val = prev + 16
        self.dma_val[nm] = val
        ev = (nm, val)

        def fn(engine, out=out, in_=in_):
            return engine.dma_start(out=out, in_=in_)
        e.prog.append(("op", fn, self.sems[nm], 16))
        self._record(ev, reads, writes)
        if is_output:
            self.out_events.append(ev)
        self.n_op += 1
        return ev

    def handoff(self, old, new):
        merged = {}
        for b in old:
            if b.w is not None:
                k, v = b.w
                if merged.get(k, 0) < v:
                    merged[k] = v
            for k, v in b.r.items():
                if merged.get(k, 0) < v:
                    merged[k] = v
        for b in new:
            b.w = None
            b.r = dict(merged)

    def finish(self):
        e = self.engs["sp"]
        deps = {}
        for k, v in self.out_events:
            deps[k] = max(deps.get(k, 0), v)
        for kk in self.COMPUTE:
            ee = self.engs[kk]
            if ee.count > 0:
                deps[ee.sem] = ee.count
        for nm, v in self.dma_val.items():
            if v > 0:
                deps[nm] = max(deps.get(nm, 0), v)
        self._emit_waits(e, deps)
        nc = self.nc
        engs = self.engs

        def replay(engine, prog):
            for item in prog:
                if item[0] == "wait":
                    engine.wait_ge(item[1], item[2])
                else:
                    _, fn, sem, n = item
                    ins = fn(engine)
                    if sem is not None:
                        ins.then_inc(sem, n)

        with nc.Block() as block:
            @block.sync
            def _(eng):
                replay(eng, engs["sp"].prog)

            @block.tensor
            def _(eng):
                replay(eng, engs["pe"].prog)

            @block.scalar
            def _(eng):
                replay(eng, engs["act"].prog)

            @block.vector
            def _(eng):
                replay(eng, engs["dve"].prog)

            @block.gpsimd
            def _(eng):
                replay(eng, engs["pool"].prog)

    def close(self):
        while self._stack:
            cm = self._stack.pop()
            cm.__exit__(None, None, None)


def _wblocks():
    L = [(("ev_in", "a"), "ev_w_in", 0, 0, 512, 8)]
    for g in range(3):
        base = 512 + g * 1536
        for i, nm in enumerate(("q", "k", "v")):
            L.append((("ev_in", g, nm), "ev_w_in", 0, base + 512 * i, 512, 8))
    for j in range(2):
        L.append((("ev_out", j), "ev_w_out", 0, 512 * j, 512, 8))
    names = ["u", "v", "go", "gi", "xi"]
    for j in range(5):
        L.append((("od_in", names[j]), "od_w_in", 0, 512 * j, 512, 8))
    for j in range(2):
        L.append((("od_out", j), "od_w_out", 0, 512 * j, 512, 8))
    for l in range(2):
        for j in range(8):
            L.append((("w1", l, j), "ffn_w1", l, 512 * j, 512, 8))
        for j in range(8):
            L.append((("w2", l, j), "ffn_w2", l, 128 * j, 128, 32))
        L.append((("proj", l), "ple_w_proj", l, 0, 1024, 2))
        for j in range(2):
            L.append((("gate", l, j), "ple_w_gate", l, 512 * j, 512, 8))
    return L


_DBG = {}


class _Stop(Exception):
    pass


PHASES = []


def ck(name):
    if _DBG.get("stop") == name:
        raise _Stop()


def build_program():
    nc = bass.Bass("TRN2", target_bir_lowering=False)
    c = Ctx(nc)

    def din(name, shape):
        return nc.dram_tensor(name, list(shape), F32, kind="ExternalInput").ap()

    def dout(name, shape):
        return nc.dram_tensor(name, list(shape), F32, kind="ExternalOutput").ap()

    WB = _wblocks()
    WIDX = {k[0]: i for i, k in enumerate(WB)}
    d_wall = din("wall", [len(WB), 128, 4096])
    d_pool_w = din("pool_w", [128, 4, 128])
    d_wsT = din("od_wsT", [128, 4, 128])
    d_vecs = din("vecs", [128, 80])
    d_bsT = din("bsT", [128, 4, 512])
    d_bsS = din("bsS", [128, 4, 64])
    d_wsS = din("wsS", [128, 4, 16])
    d_ident = din("ident", [128, 128])
    d_rotm = din("rotm", [128, 128])
    d_tri = din("trimask", [128, 128])
    d_maskSc = din("maskSc", [128, 96])
    d_maskSn = din("maskSn", [128, 3, 512])
    d_xT = din("xT", [1024, NT * TT])
    d_pT = din("pT", [2, 256, 5 * TT])
    d_cs = din("cs", [128, 2, NT * TT + NS])
    d_mask = din("mask", [5, 128, MASKW])
    d_icnt = din("icnt", [5, 128, 4, TT])
    d_xsT = din("xsT", [1024, NS])
    d_psT = din("psT", [2, 256, NS])
    d_kc = din("kc", [16, 128, 9 * 4 * 128])
    d_vc = din("vc", [16, 128, 9 * 512])
    d_spool = din("spool", [128, 960])
    d_sconv = din("sconv", [128, 128])
    o_yT = dout("o_yT", [1024, SEG])
    o_ysT = dout("o_ysT", [1024, NS])
    o_kvT = dout("o_kvT", [3, 2, 512, SEG])
    o_kvsT = dout("o_kvsT", [3, 2, 512, NS])
    o_poolT = dout("o_poolT", [128, 4, 15])
    o_poolsT = dout("o_poolsT", [128, 960])
    o_convT = dout("o_convT", [128, 4, 2])
    o_convsT = dout("o_convsT", [128, 128])
    o_cvsT = dout("o_cvsT", [512, NS])
    o_dbg = dout("o_dbg", [12, 1024, NS]) if _DBG.get("dump") else None

    rT = c.sb("rT", [128, 8, TT], F32)
    hT = c.sb("hT", [128, 8, TT], BF16)
    sq = c.sb("sq", [128, 2, TT], BF16)
    zb = c.sb("zb", [128, 2, TT], BF16)
    scr = c.sb("scr", [128, 3, TT], F32)
    cs = c.sb("cs_s", [128, 2, TT], F32)
    a_ext = c.sb("a_ext", [128, 4, 528], F32)
    pTs = c.sb("pTs_s", [128, 2, TT], BF16)
    msk = c.sb("msk_s", [128, MASKW], BF16)
    NW = 4
    wsl = c.sb("wsl", [128, NW, 4096], BF16)
    ident = c.sb("ident", [128, 128], BF16)
    rotm = c.sb("rotm_s", [128, 128], BF16)
    ones = c.sb("ones", [128, 128], BF16)
    onesn = c.sb("onesn", [128, 128], BF16)
    ones128 = c.sb("ones128", [128, 128], BF16)
    zeros = c.sb("zeros", [128, 128], BF16)
    vecs = c.sb("vecs_s", [128, 80], F32)
    bsT = c.sb("bsT_s", [128, 4, TT], BF16)
    bsS = c.sb("bsS_s", [128, 4, NS], F32)
    wsS = c.sb("wsS_s", [128, 4, 16], F32)
    wmT = c.sb("wmT", [128, 4, 128], BF16)
    wsf = c.sb("wsf", [128, 4, 128], F32)
    trim = c.sb("trim", [128, 128], F32)
    poolw = c.sb("poolw", [128, 4, 128], BF16)
    hd_halo = c.sb("hd_halo", [128, 4, 2], F32)
    maskSc = c.sb("maskSc_s", [128, 96], BF16)
    maskSn = c.sb("maskSn_s", [128, 3, 512], BF16)
    arena = c.sb("arena", [128, 16896], BF16)
    kvr = c.sb("kvr", [128, 36864], BF16)

    mmps = [c.ps("mm0", [128, 512], F32), c.ps("mm1", [128, 512], F32)]
    sps = [c.ps("s0", [128, 512], F32), c.ps("s1", [128, 512], F32)]
    nump = c.ps("nump", [128, 512], F32)
    denp = c.ps("denp", [128, 512], F32)
    auxp = c.ps("auxp", [128, 512], F32)
    trp = c.ps("trp", [128, 1024], BF16)

    del PHASES[:]

    def ph(name):
        PHASES.append((name, sum(1 for it in c.engs["pe"].prog if it[0] == "op")))

    B = {}

    def bf(name):
        if name not in B:
            B[name] = Buf(name)
        return B[name]

    def aview(off, shape, dtype=BF16):
        n = 1
        for s in shape[1:]:
            n *= s
        if dtype == F32:
            ap = arena[:, off:off + 2 * n].bitcast(F32)
        else:
            ap = arena[:, off:off + n]
        if len(shape) == 3:
            ap = ap.rearrange("p (a b) -> p a b", a=shape[1])
        elif len(shape) == 4:
            ap = ap.rearrange("p (a b c) -> p a b c", a=shape[1], b=shape[2])
        return ap

    def kview(off, shape):
        n = 1
        for s in shape[1:]:
            n *= s
        ap = kvr[:, off:off + n]
        if len(shape) == 3:
            ap = ap.rearrange("p (a b) -> p a b", a=shape[1])
        elif len(shape) == 4:
            ap = ap.rearrange("p (a b c) -> p a b c", a=shape[1], b=shape[2])
        return ap

    KT0 = kview(0, [128, 4, 2 * TT])
    KT1 = kview(4096, [128, 4, 2 * TT])
    KT2 = kview(8192, [128, 4, 5 * TT])
    VK0 = kview(18432, [128, 2, 4, 512])
    VK1 = kview(22528, [128, 2, 4, 512])
    V2A = kview(26624, [128, 16, 512])
    V2B = kview(34816, [128, 4, 512])

    wseq = []

    def wblock(key):
        i = WIDX[key]
        _, _, _, _, ncols, kcn = WB[i]
        wseq.append((key, i, kcn, ncols))

    def sched_l0(kind, t):
        if kind == "kv":
            if t == 3:
                wblock(("ev_in", "a"))
                gs = (0, 1, 2)
            else:
                gs = (2,)
            for g in gs:
                wblock(("ev_in", g, "k"))
                wblock(("ev_in", g, "v"))
            return
        wblock(("ev_in", "a"))
        for g in range(3):
            for nm in ("q", "k", "v"):
                wblock(("ev_in", g, nm))
        for j in range(2):
            wblock(("ev_out", j))
        sched_ffn_ple(0)

    def sched_ffn_ple(l):
        for j in range(8):
            wblock(("w1", l, j))
        for j in range(8):
            wblock(("w2", l, j))
        wblock(("proj", l))
        for j in range(2):
            wblock(("gate", l, j))

    def sched_l1(kind):
        names = ["u", "v", "go", "gi", "xi"]
        if kind == "halo":
            for j in (3, 4):
                wblock(("od_in", names[j]))
            return
        for j in range(5):
            wblock(("od_in", names[j]))
        for j in range(2):
            wblock(("od_out", j))
        sched_ffn_ple(1)

    tiles = [("kv", t) for t in range(4)] + [("full", t) for t in range(4, 9)] + [("sample", 9)]
    if _DBG.get("tiles") is not None:
        tiles = [tiles[i] for i in _DBG["tiles"]]
    for kind, t in tiles:
        if kind == "kv":
            sched_l0("kv", t)
        else:
            sched_l0("full", t)
            if kind == "full" and t == 4:
                sched_l1("halo")
            else:
                sched_l1("full")

    wstate = {"issued": 0, "next": 0}

    def w_issue_upto(n):
        while wstate["issued"] < min(n, len(wseq)):
            i = wstate["issued"]
            key, bi, kcn, ncols = wseq[i]
            slot = i % NW
            c.dma("pool", wsl[:, slot, 0:kcn * ncols], d_wall[bi, :, 0:kcn * ncols], writes=[bf("w%d" % slot)])
            wstate["issued"] += 1

    def getw(key, oldest=None):
        i = wstate["next"]
        assert wseq[i][0] == key, (wseq[i][0], key)
        _, bi, kcn, ncols = wseq[i]
        w_issue_upto((i if oldest is None else oldest) + NW - 1)
        slot = i % NW
        wstate["next"] += 1
        view = wsl[:, slot, 0:kcn * ncols].rearrange("p (k n) -> p k n", k=kcn)
        return view, bf("w%d" % slot)

    def w_advance():
        w_issue_upto(wstate["next"] + NW)

    rr = {"mm": 0, "s": 0, "sq": 0, "zb": 0, "scr": 0, "P": 0, "ev": 0, "trp": 0}

    WIDE = [(mmps[0], "mm0"), (mmps[1], "mm1"), (sps[0], "s0"), (sps[1], "s1"), (nump, "num"), (denp, "den")]
    SB4 = [(sps[0], "s0"), (sps[1], "s1"), (mmps[0], "mm0"), (mmps[1], "mm1")]

    def mm_bank():
        i = rr["mm"] % len(WIDE)
        rr["mm"] += 1
        return WIDE[i][0], bf(WIDE[i][1])

    def s_bank():
        i = rr["s"] % len(SB4)
        rr["s"] += 1
        return SB4[i][0], bf(SB4[i][1])

    def scr_buf():
        i = rr["scr"] % 3
        rr["scr"] += 1
        return scr[:, i, :], bf("scr%d" % i)

    def zb_buf():
        i = rr["zb"] % 2
        rr["zb"] += 1
        return zb[:, i, :], bf("zb%d" % i)

    def sq_buf():
        i = rr["sq"] % 2
        rr["sq"] += 1
        return sq[:, i, :], bf("sq%d" % i)

    def trp_half():
        i = rr["trp"] % 2
        rr["trp"] += 1
        return trp[:, 512 * i:512 * i + 512], bf("trp")

    def ev_eng():
        rr["ev"] += 1
        if _DBG.get("evac"):
            return _DBG["evac"]
        return "act" if rr["ev"] % 2 else "dve"

    def copy_op(eng, out, in_, reads, writes):
        if eng == "act":
            c.op("act", lambda e: e.copy(out, in_), reads=reads, writes=writes)
        else:
            c.op(eng, lambda e: e.tensor_copy(out, in_), reads=reads, writes=writes)

    def mm(ps_ap, lhsT, rhs, start, stop, reads, writes, inc):
        c.op("pe", lambda e: e.matmul(ps_ap, lhsT, rhs, start=start, stop=stop),
             reads=reads, writes=writes, inc=inc)

    c.dma("pool", ident[:], d_ident, writes=[bf("const")])
    c.dma("pool", rotm[:], d_rotm, writes=[bf("const")])
    c.dma("pool", poolw[:], d_pool_w, writes=[bf("const")])
    c.dma("pool", bsT[:], d_bsT, writes=[bf("const")])
    c.dma("pool", maskSc[:], d_maskSc, writes=[bf("const")])
    c.dma("pool", maskSn[:], d_maskSn, writes=[bf("const")])
    c.dma("sp", vecs[:], d_vecs, writes=[bf("const")])
    c.dma("sp", bsS[:], d_bsS, writes=[bf("const")])
    c.dma("sp", wsS[:], d_wsS, writes=[bf("const")])
    c.dma("sp", wsf[:], d_wsT, writes=[bf("wsf")])
    c.dma("sp", trim[:], d_tri, writes=[bf("trim")])
    c.op("dve", lambda e: e.memset(ones[:], 1.0), writes=[bf("const")])
    c.op("dve", lambda e: e.memset(onesn[:], 1.0 / 1024.0), writes=[bf("const")])
    c.op("dve", lambda e: e.memset(ones128[:], 1.0 / 128.0), writes=[bf("const")])
    c.op("dve", lambda e: e.memset(zeros[:], 0.0), writes=[bf("const")])
    for g in range(4):
        c.op("dve", lambda e, g=g: e.tensor_tensor(wmT[:, g, :], wsf[:, g, :], trim[:], ALU.mult),
             reads=[bf("wsf"), bf("trim")], writes=[bf("const")])
    c.op("dve", lambda e: e.memset(hd_halo[:], 0.0), writes=[bf("hd_halo")])
    c.op("dve", lambda e: e.memset(a_ext[:], 0.0), writes=[bf("a_ext")])
    CONST = bf("const")
    V_NMIX, V_NFFN, V_NPLE, V_NFIN = (0, 8), (16, 24), (32, 40), 48
    V_PSC, V_LNG, V_LNB, V_CW = 56, 60, 64, 68

    w_issue_upto(NW)

    deferred = []

    def flush_deferred(keep=0):
        while len(deferred) > keep:
            deferred.pop(0)()

    def dump(i, which, N):
        if o_dbg is None or N != NS:
            return
        dst = o_dbg[i].rearrange("(ch p) n -> p ch n", p=128)
        if which == "h":
            c.dma("pool", dst, hT[:, :, 0:N], reads=[bf("hT%d" % ch) for ch in range(8)], is_output=True)
        else:
            c.dma("sp", dst, rT[:, :, 0:N], reads=[bf("rT%d" % ch) for ch in range(8)], is_output=True)

    def norm_acc(ch, N, defer=True):
        s_ap, s_b = sq_buf()
        c.op("act", lambda e, ch=ch, s_ap=s_ap: e.activation(s_ap[:, 0:N], rT[:, ch, 0:N], AF.Square),
             reads=[bf("rT%d" % ch)], writes=[s_b])
        def later():
            mm(auxp[:, 0:N], onesn[:], s_ap[:, 0:N], ch == 0, ch == 7, [s_b, CONST], [bf("aux")], True)
        if defer:
            deferred.append(later)
        else:
            later()

    def rmsnorm(N, vcol, out_fn=None, pre=False):
        ss = auxp[:, 0:N]
        if not pre:
            for ch in range(8):
                norm_acc(ch, N, defer=False)
        flush_deferred()
        r_ap, r_b = scr_buf()
        c.op("act", lambda e: e.activation(r_ap[:, 0:N], ss, AF.Sqrt, bias=EPS, scale=1.0),
             reads=[bf("aux")], writes=[r_b])
        c.op("dve", lambda e: e.reciprocal(r_ap[:, 0:N], r_ap[:, 0:N]), reads=[r_b], writes=[r_b])
        for ch in range(8):
            if out_fn is None:
                if ch % 2 == 0:
                    c.op("dve", lambda e, ch=ch: e.scalar_tensor_tensor(
                        hT[:, ch, 0:N], rT[:, ch, 0:N], vecs[:, vcol + ch:vcol + ch + 1], r_ap[:, 0:N],
                        ALU.mult, ALU.mult), reads=[bf("rT%d" % ch), r_b, CONST], writes=[bf("hT%d" % ch)])
                else:
                    tz, tzb = zb_buf()
                    c.op("pool", lambda e, ch=ch, tz=tz: e.tensor_tensor(tz[:, 0:N], rT[:, ch, 0:N], r_ap[:, 0:N], ALU.mult),
                         reads=[bf("rT%d" % ch), r_b], writes=[tzb])
                    c.op("act", lambda e, ch=ch, tz=tz: e.activation(hT[:, ch, 0:N], tz[:, 0:N], AF.Identity,
                                                                     scale=vecs[:, vcol + ch:vcol + ch + 1]),
                         reads=[tzb, CONST], writes=[bf("hT%d" % ch)])
            else:
                out_fn(ch, r_ap, r_b)

    def linear(keys, rhs_fn, nk, N, evac, keep=0):
        for key in keys:
            wv, wb = getw(key)
            ncols = wv.shape[2]
            for oc in range(ncols // 128):
                ps, pb = mm_bank()
                for kc in range(nk):
                    rhs, rb = rhs_fn(kc)
                    mm(ps[:, 0:N], wv[:, kc, 128 * oc:128 * oc + 128], rhs, kc == 0, kc == nk - 1,
                       [wb, rb], [pb], kc == nk - 1)
                flush_deferred(keep)
                evac(key, oc, ps[:, 0:N], pb)
            w_advance()

    def h_rhs(N):
        return lambda kc: (hT[:, kc, 0:N], bf("hT%d" % kc))

    def zero_acc(ncol):
        mm(nump[:, 0:ncol], zeros[:], bsT[:, 0, 0:ncol], True, False, [CONST], [bf("num")], False)
        mm(denp[:, 0:ncol], zeros[:], bsT[:, 0, 0:ncol], True, False, [CONST], [bf("den")], True)

    def rope(ps, pb, N, out_full=None, out_halves=None, out_bufs=()):
        z_ap, z_b = zb_buf()
        copy_op("act", z_ap[:, 0:N], ps, [pb], [z_b])

        def stage2():
            rp, rpb = mm_bank()
            mm(rp[:, 0:N], rotm[:], z_ap[:, 0:N], True, True, [z_b, CONST], [rpb], True)
            t1, t1b = scr_buf()
            t2, t2b = scr_buf()
            c.op("dve", lambda e: e.tensor_tensor(t1[:, 0:N], z_ap[:, 0:N], cs[:, 0, 0:N], ALU.mult),
                 reads=[z_b, bf("cs")], writes=[t1b])
            c.op("dve", lambda e: e.tensor_tensor(t2[:, 0:N], rp[:, 0:N], cs[:, 1, 0:N], ALU.mult),
                 reads=[rpb, bf("cs")], writes=[t2b])
            if out_full is not None:
                c.op("pool", lambda e: e.tensor_tensor(out_full, t1[:, 0:N], t2[:, 0:N], ALU.add),
                     reads=[t1b, t2b], writes=list(out_bufs))
            else:
                oa, ob = out_halves
                c.op("pool", lambda e: e.tensor_tensor(oa[0:64, :], t1[0:64, 0:N], t2[0:64, 0:N], ALU.add),
                     reads=[t1b, t2b], writes=[out_bufs[0]])
                c.op("pool", lambda e: e.tensor_tensor(ob[64:128, :], t1[64:128, 0:N], t2[64:128, 0:N], ALU.add),
                     reads=[t1b, t2b], writes=[out_bufs[1]])
        deferred.append(stage2)

    def ffn_ple(l, N, p_ap, p_src):
        c.dma("pool", p_ap[:, :, 0:N], p_src.rearrange("(k p) n -> p k n", p=128), writes=[bf("pTs")])
        hid = aview(0, [128, 32, TT])
        hb = [bf("hid%d" % f) for f in range(32)]
        c.handoff(ARENA_BUFS[0], hb)
        ARENA_BUFS[0] = hb
        ph("ffn%d" % l)
        rmsnorm(N, V_NFFN[l], pre=True)

        def ev1(key, oc, ps, pb):
            f = key[2] * 4 + oc
            r_ap, r_b = zb_buf()
            if f % 2 == 0:
                c.op("act", lambda e: e.activation(r_ap[:, 0:N], ps, AF.Relu), reads=[pb], writes=[r_b])
            else:
                c.op("dve", lambda e: e.tensor_scalar(r_ap[:, 0:N], ps, 0.0, None, ALU.max), reads=[pb], writes=[r_b])
            c.op("pool", lambda e: e.tensor_tensor(hid[:, f, 0:N], r_ap[:, 0:N], r_ap[:, 0:N], ALU.mult),
                 reads=[r_b], writes=[hb[f]])
        linear([("w1", l, j) for j in range(8)], h_rhs(N), 8, N, ev1)

        def ev2(key, oc, ps, pb):
            ch = key[2]
            c.op("dve", lambda e: e.tensor_tensor(rT[:, ch, 0:N], rT[:, ch, 0:N], ps, ALU.add),
                 reads=[pb, bf("rT%d" % ch)], writes=[bf("rT%d" % ch)])
            norm_acc(ch, N)
        linear([("w2", l, j) for j in range(8)], lambda kc: (hid[:, kc, 0:N], hb[kc]), 32, N, ev2)
        dump(3 + 4 * l, "r", N)
        ph("ple%d" % l)
        rmsnorm(N, V_NPLE[l], pre=True)
        ip = wstate["next"]
        wp, wpb = getw(("proj", l))
        for j in range(2):
            wg, wgb = getw(("gate", l, j), oldest=ip)
            for oc in range(4):
                ch = 4 * j + oc
                psg, pgb = mm_bank()
                for kc in range(8):
                    mm(psg[:, 0:N], wg[:, kc, 128 * oc:128 * oc + 128], hT[:, kc, 0:N], kc == 0, kc == 7,
                       [wgb, bf("hT%d" % kc)], [pgb], kc == 7)
                psp, ppb = mm_bank()
                for kc in range(2):
                    mm(psp[:, 0:N], wp[:, kc, 128 * ch:128 * ch + 128], p_ap[:, kc, 0:N], kc == 0, kc == 1,
                       [wpb, bf("pTs")], [ppb], kc == 1)
                flush_deferred(keep=1)
                g_ap, g_b = scr_buf()
                c.op("act", lambda e, g_ap=g_ap, psg=psg: e.activation(g_ap[:, 0:N], psg[:, 0:N], AF.Sigmoid),
                     reads=[pgb], writes=[g_b])
                t_ap, t_b = scr_buf()
                c.op("dve", lambda e, g_ap=g_ap, t_ap=t_ap, psp=psp: e.tensor_tensor(
                    t_ap[:, 0:N], g_ap[:, 0:N], psp[:, 0:N], ALU.mult), reads=[g_b, ppb], writes=[t_b])
                c.op("pool", lambda e, t_ap=t_ap, ch=ch: e.tensor_tensor(
                    rT[:, ch, 0:N], rT[:, ch, 0:N], t_ap[:, 0:N], ALU.add),
                    reads=[t_b, bf("rT%d" % ch)], writes=[bf("rT%d" % ch)])
                norm_acc(ch, N)
        w_advance()
        dump(4 + 4 * l, "r", N)

    ARENA_BUFS = [[bf("arena_init")]]
    RT = [bf("rT%d" % ch) for ch in range(8)]
    cur = {"i": 0}
    xloaded = set()

    def x_load_chunk(i, ch):
        if (i, ch) in xloaded:
            return
        xloaded.add((i, ch))
        kind_, t_ = tiles[i]
        if kind_ == "sample":
            c.dma("sp", rT[:, ch, 0:NS], d_xsT[128 * ch:128 * ch + 128, :], writes=[RT[ch]])
        else:
            c.dma("sp", rT[:, ch, :], d_xT[128 * ch:128 * ch + 128, t_ * TT:(t_ + 1) * TT], writes=[RT[ch]])

    def x_prefetch_next(ch=None):
        i = cur["i"] + 1
        if i >= len(tiles):
            return
        for k in ([ch] if ch is not None else range(8)):
            x_load_chunk(i, k)
    KV_BUFS = [bf("kv_KT0"), bf("kv_KT1"), bf("kv_KT2r"), bf("kv_KT2c"), bf("kv_VK0"), bf("kv_VK1"),
               bf("kv_V2A"), bf("kv_V2B")]

    A_QA, A_QB = 0, 6144
    A_VT = 12288
    A_P = 14336
    A_RD = 15872

    def layer0_prompt(kind, t):
        N = TT
        full = kind == "full"
        main = full and t >= 5
        cur, prev = t % 2, (t + 1) % 2
        tok0 = (t - 5) * TT
        QA = aview(A_QA, [128, 3, 4, TT])
        QB = aview(A_QB, [128, 3, 4, TT])
        VT = aview(A_VT, [128, 4, TT])
        Pb = aview(A_P, [128, 3, TT])
        rden = aview(A_RD, [128, TT], F32)
        ab = {n: bf("ar_" + n) for n in ["QA", "QB", "VT", "P0", "P1", "P2", "rden"]}
        c.handoff(ARENA_BUFS[0], list(ab.values()))
        ARENA_BUFS[0] = list(ab.values())
        if full:
            c.op("dve", lambda e: e.memset(arena[64:128, 0:6144], 0.0), writes=[ab["QA"]])
            c.op("dve", lambda e: e.memset(arena[0:64, 6144:12288], 0.0), writes=[ab["QB"]])
        ph("L0norm %s%d" % (kind, t))
        ck("pre")
        rmsnorm(N, V_NMIX[0])
        ck("norm")
        if not full:
            x_prefetch_next()
        ph("L0proj")
        gs = (0, 1, 2) if (full or t == 3) else (2,)
        KTs = [KT0, KT1, KT2]
        kbufs = [bf("kv_KT0"), bf("kv_KT1"), bf("kv_KT2c")]

        def kslot(g, ch):
            if g == 2:
                return KT2[:, ch, 4 * TT:5 * TT]
            return KTs[g][:, ch, cur * TT:(cur + 1) * TT]

        if full or t == 3:
            def ev_a(key, oc, ps, pb):
                copy_op("act", a_ext[:, oc, 15:15 + N], ps, [pb], [bf("a_ext")])
            linear([("ev_in", "a")], h_rhs(N), 8, N, ev_a)
        for g in gs:
            if full:
                def ev_q(key, oc, ps, pb, g=g):
                    rope(ps, pb, N, out_halves=(QA[:, g, oc, :], QB[:, g, oc, :]), out_bufs=(ab["QA"], ab["QB"]))
                linear([("ev_in", g, "q")], h_rhs(N), 8, N, ev_q)

            def ev_k(key, oc, ps, pb, g=g):
                rope(ps, pb, N, out_full=kslot(g, oc), out_bufs=(kbufs[g],))
            linear([("ev_in", g, "k")], h_rhs(N), 8, N, ev_k)
            ck("k")

            def ev_v(key, oc, ps, pb):
                copy_op(ev_eng(), VT[:, oc, :], ps, [pb], [ab["VT"]])
            linear([("ev_in", g, "v")], h_rhs(N), 8, N, ev_v)
            flush_deferred()
            ck("v")
            if main:
                ko = o_kvT[g, 0, :, tok0:tok0 + TT].rearrange("(ch p) n -> p ch n", p=128)
                ksrc = KT2[:, :, 4 * TT:5 * TT] if g == 2 else KTs[g][:, :, cur * TT:(cur + 1) * TT]
                c.dma("pool", ko, ksrc, reads=[kbufs[g]], is_output=True)
                vo = o_kvT[g, 1, :, tok0:tok0 + TT].rearrange("(ch p) n -> p ch n", p=128)
                c.dma("pool", vo, VT[:, :, :], reads=[ab["VT"]], is_output=True)
            for b in range(4):
                th, thb = trp_half()
                for ch in range(4):
                    src = VT[:, ch, 128 * b:128 * b + 128] if g == 0 else VT[:, ch, b:TT:4]
                    c.op("pe", lambda e, th=th, ch=ch, src=src: e.transpose(th[:, 128 * ch:128 * ch + 128], src, ident[:]),
                         reads=[ab["VT"], CONST], writes=[thb], inc=(ch == 3))
                ck("trb%d" % b)
                if g == 0:
                    copy_op(ev_eng(), VK0[:, cur, b, :], th, [thb], [bf("kv_VK0")])
                elif g == 1:
                    copy_op(ev_eng(), VK1[:, cur, b, :], th, [thb], [bf("kv_VK1")])
                else:
                    copy_op(ev_eng(), V2B[:, b, :], th, [thb], [bf("kv_V2B")])
                ck("evb%d" % b)
        ph("L0att")
        ck("tr")
        if full:
            attention_prompt(t, QA, QB, Pb, rden, ab)
        ph("L0ring")
        ck("att")
        s = t % 4
        c.op("act", lambda e: e.copy(KT2[:, :, s * TT:(s + 1) * TT], KT2[:, :, 4 * TT:5 * TT]),
             reads=[bf("kv_KT2c")], writes=[bf("kv_KT2r")])
        for j in range(4):
            c.dma("sp", V2A[32 * s:32 * s + 32, 4 * j:4 * j + 4, :], V2B[j:128:4, :, :],
                  reads=[bf("kv_V2B")], writes=[bf("kv_V2A")])
        ck("ring")
        if not full:
            if t == 3:
                c.op("dve", lambda e: e.tensor_copy(a_ext[:, :, 0:15], a_ext[:, :, TT:TT + 15]),
                     reads=[bf("a_ext")], writes=[bf("a_ext")])
            return
        ph("L0pool")
        pool_mixer_prompt(t)
        ph("L0wout")
        if t == 4:
            HB = [bf("hT%d" % ch) for ch in range(8)]
            c.op("dve", lambda e: e.tensor_copy(hT[:, :, 0:2], hT[:, :, TT - 2:TT]), reads=HB, writes=HB)
            c.op("dve", lambda e: e.tensor_copy(rT[:, :, 0:2], rT[:, :, TT - 2:TT]), reads=RT, writes=RT)
            N = 2
        def ev_o(key, oc, ps, pb):
            ch = key[1] * 4 + oc
            c.op("dve", lambda e: e.tensor_tensor(rT[:, ch, 0:N], rT[:, ch, 0:N], ps, ALU.add),
                 reads=[pb, bf("rT%d" % ch)], writes=[bf("rT%d" % ch)])
            norm_acc(ch, N)
        linear([("ev_out", j) for j in range(2)], h_rhs(N), 8, N, ev_o, keep=1)
        if t == 4:
            ffn_ple(0, N, pTs, d_pT[0, :, TT - 2:TT])
        else:
            ffn_ple(0, N, pTs, d_pT[0, :, (t - 4) * TT:(t - 3) * TT])

    def attention_prompt(t, QA, QB, Pb, rden, ab):
        cur, prev = t % 2, (t + 1) % 2
        pbufs = [ab["P0"], ab["P1"], ab["P2"]]
        kb0, kb1, kb2r, kb2c = bf("kv_KT0"), bf("kv_KT1"), bf("kv_KT2r"), bf("kv_KT2c")
        vb0, vb1, vb2a, vb2b = bf("kv_VK0"), bf("kv_VK1"), bf("kv_V2A"), bf("kv_V2B")
        for ch in range(4):
            zero_acc(TT)
            all_units = []
            for hh in range(2):
                Q = QA if hh == 0 else QB
                qb_ = ab["QA"] if hh == 0 else ab["QB"]
                po = 64 * hh
                fo = 128 * ch + 64 * hh
                units = []
                for half in range(2):
                    items = []
                    for qi in range(2):
                        qb = 2 * half + qi
                        q_ap = Q[:, 0, ch, 128 * qb:128 * qb + 128]
                        if qb == 0:
                            kp = KT0[:, ch, prev * TT + 384:prev * TT + 512]
                            vp = VK0[:, prev, 3, fo:fo + 64]
                        else:
                            kp = KT0[:, ch, cur * TT + 128 * (qb - 1):cur * TT + 128 * qb]
                            vp = VK0[:, cur, qb - 1, fo:fo + 64]
                        kc_ = KT0[:, ch, cur * TT + 128 * qb:cur * TT + 128 * qb + 128]
                        vc_ = VK0[:, cur, qb, fo:fo + 64]
                        oc_ = (128 * qb, 128 * qb + 128, 1)
                        items.append((kp, kb0, q_ap, (2 * qi) * 128, 128, vp, vb0, oc_, False))
                        items.append((kc_, kb0, q_ap, (2 * qi + 1) * 128, 128, vc_, vb0, oc_, False))
                    units.append((items, half * 512))
                for half in range(2):
                    items = []
                    for qi in range(2):
                        r4 = 2 * half + qi
                        q_ap = Q[:, 1, ch, r4:TT:4]
                        kp = KT1[:, ch, prev * TT + r4:prev * TT + TT:4]
                        kc_ = KT1[:, ch, cur * TT + r4:cur * TT + TT:4]
                        vp = VK1[:, prev, r4, fo:fo + 64]
                        vc_ = VK1[:, cur, r4, fo:fo + 64]
                        oc_ = (r4, TT, 4)
                        items.append((kp, kb1, q_ap, (2 * qi) * 128, 128, vp, vb1, oc_, False))
                        items.append((kc_, kb1, q_ap, (2 * qi + 1) * 128, 128, vc_, vb1, oc_, False))
                    units.append((items, 1024 + half * 512))
                items = []
                for r in range(16):
                    items.append((KT2[:, ch, r:4 * TT:16], kb2r, Q[:, 2, ch, r:TT:16], 32 * r, 32,
                                  V2A[:, r, fo:fo + 64], vb2a, (r, TT, 16), False))
                units.append((items, 2048))
                items = []
                for r4 in range(4):
                    for rho in range(4):
                        r = r4 + 4 * rho
                        items.append((KT2[:, ch, 4 * TT + r4:5 * TT:4], kb2c, Q[:, 2, ch, r:TT:16],
                                      (4 * r4 + rho) * 32, 32, V2B[:, r4, fo:fo + 64], vb2b, (r, TT, 16), False))
                units.append((items, 2560))
                if t == 4:
                    def need(it):
                        o0, o1, os_ = it[7]
                        cols = range(o0, o1, os_)
                        return 510 in cols or 511 in cols
                    units = [([it for it in items if need(it)], mcol) for items, mcol in units]
                    units = [u for u in units if u[0]]
                for ui, (items, mcol) in enumerate(units):
                    all_units.append((items, mcol, qb_, po, ui == len(units) - 1 and hh == 1))

            def stage_s(u):
                items, mcol, qb_, po, last_unit = u
                sp_, sb_ = s_bank()
                for ii, (k_ap, kb_, q_ap, scol, ncol, v_ap, vb_, oc_, st) in enumerate(items):
                    mm(sp_[:, scol:scol + ncol], k_ap, q_ap, True, True, [kb_, qb_], [sb_], ii == len(items) - 1)
                pi = rr["P"] % 3
                rr["P"] += 1
                P = Pb[:, pi, :]
                c.op("act", lambda e, P=P, sp_=sp_: e.activation(P, sp_[:, :], AF.Exp, scale=0.125),
                     reads=[sb_], writes=[pbufs[pi]])
                c.op("dve", lambda e, P=P, mcol=mcol: e.tensor_tensor(P, P, msk[:, mcol:mcol + 512], ALU.mult),
                     reads=[pbufs[pi], bf("msk")], writes=[pbufs[pi]])
                return (P, pi)

            def stage_pv(u, pp):
                items, mcol, qb_, po, last_unit = u
                P, pi = pp
                for ii, (k_ap, kb_, q_ap, scol, ncol, v_ap, vb_, oc_, st) in enumerate(items):
                    lastmm = last_unit and ii == len(items) - 1
                    o0, o1, os_ = oc_
                    mm(nump[po:po + 64, o0:o1:os_], v_ap, P[:, scol:scol + ncol], False, lastmm,
                       [vb_, pbufs[pi]], [bf("num")], False)
                    mm(denp[po:po + 64, o0:o1:os_], ones[:, 0:64], P[:, scol:scol + ncol], False, lastmm,
                       [CONST, pbufs[pi]], [bf("den")], ii == len(items) - 1)

            pend = None
            for u in all_units:
                pp = stage_s(u)
                if pend is not None:
                    stage_pv(*pend)
                pend = (u, pp)
            stage_pv(*pend)
            c.op("dve", lambda e: e.reciprocal(rden[:, :], denp[:, :]), reads=[bf("den")], writes=[ab["rden"]])
            c.op("dve", lambda e, ch=ch: e.tensor_tensor(hT[:, 4 + ch, :], nump[:, :], rden[:, :], ALU.mult),
                 reads=[bf("num"), ab["rden"]], writes=[bf("hT%d" % (4 + ch))])

    def pool_mixer_prompt(t):
        N = TT
        W = 15 + N
        pa = aview(0, [128, 4, 528], F32)
        pb_ = aview(4224, [128, 4, 528], F32)
        icn = aview(8448, [128, 4, TT], BF16)
        pld = aview(10496, [128, 4, TT], BF16)
        nb = {n: bf("pl_" + n) for n in ["pa", "pb", "icn", "pld"]}
        c.handoff(ARENA_BUFS[0], list(nb.values()))
        ARENA_BUFS[0] = list(nb.values())
        c.dma("pool", icn[:, :, :], d_icnt[t - 4], writes=[nb["icn"]])
        for g in range(4):
            src, srcb = a_ext[:, g, :], bf("a_ext")
            off = 0
            bufs = [(pa[:, g, :], nb["pa"]), (pb_[:, g, :], nb["pb"])]
            for k in range(g + 1):
                sh = 1 << k
                dst, dstb = bufs[k % 2]
                eng = "dve" if (g + k) % 2 == 0 else "pool"
                c.op(eng, lambda e, dst=dst, src=src, off=off, sh=sh: e.tensor_tensor(
                    dst[:, off + sh:W], src[:, off + sh:W], src[:, off:W - sh], ALU.add),
                    reads=[srcb], writes=[dstb])
                src, srcb = dst, dstb
                off += sh
            tmp, tmpb = bufs[(g + 1) % 2]
            c.op("dve", lambda e, tmp=tmp, src=src, g=g: e.tensor_tensor(
                tmp[:, 15:W], src[:, 15:W], icn[:, g, :], ALU.mult), reads=[srcb, nb["icn"]], writes=[tmpb])
            c.op("pool", lambda e, tmp=tmp, g=g: e.tensor_tensor(
                pld[:, g, :], tmp[:, 15:W], a_ext[:, g, 15:W], ALU.subtract),
                reads=[tmpb, bf("a_ext")], writes=[nb["pld"]])
            ps, pb2 = mm_bank()
            mm(ps[:, 0:N], poolw[:, g, :], pld[:, g, :], True, True, [CONST, nb["pld"]], [pb2], True)
            c.op("act", lambda e, g=g, ps=ps: e.activation(hT[:, g, 0:N], ps[:, 0:N], AF.Identity,
                                                           scale=vecs[:, V_PSC + g:V_PSC + g + 1]),
                 reads=[pb2, CONST], writes=[bf("hT%d" % g)])
        if t == 8:
            c.dma("sp", o_poolT, a_ext[:, :, TT:TT + 15], reads=[bf("a_ext")], is_output=True)
        c.op("dve", lambda e: e.tensor_copy(a_ext[:, :, 0:15], a_ext[:, :, TT:TT + 15]),
             reads=[bf("a_ext")], writes=[bf("a_ext")])

    L1_U, L1_VN, L1_VTK, L1_GO, L1_GI, L1_HD = 0, 2048, 4096, 6144, 8192, 10240

    def layer1(kind, t, N):
        sample = kind == "sample"
        halo = kind == "halo"
        uT = aview(L1_U, [128, 4, TT])
        vnT = aview(L1_VN, [128, 4, TT])
        vtk = aview(L1_VTK, [128, 4, TT])
        go = aview(L1_GO, [128, 4, TT])
        gi = aview(L1_GI, [128, 4, TT])
        nb = {n: bf("l1_" + n) for n in ["u", "vn", "vtk", "go", "gi", "hd"]}
        c.handoff(ARENA_BUFS[0], list(nb.values()))
        ARENA_BUFS[0] = list(nb.values())
        if sample:
            hd = aview(L1_HD, [128, 4, 16, 6], F32)
            c.dma("sp", scr[:, 2, 0:128], d_sconv, writes=[bf("scr2")])
            c.op("dve", lambda e: e.tensor_copy(hd[:, :, :, 0:2], scr[:, 2, 0:128].rearrange("p (a b c) -> p a b c", a=4, b=16)),
                 reads=[bf("scr2")], writes=[nb["hd"]])
        else:
            hd = aview(L1_HD, [128, 4, 516], F32)
        ph("L1norm %s" % kind)
        rmsnorm(N, V_NMIX[1], pre=True)
        ph("L1proj")
        dump(9, "h", N)
        if not halo:
            def ev_u(key, oc, ps, pb):
                c.op("act", lambda e: e.activation(uT[:, oc, 0:N], ps, AF.Gelu), reads=[pb], writes=[nb["u"]])
            linear([("od_in", "u")], h_rhs(N), 8, N, ev_u)

            def ev_v(key, oc, ps, pb):
                z_ap, z_b = zb_buf()
                c.op("act", lambda e: e.activation(z_ap[:, 0:N], ps, AF.Gelu), reads=[pb], writes=[z_b])
                mm(auxp[:, 0:N], ones128[:], z_ap[:, 0:N], True, True, [z_b, CONST], [bf("aux")], True)
                vc, vcb = scr_buf()
                c.op("dve", lambda e: e.tensor_tensor(vc[:, 0:N], z_ap[:, 0:N], auxp[:, 0:N], ALU.subtract),
                     reads=[z_b, bf("aux")], writes=[vcb])
                s_ap, s_b = sq_buf()
                c.op("act", lambda e: e.activation(s_ap[:, 0:N], vc[:, 0:N], AF.Square), reads=[vcb], writes=[s_b])
                mm(auxp[:, 0:N], ones128[:], s_ap[:, 0:N], True, True, [s_b, CONST], [bf("aux")], True)
                sd, sdb = scr_buf()
                c.op("act", lambda e: e.activation(sd[:, 0:N], auxp[:, 0:N], AF.Sqrt, bias=EPS, scale=1.0),
                     reads=[bf("aux")], writes=[sdb])
                c.op("dve", lambda e: e.reciprocal(sd[:, 0:N], sd[:, 0:N]), reads=[sdb], writes=[sdb])
                c.op("dve", lambda e: e.tensor_tensor(vc[:, 0:N], vc[:, 0:N], sd[:, 0:N], ALU.mult),
                     reads=[vcb, sdb], writes=[vcb])
                c.op("act", lambda e: e.activation(vnT[:, oc, 0:N], vc[:, 0:N], AF.Identity,
                                                   bias=vecs[:, V_LNB + oc:V_LNB + oc + 1],
                                                   scale=vecs[:, V_LNG + oc:V_LNG + oc + 1]),
                     reads=[vcb, CONST], writes=[nb["vn"]])
            linear([("od_in", "v")], h_rhs(N), 8, N, ev_v)
            if sample:
                c.dma("pool", o_cvsT.rearrange("(ch p) n -> p ch n", p=128), vnT[:, :, 0:N],
                      reads=[nb["vn"]], is_output=True)
            def ev_go(key, oc, ps, pb):
                copy_op("act", go[:, oc, 0:N], ps, [pb], [nb["go"]])
            linear([("od_in", "go")], h_rhs(N), 8, N, ev_go)

        def ev_gi(key, oc, ps, pb):
            copy_op("act", gi[:, oc, 0:N], ps, [pb], [nb["gi"]])
        linear([("od_in", "gi")], h_rhs(N), 8, N, ev_gi)
        if not sample:
            c.op("pool", lambda e: e.tensor_copy(hd[:, :, 0:2], hd_halo[:, :, :]), reads=[bf("hd_halo")], writes=[nb["hd"]])

        def ev_xi(key, oc, ps, pb):
            if sample:
                c.op("dve", lambda e: e.tensor_tensor(hd[:, oc, :, 2:6], ps.rearrange("p (n i) -> p n i", i=4),
                                                      gi[:, oc, 0:N].rearrange("p (n i) -> p n i", i=4), ALU.mult),
                     reads=[pb, nb["gi"]], writes=[nb["hd"]])
            else:
                c.op("dve", lambda e: e.tensor_tensor(hd[:, oc, 2:2 + N], ps, gi[:, oc, 0:N], ALU.mult),
                     reads=[pb, nb["gi"]], writes=[nb["hd"]])
        linear([("od_in", "xi")], h_rhs(N), 8, N, ev_xi)
        if not sample:
            c.op("pool", lambda e: e.tensor_copy(hd_halo[:, :, :], hd[:, :, N:N + 2]), reads=[nb["hd"]], writes=[bf("hd_halo")])
            if t == 8:
                c.dma("sp", o_convT, hd[:, :, N:N + 2], reads=[nb["hd"]], is_output=True)
        else:
            c.op("dve", lambda e: e.tensor_copy(scr[:, 2, 0:128].rearrange("p (a b c) -> p a b c", a=4, b=16), hd[:, :, :, 4:6]),
                 reads=[nb["hd"]], writes=[bf("scr2")])
            c.dma("sp", o_convsT, scr[:, 2, 0:128], reads=[bf("scr2")], is_output=True)
        if halo:
            return
        ph("L1gate")
        for g in range(4):
            tmp, tmpb = scr_buf()
            if not sample:
                th, thb = trp_half()
                for blk in range(4):
                    c.op("pe", lambda e, th=th, blk=blk, g=g: e.transpose(
                        th[:, 128 * blk:128 * blk + 128], vnT[:, g, 128 * blk:128 * blk + 128], ident[:]),
                        reads=[nb["vn"], CONST], writes=[thb], inc=(blk == 3))
                copy_op(ev_eng(), vtk[:, g, :], th, [thb], [nb["vtk"]])
                ps, pb = mm_bank()
                for blk in range(4):
                    mm(ps[:, 128 * blk:128 * blk + 128], vtk[:, g, 128 * blk:128 * blk + 128], wmT[:, g, :],
                       True, True, [nb["vtk"], CONST], [pb], blk == 3)
                c.op("dve", lambda e, tmp=tmp, ps=ps, g=g: e.tensor_tensor(tmp[:, 0:N], ps[:, 0:N], bsT[:, g, :], ALU.add),
                     reads=[pb, CONST], writes=[tmpb])
            else:
                vv = vnT[:, g, 0:N].rearrange("p (n i) -> p n i", i=4)
                tv = tmp[:, 0:N].rearrange("p (n i) -> p n i", i=4)
                for ti in range(4):
                    c.op("act", lambda e, ti=ti, g=g, tv=tv, vv=vv: e.activation(
                        tv[:, :, ti], vv[:, :, 0], AF.Identity, scale=wsS[:, g, 4 * ti:4 * ti + 1]),
                        reads=[nb["vn"], CONST], writes=[tmpb])
                    for si in range(1, ti + 1):
                        c.op("dve", lambda e, ti=ti, si=si, g=g, tv=tv, vv=vv: e.scalar_tensor_tensor(
                            tv[:, :, ti], vv[:, :, si], wsS[:, g, 4 * ti + si:4 * ti + si + 1], tv[:, :, ti],
                            ALU.mult, ALU.add), reads=[nb["vn"], CONST, tmpb], writes=[tmpb])
                c.op("dve", lambda e, tmp=tmp, g=g: e.tensor_tensor(tmp[:, 0:N], tmp[:, 0:N], bsS[:, g, :], ALU.add),
                     reads=[tmpb, CONST], writes=[tmpb])
            c.op("dve", lambda e, tmp=tmp, g=g: e.tensor_tensor(hT[:, g, 0:N], tmp[:, 0:N], uT[:, g, 0:N], ALU.mult),
                 reads=[tmpb, nb["u"]], writes=[bf("hT%d" % g)])

        ph("L1conv")
        for ch in range(4):
            acc, accb = scr_buf()
            cw = lambda j, ch=ch: vecs[:, V_CW + 3 * ch + j:V_CW + 3 * ch + j + 1]
            if sample:
                av = acc[:, 0:N].rearrange("p (n i) -> p n i", i=4)
                hv = lambda j, ch=ch: hd[:, ch, :, j:j + 4]
            else:
                av = acc[:, 0:N]
                hv = lambda j, ch=ch: hd[:, ch, j:j + N]
            c.op("act", lambda e, av=av, hv=hv, cw=cw: e.activation(av, hv(0), AF.Identity, scale=cw(0)),
                 reads=[nb["hd"], CONST], writes=[accb])
            for j in (1, 2):
                c.op("dve", lambda e, av=av, hv=hv, cw=cw, j=j: e.scalar_tensor_tensor(
                    av, hv(j), cw(j), av, ALU.mult, ALU.add), reads=[nb["hd"], CONST, accb], writes=[accb])
            c.op("dve", lambda e, acc=acc, ch=ch: e.tensor_tensor(hT[:, 4 + ch, 0:N], go[:, ch, 0:N], acc[:, 0:N], ALU.mult),
                 reads=[accb, nb["go"]], writes=[bf("hT%d" % (4 + ch))])

        def ev_o(key, oc, ps, pb):
            ch = key[1] * 4 + oc
            c.op("dve", lambda e: e.tensor_tensor(rT[:, ch, 0:N], rT[:, ch, 0:N], ps, ALU.add),
                 reads=[pb, bf("rT%d" % ch)], writes=[bf("rT%d" % ch)])
            norm_acc(ch, N)
        dump(5, "h", N)
        ph("L1wout")
        linear([("od_out", j) for j in range(2)], h_rhs(N), 8, N, ev_o, keep=1)
        dump(6, "r", N)
        ffn_ple(1, N, pTs, d_psT[1] if sample else d_pT[1, :, (t - 4) * TT:(t - 3) * TT])

    def final_out(N, o_ap, col0):
        def out_fn(ch, r_ap, r_b):
            ridx = (rr["scr"] - 1) % 3 if ch == 0 else out_fn.ridx
            out_fn.ridx = ridx
            yi = (ridx + 1 + (ch % 2)) % 3
            y, yb = scr[:, yi, :], bf("scr%d" % yi)
            c.op("dve", lambda e: e.scalar_tensor_tensor(y[:, 0:N], rT[:, ch, 0:N], vecs[:, V_NFIN + ch:V_NFIN + ch + 1],
                                                         r_ap[:, 0:N], ALU.mult, ALU.mult),
                 reads=[bf("rT%d" % ch), r_b, CONST], writes=[yb])
            c.dma("sp", o_ap[128 * ch:128 * ch + 128, col0:col0 + N], y[:, 0:N], reads=[yb], is_output=True)
            x_prefetch_next(ch)
        ph("final")
        rmsnorm(N, V_NFIN, out_fn=out_fn, pre=True)

    def layer0_sample():
        N = NS
        QA = aview(A_QA, [128, 3, 4, TT])
        QB = aview(A_QB, [128, 3, 4, TT])
        VT = aview(A_VT, [128, 4, TT])
        Pb = aview(A_P, [128, 3, TT])
        rden = aview(A_RD, [128, TT], F32)
        ab = {n: bf("ar_" + n) for n in ["QA", "QB", "VT", "P0", "P1", "P2", "rden"]}
        c.handoff(ARENA_BUFS[0], list(ab.values()))
        ARENA_BUFS[0] = list(ab.values())
        kcb = [kview(0, [128, 9, 4, 128]), kview(4608, [128, 9, 4, 128])]
        vcb = [kview(9216, [128, 9, 512]), kview(13824, [128, 9, 512])]
        KTs = kview(18432, [128, 3, 4, NS])
        VsT = kview(19200, [128, 3, 512])
        Pn = kview(20736, [128, 512])
        a_s = kvr[:, 21248:21248 + 2 * 4 * 16 * 19].bitcast(F32).rearrange("p (a b c) -> p a b c", a=4, b=16)
        p1 = kvr[:, 23680:23680 + 2 * 16 * 19].bitcast(F32).rearrange("p (b c) -> p b c", b=16)
        p2 = kvr[:, 24288:24288 + 2 * 16 * 19].bitcast(F32).rearrange("p (b c) -> p b c", b=16)
        plds = kview(24896, [128, 4, NS])
        sb_ = {n: bf("sk_" + n) for n in ["kc0", "kc1", "vc0", "vc1", "KTs", "VsT", "Pn", "a_s", "p1", "p2", "pld"]}
        c.handoff(KV_BUFS, list(sb_.values()))
        c.op("dve", lambda e: e.memset(arena[64:128, 0:6144], 0.0), writes=[ab["QA"]])
        c.op("dve", lambda e: e.memset(arena[0:64, 6144:12288], 0.0), writes=[ab["QB"]])
        c.op("pool", lambda e: e.memset(VsT[:, :, :], 0.0), writes=[sb_["VsT"]])
        c.op("pool", lambda e: e.memset(Pn[:, :], 0.0), writes=[sb_["Pn"]])
        stg = scr[:, 0:2, :].rearrange("p a b -> p (a b)")
        c.dma("sp", stg[:, 0:960], d_spool, writes=[bf("scr0"), bf("scr1")])
        c.op("dve", lambda e: e.tensor_copy(a_s[:, :, :, 0:15], stg[:, 0:960].rearrange("p (a b c) -> p a b c", a=4, b=16)),
             reads=[bf("scr0"), bf("scr1")], writes=[sb_["a_s"]])
        ph("S L0norm")
        rmsnorm(N, V_NMIX[0])
        dump(0, "h", N)
        ph("S L0proj")

        def ev_a(key, oc, ps, pb):
            c.op("act", lambda e: e.copy(a_s[:, oc, :, 15:19], ps.rearrange("p (n i) -> p n i", i=4)),
                 reads=[pb], writes=[sb_["a_s"]])
        linear([("ev_in", "a")], h_rhs(N), 8, N, ev_a)
        for g in range(3):
            def ev_q(key, oc, ps, pb, g=g):
                rope(ps, pb, N, out_halves=(QA[:, g, oc, 0:N], QB[:, g, oc, 0:N]), out_bufs=(ab["QA"], ab["QB"]))
            linear([("ev_in", g, "q")], h_rhs(N), 8, N, ev_q)

            def ev_k(key, oc, ps, pb, g=g):
                rope(ps, pb, N, out_full=KTs[:, g, oc, :], out_bufs=(sb_["KTs"],))
            linear([("ev_in", g, "k")], h_rhs(N), 8, N, ev_k)

            def ev_v(key, oc, ps, pb):
                copy_op(ev_eng(), VT[:, oc, 0:N], ps, [pb], [ab["VT"]])
            linear([("ev_in", g, "v")], h_rhs(N), 8, N, ev_v)
            flush_deferred()
            c.dma("pool", o_kvsT[g, 0].rearrange("(ch p) n -> p ch n", p=128), KTs[:, g, :, :],
                  reads=[sb_["KTs"]], is_output=True)
            c.dma("pool", o_kvsT[g, 1].rearrange("(ch p) n -> p ch n", p=128), VT[:, :, 0:N],
                  reads=[ab["VT"]], is_output=True)
            th, thb = trp_half()
            for ch in range(4):
                c.op("pe", lambda e, th=th, ch=ch: e.transpose(th[0:N, 128 * ch:128 * ch + 128], VT[:, ch, 0:N], ident[:]),
                     reads=[ab["VT"], CONST], writes=[thb], inc=(ch == 3))
            copy_op(ev_eng(), VsT[0:N, g, :], th[0:N, :], [thb], [sb_["VsT"]])
        ph("S att new")
        pbufs = [ab["P0"], ab["P1"], ab["P2"]]
        zero_acc(256)
        for g in range(3):
            sp_, sbk = s_bank()
            for h in range(8):
                ch, hh = h // 2, h % 2
                Q = QA if hh == 0 else QB
                mm(sp_[0:N, 64 * h:64 * h + 64], KTs[:, g, ch, :], Q[:, g, ch, 0:N], True, True,
                   [sb_["KTs"], ab["QA"], ab["QB"]], [sbk], h == 7)
            c.op("act", lambda e, sp_=sp_: e.activation(Pn[0:N, :], sp_[0:N, :], AF.Exp, scale=0.125),
                 reads=[sbk], writes=[sb_["Pn"]])
            c.op("dve", lambda e, g=g: e.tensor_tensor(Pn[0:N, :], Pn[0:N, :], maskSn[0:N, g, :], ALU.mult),
                 reads=[sb_["Pn"], CONST], writes=[sb_["Pn"]])
            for h in range(8):
                ch, hh = h // 2, h % 2
                po = 64 * hh
                mm(nump[po:po + 64, 64 * ch:64 * ch + 64], VsT[:, g, 64 * h:64 * h + 64], Pn[:, 64 * h:64 * h + 64],
                   False, False, [sb_["VsT"], sb_["Pn"]], [bf("num")], False)
                mm(denp[po:po + 64, 64 * ch:64 * ch + 64], ones[:, 0:64], Pn[:, 64 * h:64 * h + 64],
                   False, False, [CONST, sb_["Pn"]], [bf("den")], h == 7)
        ph("S att cache")
        for n in range(16):
            kb_ap, vb_ap = kcb[n % 2], vcb[n % 2]
            kbb, vbb = sb_["kc%d" % (n % 2)], sb_["vc%d" % (n % 2)]
            c.dma("pool", kb_ap.rearrange("p a b c -> p (a b c)"), d_kc[n], writes=[kbb])
            c.dma("pool", vb_ap.rearrange("p a b -> p (a b)"), d_vc[n], writes=[vbb])
            sp_, sbk = s_bank()
            sets = [(0, 0, 4 * n, 4, 0)] + [(1, 1 + i, 4 * n + i, 1, 4 + i) for i in range(4)] + \
                   [(2, 5 + i, 4 * n + i, 1, 8 + i) for i in range(4)]
            for h in range(8):
                ch, hh = h // 2, h % 2
                Q = QA if hh == 0 else QB
                for si, (g, s, q0, nq, k0) in enumerate(sets):
                    mm(sp_[:, 12 * h + k0:12 * h + k0 + nq], kb_ap[:, s, ch, :], Q[:, g, ch, q0:q0 + nq], True, True,
                       [kbb, ab["QA"], ab["QB"]], [sbk], h == 7 and si == 8)
            pi = rr["P"] % 3
            rr["P"] += 1
            P = Pb[:, pi, 0:96]
            c.op("act", lambda e, P=P, sp_=sp_: e.activation(P, sp_[:, 0:96], AF.Exp, scale=0.125),
                 reads=[sbk], writes=[pbufs[pi]])
            c.op("dve", lambda e, P=P: e.tensor_tensor(P, P, maskSc[:, :], ALU.mult),
                 reads=[pbufs[pi], CONST], writes=[pbufs[pi]])
            for h in range(8):
                ch, hh = h // 2, h % 2
                po = 64 * hh
                for si, (g, s, q0, nq, k0) in enumerate(sets):
                    last = (n == 15)
                    mm(nump[po:po + 64, 64 * ch + q0:64 * ch + q0 + nq], vb_ap[:, s, 64 * h:64 * h + 64],
                       P[:, 12 * h + k0:12 * h + k0 + nq], False, last, [vbb, pbufs[pi]], [bf("num")], False)
                    mm(denp[po:po + 64, 64 * ch + q0:64 * ch + q0 + nq], ones[:, 0:64],
                       P[:, 12 * h + k0:12 * h + k0 + nq], False, last, [CONST, pbufs[pi]], [bf("den")],
                       h == 7 and si == 8)
        c.op("dve", lambda e: e.reciprocal(rden[:, 0:256], denp[:, 0:256]), reads=[bf("den")], writes=[ab["rden"]])
        for ch in range(4):
            c.op("dve", lambda e, ch=ch: e.tensor_tensor(hT[:, 4 + ch, 0:N], nump[:, 64 * ch:64 * ch + 64],
                                                         rden[:, 64 * ch:64 * ch + 64], ALU.mult),
                 reads=[bf("num"), ab["rden"]], writes=[bf("hT%d" % (4 + ch))])
        ph("S pool")
        for g in range(4):
            w = 2 << g
            src, srcb = a_s[:, g, :, :], sb_["a_s"]
            off = 0
            bufs = [(p1, sb_["p1"]), (p2, sb_["p2"])]
            for k in range(g + 1):
                sh = 1 << k
                dst, dstb = bufs[k % 2]
                c.op("dve", lambda e, dst=dst, src=src, off=off, sh=sh: e.tensor_tensor(
                    dst[:, :, off + sh:19], src[:, :, off + sh:19], src[:, :, off:19 - sh], ALU.add),
                    reads=[srcb], writes=[dstb])
                src, srcb = dst, dstb
                off += sh
            c.op("dve", lambda e, src=src, g=g, w=w: e.scalar_tensor_tensor(
                plds[:, g, :].rearrange("p (n i) -> p n i", i=4), src[:, :, 15:19], 1.0 / w, a_s[:, g, :, 15:19],
                ALU.mult, ALU.subtract), reads=[srcb, sb_["a_s"]], writes=[sb_["pld"]])
            ps, pb2 = mm_bank()
            mm(ps[:, 0:N], poolw[:, g, :], plds[:, g, :], True, True, [CONST, sb_["pld"]], [pb2], True)
            c.op("act", lambda e, g=g, ps=ps: e.activation(hT[:, g, 0:N], ps[:, 0:N], AF.Identity,
                                                           scale=vecs[:, V_PSC + g:V_PSC + g + 1]),
                 reads=[pb2, CONST], writes=[bf("hT%d" % g)])
        stg = scr[:, 0:2, :].rearrange("p a b -> p (a b)")
        c.op("dve", lambda e: e.tensor_copy(stg[:, 0:960].rearrange("p (a b c) -> p a b c", a=4, b=16), a_s[:, :, :, 4:19]),
             reads=[sb_["a_s"]], writes=[bf("scr0"), bf("scr1")])
        c.dma("sp", o_poolsT, stg[:, 0:960], reads=[bf("scr0"), bf("scr1")], is_output=True)

        def ev_o(key, oc, ps, pb):
            ch = key[1] * 4 + oc
            c.op("dve", lambda e: e.tensor_tensor(rT[:, ch, 0:N], rT[:, ch, 0:N], ps, ALU.add),
                 reads=[pb, bf("rT%d" % ch)], writes=[bf("rT%d" % ch)])
            norm_acc(ch, N)
        ph("S wout")
        dump(1, "h", N)
        linear([("ev_out", j) for j in range(2)], h_rhs(N), 8, N, ev_o, keep=1)
        dump(2, "r", N)
        ffn_ple(0, N, pTs, d_psT[0])

    def _main_loop():
        for ti_, (kind, t) in enumerate(tiles):
            cur["i"] = ti_
            for ch_ in range(8):
                x_load_chunk(ti_, ch_)
            if kind != "sample":
                c.dma("sp", cs[:, :, :], d_cs[:, :, t * TT:(t + 1) * TT], writes=[bf("cs")])
                if kind == "full":
                    c.dma("pool", msk[:, :], d_mask[t - 4], writes=[bf("msk")])
                layer0_prompt(kind, t)
            else:
                c.dma("sp", cs[:, :, 0:NS], d_cs[:, :, NT * TT:NT * TT + NS], writes=[bf("cs")])
                layer0_sample()
            if kind == "kv":
                continue
            if kind == "full" and t == 4:
                layer1("halo", t, 2)
                continue
            if kind == "full":
                layer1("full", t, TT)
                final_out(TT, o_yT, (t - 5) * TT)
            else:
                layer1("sample", t, NS)
                final_out(NS, o_ysT, 0)

    try:
        _main_loop()
    except _Stop:
        pass
    ph("end")
    c.finish()
    c.close()
    return nc, c


_CACHE = {}


def _vec_pc(v):
    return np.ascontiguousarray(v.reshape(-1, 128).T)


def _shared_inputs(inp):
    f = np.float32
    sh = {}
    WB = _wblocks()
    wall = np.zeros((len(WB), 128, 4096), f)
    for i, (key, nm, l, c0, ncols, kcn) in enumerate(WB):
        W = np.asarray(inp[nm][l], dtype=f)[:, c0:c0 + ncols]
        wall[i, :, 0:kcn * ncols] = W.reshape(kcn, 128, ncols).transpose(1, 0, 2).reshape(128, kcn * ncols)
    sh["wall"] = wall
    sh["pool_w"] = np.ascontiguousarray(inp["ev_pool_w"][0].transpose(1, 0, 2), dtype=f)
    ws = np.asarray(inp["od_ws"][0], dtype=f)
    bs = np.asarray(inp["od_bs"][0], dtype=f)
    sh["od_wsT"] = np.ascontiguousarray(ws.transpose(2, 0, 1))
    vec = np.zeros((128, 80), f)
    vec[:, 0:8] = _vec_pc(inp["norm_mix"][0])
    vec[:, 8:16] = _vec_pc(inp["norm_mix"][1])
    vec[:, 16:24] = _vec_pc(inp["norm_ffn"][0])
    vec[:, 24:32] = _vec_pc(inp["norm_ffn"][1])
    vec[:, 32:40] = _vec_pc(inp["norm_ple"][0])
    vec[:, 40:48] = _vec_pc(inp["norm_ple"][1])
    vec[:, 48:56] = _vec_pc(inp["norm_final"])
    vec[:, 56:60] = _vec_pc(inp["ev_pool_scale"][0])
    vec[:, 60:64] = _vec_pc(inp["od_ln_g"][0])
    vec[:, 64:68] = _vec_pc(inp["od_ln_b"][0])
    cw = np.asarray(inp["od_conv_w"][0], dtype=f)
    for ch in range(4):
        for j in range(3):
            vec[:, 68 + 3 * ch + j] = cw[j, ch * 128:(ch + 1) * 128]
    sh["vecs"] = vec
    sh["bsT"] = np.ascontiguousarray(np.broadcast_to(np.tile(bs, (1, 4))[None], (128, 4, 512)), dtype=f)
    sh["bsS"] = np.ascontiguousarray(np.broadcast_to(np.tile(bs[:, 0:4], (1, 16))[None], (128, 4, 64)), dtype=f)
    sh["wsS"] = np.ascontiguousarray(np.broadcast_to(ws[:, 0:4, 0:4].reshape(4, 16)[None], (128, 4, 16)), dtype=f)
    sh["ident"] = np.eye(128, dtype=f)
    rot = np.zeros((128, 128), f)
    for m in range(128):
        k = m + 32 if (m % 64) < 32 else m - 32
        rot[k, m] = 1.0
    sh["rotm"] = rot
    sidx = np.arange(128)
    sh["trimask"] = (sidx[:, None] <= sidx[None, :]).astype(f)
    msc = np.ones((128, 96), f)
    for h in range(8):
        for k in range(4):
            msc[:, 12 * h + k] = (sidx >= k).astype(f)
    sh["maskSc"] = msc
    msn = np.zeros((128, 3, 512), f)
    row = np.arange(64)
    col = np.arange(64)
    same = (row[:, None] // 4) == (col[None, :] // 4)
    m0 = same & ((row[:, None] % 4) <= (col[None, :] % 4))
    m1 = same & ((row[:, None] % 4) == (col[None, :] % 4))
    for h in range(8):
        msn[0:64, 0, 64 * h:64 * h + 64] = m0
        msn[0:64, 1, 64 * h:64 * h + 64] = m1
        msn[0:64, 2, 64 * h:64 * h + 64] = m1
    sh["maskSn"] = msn
    return sh


def _rope_tables(pos):
    f = np.float32
    half = 32
    inv = np.power(f(10000.0), -np.arange(half, dtype=f) / f(half)).astype(f)
    ang = pos.astype(f)[None, :] * inv[:, None]
    cos = np.cos(ang).astype(f)
    sin = np.sin(ang).astype(f)
    p = np.arange(128)
    fi = p % 32
    sign = np.where((p % 64) < 32, -1.0, 1.0).astype(f)
    out = np.empty((128, 2, pos.shape[0]), f)
    out[:, 0, :] = cos[fi]
    out[:, 1, :] = sin[fi] * sign[:, None]
    return out


def _core_masks(s):
    f = np.float32
    k = np.arange(128)[:, None]
    q128 = np.arange(128)[None, :]
    prev_m = (k >= q128).astype(f)
    cur_m = (k <= q128).astype(f)
    out = np.zeros((5, 128, MASKW), f)
    for ti in range(5):
        tau = 4 + ti
        valid = lambda tp: 1.0 if (s - 2560 + 512 * tp) >= 0 else 0.0
        m = out[ti]
        for qb in range(4):
            m[:, (qb * 2) * 128:(qb * 2 + 1) * 128] = prev_m * (valid(tau - 1) if qb == 0 else 1.0)
            m[:, (qb * 2 + 1) * 128:(qb * 2 + 2) * 128] = cur_m
        for r4 in range(4):
            m[:, 1024 + (r4 * 2) * 128:1024 + (r4 * 2 + 1) * 128] = prev_m * valid(tau - 1)
            m[:, 1024 + (r4 * 2 + 1) * 128:1024 + (r4 * 2 + 2) * 128] = cur_m
        i32 = np.arange(32)[None, :]
        blk = np.zeros((128, 32), f)
        for slot in range(4):
            tp = [x for x in range(tau - 4, tau) if x % 4 == slot][0]
            u = np.arange(32)[:, None]
            rel = 32 * (tau - tp) + i32 - u
            blk[32 * slot:32 * slot + 32, :] = ((rel <= 128) & (rel >= 0)).astype(f) * valid(tp)
        for r in range(16):
            m[:, 2048 + 32 * r:2048 + 32 * r + 32] = blk
        kk = np.arange(128)[:, None]
        for r4 in range(4):
            for rho in range(4):
                mb = ((kk % 4) == rho) & ((kk // 4) <= i32)
                c0 = 2560 + (4 * r4 + rho) * 32
                m[:, c0:c0 + 32] = mb.astype(f)
    return out


def _core_icnt(s):
    f = np.float32
    out = np.ones((5, 128, 4, TT), f)
    for ti in range(5):
        pos = s - 2560 + 512 * (4 + ti) + np.arange(TT)
        for g in range(4):
            w = 2 << g
            cnt = np.minimum(w, np.maximum(pos, 0) + 1).astype(f)
            out[ti, :, g, :] = (f(1.0) / cnt)[None, :]
    return out


def _gather_cache(inp, n):
    f = np.float32
    kc = np.empty((128, 9, 4, 128), f)
    vc = np.empty((128, 9, 512), f)
    sets = [(inp["cache_kv_w128"], np.arange(128))]
    for i in range(4):
        sets.append((inp["cache_kv_w512"], i + 4 * np.arange(128)))
    for i in range(4):
        sets.append((inp["cache_kv_w2048"], i + 16 * np.arange(128)))
    for si, (arr, rows) in enumerate(sets):
        blk = np.asarray(arr[0, n, rows], dtype=f)
        kk = blk[:, 0].reshape(128, 512)
        vv = blk[:, 1].reshape(128, 512)
        kc[:, si, :, :] = kk.T.reshape(4, 128, 128).transpose(1, 0, 2)
        vc[:, si, :] = vv
    return kc.reshape(128, 9 * 4 * 128), vc.reshape(128, 9 * 512)


def kernel(**inp):
    f = np.float32
    inp = {k: np.asarray(v) for k, v in inp.items()}
    if "prog" not in _CACHE:
        _CACHE["prog"] = build_program()
    nc, _ctx = _CACHE["prog"]
    sh = _shared_inputs(inp)
    xp = np.asarray(inp["x_prompt"], dtype=f)
    pp = np.asarray(inp["p_prompt"], dtype=f)
    xs = np.asarray(inp["x_sample"], dtype=f)
    ps_ = np.asarray(inp["p_sample"], dtype=f)
    in_maps = []
    for core in (_DBG.get("cores") or range(NCORES)):
        b, s = core // 4, (core % 4) * SEG
        m = dict(sh)
        tok = s - 2560 + np.arange(NT * TT)
        xT = np.zeros((1024, NT * TT), f)
        ok = tok >= 0
        xT[:, ok] = xp[b, tok[ok]].T
        m["xT"] = xT
        tokp = s - 512 + np.arange(5 * TT)
        pT = np.zeros((2, 256, 5 * TT), f)
        okp = tokp >= 0
        pT[:, :, okp] = pp[:, b, tokp[okp]].transpose(0, 2, 1)
        m["pT"] = pT
        pos = np.concatenate([np.maximum(tok, 0), 2048 + (np.arange(NS) % 4)])
        m["cs"] = _rope_tables(pos)
        m["mask"] = _core_masks(s)
        m["icnt"] = _core_icnt(s)
        n0 = 16 * core
        m["xsT"] = np.ascontiguousarray(xs[n0:n0 + 16].reshape(NS, 1024).T)
        m["psT"] = np.ascontiguousarray(ps_[:, n0:n0 + 16].reshape(2, NS, 256).transpose(0, 2, 1))
        kcs, vcs = [], []
        for j in range(16):
            a, b_ = _gather_cache(inp, n0 + j)
            kcs.append(a)
            vcs.append(b_)
        m["kc"] = np.stack(kcs)
        m["vc"] = np.stack(vcs)
        sp = np.asarray(inp["state_pool"][0, n0:n0 + 16], dtype=f)
        m["spool"] = np.ascontiguousarray(sp.reshape(16, 15, 4, 128).transpose(3, 2, 0, 1)).reshape(128, 960)
        sc = np.asarray(inp["state_conv"][0, n0:n0 + 16], dtype=f)
        m["sconv"] = np.ascontiguousarray(sc.reshape(16, 2, 4, 128).transpose(3, 2, 0, 1)).reshape(128, 128)
        in_maps.append(m)
    if _DBG.get("cores"):
        res = run_bass_kernel_spmd(nc, in_maps, core_ids=list(range(len(in_maps))))
        return res.results
    res = run_bass_kernel_spmd(nc, in_maps, core_ids=list(range(NCORES)))
    R = res.results
    y_prompt = np.empty((2, SEQ, 1024), f)
    y_sample = np.empty((128, 4, 1024), f)
    kvp = [np.empty((1, 2, w, 2, 8, 64), f) for w in (128, 512, 2048)]
    kvs = [np.empty((1, 128, 4, 2, 8, 64), f) for _ in range(3)]
    pool_p = np.empty((1, 2, 15, 512), f)
    pool_s = np.empty((1, 128, 15, 512), f)
    conv_p = np.empty((1, 2, 2, 512), f)
    conv_s = np.empty((1, 128, 2, 512), f)
    cv_s = np.empty((1, 128, 4, 512), f)
    for core in range(NCORES):
        r = R[core]
        b, s = core // 4, (core % 4) * SEG
        n0 = 16 * core
        y_prompt[b, s:s + SEG] = r["o_yT"].T
        y_sample[n0:n0 + 16] = r["o_ysT"].T.reshape(16, 4, 1024)
        for g in range(3):
            t = r["o_kvsT"][g]
            kvs[g][0, n0:n0 + 16] = t.transpose(2, 0, 1).reshape(16, 4, 2, 8, 64)
        pool_s[0, n0:n0 + 16] = r["o_poolsT"].reshape(128, 4, 16, 15).transpose(2, 3, 1, 0).reshape(16, 15, 512)
        conv_s[0, n0:n0 + 16] = r["o_convsT"].reshape(128, 4, 16, 2).transpose(2, 3, 1, 0).reshape(16, 2, 512)
        cv_s[0, n0:n0 + 16] = r["o_cvsT"].T.reshape(16, 4, 512)
        if core % 4 == 3:
            for g, w in enumerate((128, 512, 2048)):
                t = r["o_kvT"][g][:, :, SEG - w:]
                kvp[g][0, b] = t.transpose(2, 0, 1).reshape(w, 2, 8, 64)
            pool_p[0, b] = r["o_poolT"].transpose(2, 1, 0).reshape(15, 512)
            conv_p[0, b] = r["o_convT"].transpose(2, 1, 0).reshape(2, 512)
    return (y_prompt, y_sample, kvp[0], kvp[1], kvp[2], kvs[0], kvs[1], kvs[2],
            pool_p, pool_s, conv_p, conv_s, cv_s)
```

```python
import numpy as np
import concourse.bass as bass
import concourse.mybir as mybir
from concourse.bass_utils import run_bass_kernel_spmd

F32 = mybir.dt.float32
BF16 = mybir.dt.bfloat16
ALU = mybir.AluOpType
AF = mybir.ActivationFunctionType

NCORES = 8
SEQ = 8192
SEG = 2048
TT = 512
NT = 9
NS = 64
EPS = 1e-6
MASKW = 3072


class Buf:
    __slots__ = ("name", "w", "r")

    def __init__(self, name=""):
        self.name = name
        self.w = None
        self.r = {}


class Eng:
    def __init__(self, key, sem):
        self.key = key
        self.sem = sem
        self.count = 0
        self.waited = {}
        self.prog = []


class Ctx:
    COMPUTE = ("pe", "act", "dve", "pool")

    def __init__(self, nc, n_dma_sems=8):
        self.nc = nc
        self.sems = {}
        self.engs = {}
        self._stack = []
        for k in ("pe", "act", "dve", "pool", "sp"):
            self._sem("e_" + k)
            self.engs[k] = Eng(k, "e_" + k)
        self.dma_pool = {}
        self.dma_rr = {}
        self.dma_val = {}
        for q in ("sp", "act", "pool"):
            names = []
            for i in range(n_dma_sems):
                nm = "d_%s_%d" % (q, i)
                self._sem(nm)
                names.append(nm)
                self.dma_val[nm] = 0
            self.dma_pool[q] = names
            self.dma_rr[q] = 0
        self.out_events = []
        self.n_wait = 0
        self.n_op = 0

    def _sem(self, name):
        cm = self.nc.semaphore(name)
        h = cm.__enter__()
        self._stack.append(cm)
        self.sems[name] = h
        return h

    def sb(self, name, shape, dtype):
        cm = self.nc.sbuf_tensor("sb_" + name, shape, dtype)
        t = cm.__enter__()
        self._stack.append(cm)
        return t

    def ps(self, name, shape, dtype):
        cm = self.nc.psum_tensor("ps_" + name, shape, dtype)
        t = cm.__enter__()
        self._stack.append(cm)
        return t

    def _deps(self, reads, writes):
        deps = {}
        for b in reads:
            if b.w is not None:
                k, v = b.w
                if deps.get(k, 0) < v:
                    deps[k] = v
        for b in writes:
            if b.w is not None:
                k, v = b.w
                if deps.get(k, 0) < v:
                    deps[k] = v
            for k, v in b.r.items():
                if deps.get(k, 0) < v:
                    deps[k] = v
        return deps

    def _emit_waits(self, e, deps, skip_own=False):
        for k, v in deps.items():
            if skip_own and k == e.sem:
                continue
            if e.waited.get(k, 0) >= v:
                continue
            e.waited[k] = v
            e.prog.append(("wait", self.sems[k], v))
            self.n_wait += 1

    def _record(self, ev, reads, writes):
        k, v = ev
        for b in reads:
            if b.r.get(k, 0) < v:
                b.r[k] = v
        for b in writes:
            b.w = ev
            b.r = {}

    def op(self, eng, fn, reads=(), writes=(), inc=True):
        e = self.engs[eng]
        deps = self._deps(reads, writes)
        self._emit_waits(e, deps, skip_own=(eng == "pe"))
        if inc:
            e.count += 1
            ev = (e.sem, e.count)
            e.prog.append(("op", fn, self.sems[e.sem], 1))
        else:
            ev = (e.sem, e.count + 1)
            e.prog.append(("op", fn, None, 0))
        self._record(ev, reads, writes)
        self.n_op += 1
        return ev

    def dma(self, q, out, in_, reads=(), writes=(), is_output=False):
        e = self.engs[q]
        deps = self._deps(reads, writes)
        names = self.dma_pool[q]
        nm = names[self.dma_rr[q] % len(names)]
        self.dma_rr[q] += 1
        prev = self.dma_val[nm]
        if prev > 0 and deps.get(nm, 0) < prev:
            deps[nm] = prev
        self._emit_waits(e, deps)
        val = prev + 16
        self.dma_val[nm] = val
        ev = (nm, val)

        def fn(engine, out=out, in_=in_):
            return engine.dma_start(out=out, in_=in_)
        e.prog.append(("op", fn, self.sems[nm], 16))
        self._record(ev, reads, writes)
        if is_output:
            self.out_events.append(ev)
        self.n_op += 1
        return ev

    def handoff(self, old, new):
        merged = {}
        for b in old:
            if b.w is not None:
                k, v = b.w
                if merged.get(k, 0) < v:
                    merged[k] = v
            for k, v in b.r.items():
                if merged.get(k, 0) < v:
                    merged[k] = v
        for b in new:
            b.w = None
            b.r = dict(merged)

    def finish(self):
        e = self.engs["sp"]
        deps = {}
        for k, v in self.out_events:
            deps[k] = max(deps.get(k, 0), v)
        for kk in self.COMPUTE:
            ee = self.engs[kk]
            if ee.count > 0:
                deps[ee.sem] = ee.count
        for nm, v in self.dma_val.items():
            if v > 0:
                deps[nm] = max(deps.get(nm, 0), v)
        self._emit_waits(e, deps)
        nc = self.nc
        engs = self.engs

        def replay(engine, prog):
            for item in prog:
                if item[0] == "wait":
                    engine.wait_ge(item[1], item[2])
                else:
                    _, fn, sem, n = item
                    ins = fn(engine)
                    if sem is not None:
                        ins.then_inc(sem, n)

        with nc.Block() as block:
            @block.sync
            def _(eng):
                replay(eng, engs["sp"].prog)

            @block.tensor
            def _(eng):
                replay(eng, engs["pe"].prog)

            @block.scalar
            def _(eng):
                replay(eng, engs["act"].prog)

            @block.vector
            def _(eng):
                replay(eng, engs["dve"].prog)

            @block.gpsimd
            def _(eng):
                replay(eng, engs["pool"].prog)

    def close(self):
        while self._stack:
            cm = self._stack.pop()
            cm.__exit__(None, None, None)


def _wblocks():
    L = [(("ev_in", "a"), "ev_w_in", 0, 0, 512, 8)]
    for g in range(3):
        base = 512 + g * 1536
        for i, nm in enumerate(("q", "k", "v")):
            L.append((("ev_in", g, nm), "ev_w_in", 0, base + 512 * i, 512, 8))
    for j in range(2):
        L.append((("ev_out", j), "ev_w_out", 0, 512 * j, 512, 8))
    names = ["u", "v", "go", "gi", "xi"]
    for j in range(5):
        L.append((("od_in", names[j]), "od_w_in", 0, 512 * j, 512, 8))
    for j in range(2):
        L.append((("od_out", j), "od_w_out", 0, 512 * j, 512, 8))
    for l in range(2):
        for j in range(8):
            L.append((("w1", l, j), "ffn_w1", l, 512 * j, 512, 8))
        for j in range(8):
            L.append((("w2", l, j), "ffn_w2", l, 128 * j, 128, 32))
        L.append((("proj", l), "ple_w_proj", l, 0, 1024, 2))
        for j in range(2):
            L.append((("gate", l, j), "ple_w_gate", l, 512 * j, 512, 8))
    return L


_DBG = {}


class _Stop(Exception):
    pass


PHASES = []


def ck(name):
    if _DBG.get("stop") == name:
        raise _Stop()


def build_program():
    nc = bass.Bass("TRN2", target_bir_lowering=False)
    c = Ctx(nc)

    def din(name, shape):
        return nc.dram_tensor(name, list(shape), F32, kind="ExternalInput").ap()

    def dout(name, shape):
        return nc.dram_tensor(name, list(shape), F32, kind="ExternalOutput").ap()

    WB = _wblocks()
    WIDX = {k[0]: i for i, k in enumerate(WB)}
    d_wall = din("wall", [len(WB), 128, 4096])
    d_pool_w = din("pool_w", [128, 4, 128])
    d_wsT = din("od_wsT", [128, 4, 128])
    d_vecs = din("vecs", [128, 80])
    d_bsT = din("bsT", [128, 4, 512])
    d_bsS = din("bsS", [128, 4, 64])
    d_wsS = din("wsS", [128, 4, 16])
    d_ident = din("ident", [128, 128])
    d_rotm = din("rotm", [128, 128])
    d_tri = din("trimask", [128, 128])
    d_maskSc = din("maskSc", [128, 96])
    d_maskSn = din("maskSn", [128, 3, 512])
    d_xT = din("xT", [1024, NT * TT])
    d_pT = din("pT", [2, 256, 5 * TT])
    d_cs = din("cs", [128, 2, NT * TT + NS])
    d_mask = din("mask", [5, 128, MASKW])
    d_icnt = din("icnt", [5, 128, 4, TT])
    d_xsT = din("xsT", [1024, NS])
    d_psT = din("psT", [2, 256, NS])
    d_kc = din("kc", [16, 128, 9 * 4 * 128])
    d_vc = din("vc", [16, 128, 9 * 512])
    d_spool = din("spool", [128, 960])
    d_sconv = din("sconv", [128, 128])
    o_yT = dout("o_yT", [1024, SEG])
    o_ysT = dout("o_ysT", [1024, NS])
    o_kvT = dout("o_kvT", [3, 2, 512, SEG])
    o_kvsT = dout("o_kvsT", [3, 2, 512, NS])
    o_poolT = dout("o_poolT", [128, 4, 15])
    o_poolsT = dout("o_poolsT", [128, 960])
    o_convT = dout("o_convT", [128, 4, 2])
    o_convsT = dout("o_convsT", [128, 128])
    o_cvsT = dout("o_cvsT", [512, NS])
    o_dbg = dout("o_dbg", [12, 1024, NS]) if _DBG.get("dump") else None

    rT = c.sb("rT", [128, 8, TT], F32)
    hT = c.sb("hT", [128, 8, TT], BF16)
    sq = c.sb("sq", [128, 2, TT], BF16)
    zb = c.sb("zb", [128, 2, TT], BF16)
    scr = c.sb("scr", [128, 3, TT], F32)
    cs = c.sb("cs_s", [128, 2, TT], F32)
    a_ext = c.sb("a_ext", [128, 4, 528], F32)
    pTs = c.sb("pTs_s", [128, 2, TT], BF16)
    msk = c.sb("msk_s", [128, MASKW], BF16)
    NW = 4
    wsl = c.sb("wsl", [128, NW, 4096], BF16)
    ident = c.sb("ident", [128, 128], BF16)
    rotm = c.sb("rotm_s", [128, 128], BF16)
    ones = c.sb("ones", [128, 128], BF16)
    onesn = c.sb("onesn", [128, 128], BF16)
    ones128 = c.sb("ones128", [128, 128], BF16)
    zeros = c.sb("zeros", [128, 128], BF16)
    vecs = c.sb("vecs_s", [128, 80], F32)
    bsT = c.sb("bsT_s", [128, 4, TT], BF16)
    bsS = c.sb("bsS_s", [128, 4, NS], F32)
    wsS = c.sb("wsS_s", [128, 4, 16], F32)
    wmT = c.sb("wmT", [128, 4, 128], BF16)
    wsf = c.sb("wsf", [128, 4, 128], F32)
    trim = c.sb("trim", [128, 128], F32)
    poolw = c.sb("poolw", [128, 4, 128], BF16)
    hd_halo = c.sb("hd_halo", [128, 4, 2], F32)
    maskSc = c.sb("maskSc_s", [128, 96], BF16)
    maskSn = c.sb("maskSn_s", [128, 3, 512], BF16)
    arena = c.sb("arena", [128, 16896], BF16)
    kvr = c.sb("kvr", [128, 36864], BF16)

    mmps = [c.ps("mm0", [128, 512], F32), c.ps("mm1", [128, 512], F32)]
    sps = [c.ps("s0", [128, 512], F32), c.ps("s1", [128, 512], F32)]
    nump = c.ps("nump", [128, 512], F32)
    denp = c.ps("denp", [128, 512], F32)
    auxp = c.ps("auxp", [128, 512], F32)
    trp = c.ps("trp", [128, 1024], BF16)

    del PHASES[:]

    def ph(name):
        PHASES.append((name, sum(1 for it in c.engs["pe"].prog if it[0] == "op")))

    B = {}

    def bf(name):
        if name not in B:
            B[name] = Buf(name)
        return B[name]

    def aview(off, shape, dtype=BF16):
        n = 1
        for s in shape[1:]:
            n *= s
        if dtype == F32:
            ap = arena[:, off:off + 2 * n].bitcast(F32)
        else:
            ap = arena[:, off:off + n]
        if len(shape) == 3:
            ap = ap.rearrange("p (a b) -> p a b", a=shape[1])
        elif len(shape) == 4:
            ap = ap.rearrange("p (a b c) -> p a b c", a=shape[1], b=shape[2])
        return ap

    def kview(off, shape):
        n = 1
        for s in shape[1:]:
            n *= s
        ap = kvr[:, off:off + n]
        if len(shape) == 3:
            ap = ap.rearrange("p (a b) -> p a b", a=shape[1])
        elif len(shape) == 4:
            ap = ap.rearrange("p (a b c) -> p a b c", a=shape[1], b=shape[2])
        return ap

    KT0 = kview(0, [128, 4, 2 * TT])
    KT1 = kview(4096, [128, 4, 2 * TT])
    KT2 = kview(8192, [128, 4, 5 * TT])
    VK0 = kview(18432, [128, 2, 4, 512])
    VK1 = kview(22528, [128, 2, 4, 512])
    V2A = kview(26624, [128, 16, 512])
    V2B = kview(34816, [128, 4, 512])

    wseq = []

    def wblock(key):
        i = WIDX[key]
        _, _, _, _, ncols, kcn = WB[i]
        wseq.append((key, i, kcn, ncols))

    def sched_l0(kind, t):
        if kind == "kv":
            if t == 3:
                wblock(("ev_in", "a"))
                gs = (0, 1, 2)
            else:
                gs = (2,)
            for g in gs:
                wblock(("ev_in", g, "k"))
                wblock(("ev_in", g, "v"))
            return
        wblock(("ev_in", "a"))
        for g in range(3):
            for nm in ("q", "k", "v"):
                wblock(("ev_in", g, nm))
        for j in range(2):
            wblock(("ev_out", j))
        sched_ffn_ple(0)

    def sched_ffn_ple(l):
        for j in range(8):
            wblock(("w1", l, j))
        for j in range(8):
            wblock(("w2", l, j))
        wblock(("proj", l))
        for j in range(2):
            wblock(("gate", l, j))

    def sched_l1(kind):
        names = ["u", "v", "go", "gi", "xi"]
        if kind == "halo":
            for j in (3, 4):
                wblock(("od_in", names[j]))
            return
        for j in range(5):
            wblock(("od_in", names[j]))
        for j in range(2):
            wblock(("od_out", j))
        sched_ffn_ple(1)

    tiles = [("kv", t) for t in range(4)] + [("full", t) for t in range(4, 9)] + [("sample", 9)]
    if _DBG.get("tiles") is not None:
        tiles = [tiles[i] for i in _DBG["tiles"]]
    for kind, t in tiles:
        if kind == "kv":
            sched_l0("kv", t)
        else:
            sched_l0("full", t)
            if kind == "full" and t == 4:
                sched_l1("halo")
            else:
                sched_l1("full")

    wstate = {"issued": 0, "next": 0}

    def w_issue_upto(n):
        while wstate["issued"] < min(n, len(wseq)):
            i = wstate["issued"]
            key, bi, kcn, ncols = wseq[i]
            slot = i % NW
            c.dma("pool", wsl[:, slot, 0:kcn * ncols], d_wall[bi, :, 0:kcn * ncols], writes=[bf("w%d" % slot)])
            wstate["issued"] += 1

    def getw(key, oldest=None):
        i = wstate["next"]
        assert wseq[i][0] == key, (wseq[i][0], key)
        _, bi, kcn, ncols = wseq[i]
        w_issue_upto((i if oldest is None else oldest) + NW - 1)
        slot = i % NW
        wstate["next"] += 1
        view = wsl[:, slot, 0:kcn * ncols].rearrange("p (k n) -> p k n", k=kcn)
        return view, bf("w%d" % slot)

    def w_advance():
        w_issue_upto(wstate["next"] + NW)

    rr = {"mm": 0, "s": 0, "sq": 0, "zb": 0, "scr": 0, "P": 0, "ev": 0, "trp": 0}

    WIDE = [(mmps[0], "mm0"), (mmps[1], "mm1"), (sps[0], "s0"), (sps[1], "s1"), (nump, "num"), (denp, "den")]
    SB4 = [(sps[0], "s0"), (sps[1], "s1"), (mmps[0], "mm0"), (mmps[1], "mm1")]

    def mm_bank():
        i = rr["mm"] % len(WIDE)
        rr["mm"] += 1
        return WIDE[i][0], bf(WIDE[i][1])

    def s_bank():
        i = rr["s"] % len(SB4)
        rr["s"] += 1
        return SB4[i][0], bf(SB4[i][1])

    def scr_buf():
        i = rr["scr"] % 3
        rr["scr"] += 1
        return scr[:, i, :], bf("scr%d" % i)

    def zb_buf():
        i = rr["zb"] % 2
        rr["zb"] += 1
        return zb[:, i, :], bf("zb%d" % i)

    def sq_buf():
        i = rr["sq"] % 2
        rr["sq"] += 1
        return sq[:, i, :], bf("sq%d" % i)

    def trp_half():
        i = rr["trp"] % 2
        rr["trp"] += 1
        return trp[:, 512 * i:512 * i + 512], bf("trp")

    def ev_eng():
        rr["ev"] += 1
        if _DBG.get("evac"):
            return _DBG["evac"]
        return "act" if rr["ev"] % 2 else "dve"

    def copy_op(eng, out, in_, reads, writes):
        if eng == "act":
            c.op("act", lambda e: e.copy(out, in_), reads=reads, writes=writes)
        else:
            c.op(eng, lambda e: e.tensor_copy(out, in_), reads=reads, writes=writes)

    def mm(ps_ap, lhsT, rhs, start, stop, reads, writes, inc):
        c.op("pe", lambda e: e.matmul(ps_ap, lhsT, rhs, start=start, stop=stop),
             reads=reads, writes=writes, inc=inc)

    c.dma("pool", ident[:], d_ident, writes=[bf("const")])
    c.dma("pool", rotm[:], d_rotm, writes=[bf("const")])
    c.dma("pool", poolw[:], d_pool_w, writes=[bf("const")])
    c.dma("pool", bsT[:], d_bsT, writes=[bf("const")])
    c.dma("pool", maskSc[:], d_maskSc, writes=[bf("const")])
    c.dma("pool", maskSn[:], d_maskSn, writes=[bf("const")])
    c.dma("sp", vecs[:], d_vecs, writes=[bf("const")])
    c.dma("sp", bsS[:], d_bsS, writes=[bf("const")])
    c.dma("sp", wsS[:], d_wsS, writes=[bf("const")])
    c.dma("sp", wsf[:], d_wsT, writes=[bf("wsf")])
    c.dma("sp", trim[:], d_tri, writes=[bf("trim")])
    c.op("dve", lambda e: e.memset(ones[:], 1.0), writes=[bf("const")])
    c.op("dve", lambda e: e.memset(onesn[:], 1.0 / 1024.0), writes=[bf("const")])
    c.op("dve", lambda e: e.memset(ones128[:], 1.0 / 128.0), writes=[bf("const")])
    c.op("dve", lambda e: e.memset(zeros[:], 0.0), writes=[bf("const")])
    for g in range(4):
        c.op("dve", lambda e, g=g: e.tensor_tensor(wmT[:, g, :], wsf[:, g, :], trim[:], ALU.mult),
             reads=[bf("wsf"), bf("trim")], writes=[bf("const")])
    c.op("dve", lambda e: e.memset(hd_halo[:], 0.0), writes=[bf("hd_halo")])
    c.op("dve", lambda e: e.memset(a_ext[:], 0.0), writes=[bf("a_ext")])
    CONST = bf("const")
    V_NMIX, V_NFFN, V_NPLE, V_NFIN = (0, 8), (16, 24), (32, 40), 48
    V_PSC, V_LNG, V_LNB, V_CW = 56, 60, 64, 68

    w_issue_upto(NW)

    deferred = []

    def flush_deferred(keep=0):
        while len(deferred) > keep:
            deferred.pop(0)()

    def dump(i, which, N):
        if o_dbg is None or N != NS:
            return
        dst = o_dbg[i].rearrange("(ch p) n -> p ch n", p=128)
        if which == "h":
            c.dma("pool", dst, hT[:, :, 0:N], reads=[bf("hT%d" % ch) for ch in range(8)], is_output=True)
        else:
            c.dma("sp", dst, rT[:, :, 0:N], reads=[bf("rT%d" % ch) for ch in range(8)], is_output=True)

    def norm_acc(ch, N, defer=True):
        s_ap, s_b = sq_buf()
        c.op("act", lambda e, ch=ch, s_ap=s_ap: e.activation(s_ap[:, 0:N], rT[:, ch, 0:N], AF.Square),
             reads=[bf("rT%d" % ch)], writes=[s_b])
        def later():
            mm(auxp[:, 0:N], onesn[:], s_ap[:, 0:N], ch == 0, ch == 7, [s_b, CONST], [bf("aux")], True)
        if defer:
            deferred.append(later)
        else:
            later()

    def rmsnorm(N, vcol, out_fn=None, pre=False):
        ss = auxp[:, 0:N]
        if not pre:
            for ch in range(8):
                norm_acc(ch, N, defer=False)
        flush_deferred()
        r_ap, r_b = scr_buf()
        c.op("act", lambda e: e.activation(r_ap[:, 0:N], ss, AF.Sqrt, bias=EPS, scale=1.0),
             reads=[bf("aux")], writes=[r_b])
        c.op("dve", lambda e: e.reciprocal(r_ap[:, 0:N], r_ap[:, 0:N]), reads=[r_b], writes=[r_b])
        for ch in range(8):
            if out_fn is None:
                if ch % 2 == 0:
                    c.op("dve", lambda e, ch=ch: e.scalar_tensor_tensor(
                        hT[:, ch, 0:N], rT[:, ch, 0:N], vecs[:, vcol + ch:vcol + ch + 1], r_ap[:, 0:N],
                        ALU.mult, ALU.mult), reads=[bf("rT%d" % ch), r_b, CONST], writes=[bf("hT%d" % ch)])
                else:
                    tz, tzb = zb_buf()
                    c.op("pool", lambda e, ch=ch, tz=tz: e.tensor_tensor(tz[:, 0:N], rT[:, ch, 0:N], r_ap[:, 0:N], ALU.mult),
                         reads=[bf("rT%d" % ch), r_b], writes=[tzb])
                    c.op("act", lambda e, ch=ch, tz=tz: e.activation(hT[:, ch, 0:N], tz[:, 0:N], AF.Identity,
                                                                     scale=vecs[:, vcol + ch:vcol + ch + 1]),
                         reads=[tzb, CONST], writes=[bf("hT%d" % ch)])
            else:
                out_fn(ch, r_ap, r_b)

    def linear(keys, rhs_fn, nk, N, evac, keep=0):
        for key in keys:
            wv, wb = getw(key)
            ncols = wv.shape[2]
            for oc in range(ncols // 128):
                ps, pb = mm_bank()
                for kc in range(nk):
                    rhs, rb = rhs_fn(kc)
                    mm(ps[:, 0:N], wv[:, kc, 128 * oc:128 * oc + 128], rhs, kc == 0, kc == nk - 1,
                       [wb, rb], [pb], kc == nk - 1)
                flush_deferred(keep)
                evac(key, oc, ps[:, 0:N], pb)
            w_advance()

    def h_rhs(N):
        return lambda kc: (hT[:, kc, 0:N], bf("hT%d" % kc))

    def zero_acc(ncol):
        mm(nump[:, 0:ncol], zeros[:], bsT[:, 0, 0:ncol], True, False, [CONST], [bf("num")], False)
        mm(denp[:, 0:ncol], zeros[:], bsT[:, 0, 0:ncol], True, False, [CONST], [bf("den")], True)

    def rope(ps, pb, N, out_full=None, out_halves=None, out_bufs=()):
        z_ap, z_b = zb_buf()
        copy_op("act", z_ap[:, 0:N], ps, [pb], [z_b])

        def stage2():
            rp, rpb = mm_bank()
            mm(rp[:, 0:N], rotm[:], z_ap[:, 0:N], True, True, [z_b, CONST], [rpb], True)
            t1, t1b = scr_buf()
            t2, t2b = scr_buf()
            c.op("dve", lambda e: e.tensor_tensor(t1[:, 0:N], z_ap[:, 0:N], cs[:, 0, 0:N], ALU.mult),
                 reads=[z_b, bf("cs")], writes=[t1b])
            c.op("dve", lambda e: e.tensor_tensor(t2[:, 0:N], rp[:, 0:N], cs[:, 1, 0:N], ALU.mult),
                 reads=[rpb, bf("cs")], writes=[t2b])
            if out_full is not None:
                c.op("pool", lambda e: e.tensor_tensor(out_full, t1[:, 0:N], t2[:, 0:N], ALU.add),
                     reads=[t1b, t2b], writes=list(out_bufs))
            else:
                oa, ob = out_halves
                c.op("pool", lambda e: e.tensor_tensor(oa[0:64, :], t1[0:64, 0:N], t2[0:64, 0:N], ALU.add),
                     reads=[t1b, t2b], writes=[out_bufs[0]])
                c.op("pool", lambda e: e.tensor_tensor(ob[64:128, :], t1[64:128, 0:N], t2[64:128, 0:N], ALU.add),
                     reads=[t1b, t2b], writes=[out_bufs[1]])
        deferred.append(stage2)

    def ffn_ple(l, N, p_ap, p_src):
        c.dma("pool", p_ap[:, :, 0:N], p_src.rearrange("(k p) n -> p k n", p=128), writes=[bf("pTs")])
        hid = aview(0, [128, 32, TT])
        hb = [bf("hid%d" % f) for f in range(32)]
        c.handoff(ARENA_BUFS[0], hb)
        ARENA_BUFS[0] = hb
        ph("ffn%d" % l)
        rmsnorm(N, V_NFFN[l], pre=True)

        def ev1(key, oc, ps, pb):
            f = key[2] * 4 + oc
            r_ap, r_b = zb_buf()
            if f % 2 == 0:
                c.op("act", lambda e: e.activation(r_ap[:, 0:N], ps, AF.Relu), reads=[pb], writes=[r_b])
            else:
                c.op("dve", lambda e: e.tensor_scalar(r_ap[:, 0:N], ps, 0.0, None, ALU.max), reads=[pb], writes=[r_b])
            c.op("pool", lambda e: e.tensor_tensor(hid[:, f, 0:N], r_ap[:, 0:N], r_ap[:, 0:N], ALU.mult),
                 reads=[r_b], writes=[hb[f]])
        linear([("w1", l, j) for j in range(8)], h_rhs(N), 8, N, ev1)

        def ev2(key, oc, ps, pb):
            ch = key[2]
            c.op("dve", lambda e: e.tensor_tensor(rT[:, ch, 0:N], rT[:, ch, 0:N], ps, ALU.add),
                 reads=[pb, bf("rT%d" % ch)], writes=[bf("rT%d" % ch)])
            norm_acc(ch, N)
        linear([("w2", l, j) for j in range(8)], lambda kc: (hid[:, kc, 0:N], hb[kc]), 32, N, ev2)
        dump(3 + 4 * l, "r", N)
        ph("ple%d" % l)
        rmsnorm(N, V_NPLE[l], pre=True)
        ip = wstate["next"]
        wp, wpb = getw(("proj", l))
        pend_sq = [None]
        for j in range(2):
            wg, wgb = getw(("gate", l, j), oldest=ip)
            for oc in range(4):
                ch = 4 * j + oc
                psg, pgb = mm_bank()
                for kc in range(8):
                    mm(psg[:, 0:N], wg[:, kc, 128 * oc:128 * oc + 128], hT[:, kc, 0:N], kc == 0, kc == 7,
                       [wgb, bf("hT%d" % kc)], [pgb], kc == 7)
                psp, ppb = mm_bank()
                for kc in range(2):
                    mm(psp[:, 0:N], wp[:, kc, 128 * ch:128 * ch + 128], p_ap[:, kc, 0:N], kc == 0, kc == 1,
                       [wpb, bf("pTs")], [ppb], kc == 1)
                flush_deferred()
                g_ap, g_b = scr_buf()
                c.op("act", lambda e, g_ap=g_ap, psg=psg: e.activation(g_ap[:, 0:N], psg[:, 0:N], AF.Sigmoid),
                     reads=[pgb], writes=[g_b])
                if pend_sq[0] is not None:
                    norm_acc(pend_sq[0], N)
                t_ap, t_b = scr_buf()
                c.op("dve", lambda e, g_ap=g_ap, t_ap=t_ap, psp=psp: e.tensor_tensor(
                    t_ap[:, 0:N], g_ap[:, 0:N], psp[:, 0:N], ALU.mult), reads=[g_b, ppb], writes=[t_b])
                c.op("pool", lambda e, t_ap=t_ap, ch=ch: e.tensor_tensor(
                    rT[:, ch, 0:N], rT[:, ch, 0:N], t_ap[:, 0:N], ALU.add),
                    reads=[t_b, bf("rT%d" % ch)], writes=[bf("rT%d" % ch)])
                pend_sq[0] = ch
        norm_acc(pend_sq[0], N)
        w_advance()
        dump(4 + 4 * l, "r", N)

    ARENA_BUFS = [[bf("arena_init")]]
    RT = [bf("rT%d" % ch) for ch in range(8)]
    cur = {"i": 0}
    xloaded = set()

    def x_load_chunk(i, ch):
        if (i, ch) in xloaded:
            return
        xloaded.add((i, ch))
        kind_, t_ = tiles[i]
        if kind_ == "sample":
            c.dma("sp", rT[:, ch, 0:NS], d_xsT[128 * ch:128 * ch + 128, :], writes=[RT[ch]])
        else:
            c.dma("sp", rT[:, ch, :], d_xT[128 * ch:128 * ch + 128, t_ * TT:(t_ + 1) * TT], writes=[RT[ch]])

    def x_prefetch_next(ch=None):
        i = cur["i"] + 1
        if i >= len(tiles):
            return
        for k in ([ch] if ch is not None else range(8)):
            x_load_chunk(i, k)
    KV_BUFS = [bf("kv_KT0"), bf("kv_KT1"), bf("kv_KT2r"), bf("kv_KT2c"), bf("kv_VK0"), bf("kv_VK1"),
               bf("kv_V2A"), bf("kv_V2B")]

    A_QA, A_QB = 0, 6144
    A_VT = 12288
    A_P = 14336
    A_RD = 15872

    def layer0_prompt(kind, t):
        N = TT
        full = kind == "full"
        main = full and t >= 5
        cur, prev = t % 2, (t + 1) % 2
        tok0 = (t - 5) * TT
        QA = aview(A_QA, [128, 3, 4, TT])
        QB = aview(A_QB, [128, 3, 4, TT])
        VT = aview(A_VT, [128, 4, TT])
        Pb = aview(A_P, [128, 3, TT])
        rden = aview(A_RD, [128, TT], F32)
        ab = {n: bf("ar_" + n) for n in ["QA", "QB", "VT", "P0", "P1", "P2", "rden"]}
        c.handoff(ARENA_BUFS[0], list(ab.values()))
        ARENA_BUFS[0] = list(ab.values())
        if full:
            c.op("dve", lambda e: e.memset(arena[64:128, 0:6144], 0.0), writes=[ab["QA"]])
            c.op("dve", lambda e: e.memset(arena[0:64, 6144:12288], 0.0), writes=[ab["QB"]])
        ph("L0norm %s%d" % (kind, t))
        ck("pre")
        rmsnorm(N, V_NMIX[0])
        ck("norm")
        if not full:
            x_prefetch_next()
        ph("L0proj")
        gs = (0, 1, 2) if (full or t == 3) else (2,)
        KTs = [KT0, KT1, KT2]
        kbufs = [bf("kv_KT0"), bf("kv_KT1"), bf("kv_KT2c")]

        def kslot(g, ch):
            if g == 2:
                return KT2[:, ch, 4 * TT:5 * TT]
            return KTs[g][:, ch, cur * TT:(cur + 1) * TT]

        if full or t == 3:
            def ev_a(key, oc, ps, pb):
                copy_op("act", a_ext[:, oc, 15:15 + N], ps, [pb], [bf("a_ext")])
            linear([("ev_in", "a")], h_rhs(N), 8, N, ev_a)
        for g in gs:
            if full:
                def ev_q(key, oc, ps, pb, g=g):
                    rope(ps, pb, N, out_halves=(QA[:, g, oc, :], QB[:, g, oc, :]), out_bufs=(ab["QA"], ab["QB"]))
                linear([("ev_in", g, "q")], h_rhs(N), 8, N, ev_q)

            def ev_k(key, oc, ps, pb, g=g):
                rope(ps, pb, N, out_full=kslot(g, oc), out_bufs=(kbufs[g],))
            linear([("ev_in", g, "k")], h_rhs(N), 8, N, ev_k)
            ck("k")

            def ev_v(key, oc, ps, pb):
                copy_op(ev_eng(), VT[:, oc, :], ps, [pb], [ab["VT"]])
            linear([("ev_in", g, "v")], h_rhs(N), 8, N, ev_v)
            flush_deferred()
            ck("v")
            if main:
                ko = o_kvT[g, 0, :, tok0:tok0 + TT].rearrange("(ch p) n -> p ch n", p=128)
                ksrc = KT2[:, :, 4 * TT:5 * TT] if g == 2 else KTs[g][:, :, cur * TT:(cur + 1) * TT]
                c.dma("pool", ko, ksrc, reads=[kbufs[g]], is_output=True)
                vo = o_kvT[g, 1, :, tok0:tok0 + TT].rearrange("(ch p) n -> p ch n", p=128)
                c.dma("pool", vo, VT[:, :, :], reads=[ab["VT"]], is_output=True)
            for b in range(4):
                th, thb = trp_half()
                for ch in range(4):
                    src = VT[:, ch, 128 * b:128 * b + 128] if g == 0 else VT[:, ch, b:TT:4]
                    c.op("pe", lambda e, th=th, ch=ch, src=src: e.transpose(th[:, 128 * ch:128 * ch + 128], src, ident[:]),
                         reads=[ab["VT"], CONST], writes=[thb], inc=(ch == 3))
                ck("trb%d" % b)
                if g == 0:
                    copy_op(ev_eng(), VK0[:, cur, b, :], th, [thb], [bf("kv_VK0")])
                elif g == 1:
                    copy_op(ev_eng(), VK1[:, cur, b, :], th, [thb], [bf("kv_VK1")])
                else:
                    copy_op(ev_eng(), V2B[:, b, :], th, [thb], [bf("kv_V2B")])
                ck("evb%d" % b)
        ph("L0att")
        ck("tr")
        if full:
            attention_prompt(t, QA, QB, Pb, rden, ab)
        ph("L0ring")
        ck("att")
        s = t % 4
        c.op("act", lambda e: e.copy(KT2[:, :, s * TT:(s + 1) * TT], KT2[:, :, 4 * TT:5 * TT]),
             reads=[bf("kv_KT2c")], writes=[bf("kv_KT2r")])
        for j in range(4):
            c.dma("sp", V2A[32 * s:32 * s + 32, 4 * j:4 * j + 4, :], V2B[j:128:4, :, :],
                  reads=[bf("kv_V2B")], writes=[bf("kv_V2A")])
        ck("ring")
        if not full:
            if t == 3:
                c.op("dve", lambda e: e.tensor_copy(a_ext[:, :, 0:15], a_ext[:, :, TT:TT + 15]),
                     reads=[bf("a_ext")], writes=[bf("a_ext")])
            return
        ph("L0pool")
        pool_mixer_prompt(t)
        ph("L0wout")
        if t == 4:
            HB = [bf("hT%d" % ch) for ch in range(8)]
            c.op("dve", lambda e: e.tensor_copy(hT[:, :, 0:2], hT[:, :, TT - 2:TT]), reads=HB, writes=HB)
            c.op("dve", lambda e: e.tensor_copy(rT[:, :, 0:2], rT[:, :, TT - 2:TT]), reads=RT, writes=RT)
            N = 2
        def ev_o(key, oc, ps, pb):
            ch = key[1] * 4 + oc
            c.op("dve", lambda e: e.tensor_tensor(rT[:, ch, 0:N], rT[:, ch, 0:N], ps, ALU.add),
                 reads=[pb, bf("rT%d" % ch)], writes=[bf("rT%d" % ch)])
            norm_acc(ch, N)
        linear([("ev_out", j) for j in range(2)], h_rhs(N), 8, N, ev_o, keep=1)
        if t == 4:
            ffn_ple(0, N, pTs, d_pT[0, :, TT - 2:TT])
        else:
            ffn_ple(0, N, pTs, d_pT[0, :, (t - 4) * TT:(t - 3) * TT])

    def attention_prompt(t, QA, QB, Pb, rden, ab):
        cur, prev = t % 2, (t + 1) % 2
        pbufs = [ab["P0"], ab["P1"], ab["P2"]]
        kb0, kb1, kb2r, kb2c = bf("kv_KT0"), bf("kv_KT1"), bf("kv_KT2r"), bf("kv_KT2c")
        vb0, vb1, vb2a, vb2b = bf("kv_VK0"), bf("kv_VK1"), bf("kv_V2A"), bf("kv_V2B")
        for ch in range(4):
            zero_acc(TT)
            all_units = []
            for hh in range(2):
                Q = QA if hh == 0 else QB
                qb_ = ab["QA"] if hh == 0 else ab["QB"]
                po = 64 * hh
                fo = 128 * ch + 64 * hh
                units = []
                for half in range(2):
                    items = []
                    for qi in range(2):
                        qb = 2 * half + qi
                        q_ap = Q[:, 0, ch, 128 * qb:128 * qb + 128]
                        if qb == 0:
                            kp = KT0[:, ch, prev * TT + 384:prev * TT + 512]
                            vp = VK0[:, prev, 3, fo:fo + 64]
                        else:
                            kp = KT0[:, ch, cur * TT + 128 * (qb - 1):cur * TT + 128 * qb]
                            vp = VK0[:, cur, qb - 1, fo:fo + 64]
                        kc_ = KT0[:, ch, cur * TT + 128 * qb:cur * TT + 128 * qb + 128]
                        vc_ = VK0[:, cur, qb, fo:fo + 64]
                        oc_ = (128 * qb, 128 * qb + 128, 1)
                        items.append((kp, kb0, q_ap, (2 * qi) * 128, 128, vp, vb0, oc_, False))
                        items.append((kc_, kb0, q_ap, (2 * qi + 1) * 128, 128, vc_, vb0, oc_, False))
                    units.append((items, half * 512))
                for half in range(2):
                    items = []
                    for qi in range(2):
                        r4 = 2 * half + qi
                        q_ap = Q[:, 1, ch, r4:TT:4]
                        kp = KT1[:, ch, prev * TT + r4:prev * TT + TT:4]
                        kc_ = KT1[:, ch, cur * TT + r4:cur * TT + TT:4]
                        vp = VK1[:, prev, r4, fo:fo + 64]
                        vc_ = VK1[:, cur, r4, fo:fo + 64]
                        oc_ = (r4, TT, 4)
                        items.append((kp, kb1, q_ap, (2 * qi) * 128, 128, vp, vb1, oc_, False))
                        items.append((kc_, kb1, q_ap, (2 * qi + 1) * 128, 128, vc_, vb1, oc_, False))
                    units.append((items, 1024 + half * 512))
                items = []
                for r in range(16):
                    items.append((KT2[:, ch, r:4 * TT:16], kb2r, Q[:, 2, ch, r:TT:16], 32 * r, 32,
                                  V2A[:, r, fo:fo + 64], vb2a, (r, TT, 16), False))
                units.append((items, 2048))
                items = []
                for r4 in range(4):
                    for rho in range(4):
                        r = r4 + 4 * rho
                        items.append((KT2[:, ch, 4 * TT + r4:5 * TT:4], kb2c, Q[:, 2, ch, r:TT:16],
                                      (4 * r4 + rho) * 32, 32, V2B[:, r4, fo:fo + 64], vb2b, (r, TT, 16), False))
                units.append((items, 2560))
                if t == 4:
                    def need(it):
                        o0, o1, os_ = it[7]
                        cols = range(o0, o1, os_)
                        return 510 in cols or 511 in cols
                    units = [([it for it in items if need(it)], mcol) for items, mcol in units]
                    units = [u for u in units if u[0]]
                for ui, (items, mcol) in enumerate(units):
                    all_units.append((items, mcol, qb_, po, ui == len(units) - 1 and hh == 1))

            def stage_s(u):
                items, mcol, qb_, po, last_unit = u
                sp_, sb_ = s_bank()
                for ii, (k_ap, kb_, q_ap, scol, ncol, v_ap, vb_, oc_, st) in enumerate(items):
                    mm(sp_[:, scol:scol + ncol], k_ap, q_ap, True, True, [kb_, qb_], [sb_], ii == len(items) - 1)
                pi = rr["P"] % 3
                rr["P"] += 1
                P = Pb[:, pi, :]
                c.op("act", lambda e, P=P, sp_=sp_: e.activation(P, sp_[:, :], AF.Exp, scale=0.125),
                     reads=[sb_], writes=[pbufs[pi]])
                c.op("dve", lambda e, P=P, mcol=mcol: e.tensor_tensor(P, P, msk[:, mcol:mcol + 512], ALU.mult),
                     reads=[pbufs[pi], bf("msk")], writes=[pbufs[pi]])
                return (P, pi)

            def stage_pv(u, pp):
                items, mcol, qb_, po, last_unit = u
                P, pi = pp
                for ii, (k_ap, kb_, q_ap, scol, ncol, v_ap, vb_, oc_, st) in enumerate(items):
                    lastmm = last_unit and ii == len(items) - 1
                    o0, o1, os_ = oc_
                    mm(nump[po:po + 64, o0:o1:os_], v_ap, P[:, scol:scol + ncol], False, lastmm,
                       [vb_, pbufs[pi]], [bf("num")], False)
                    mm(denp[po:po + 64, o0:o1:os_], ones[:, 0:64], P[:, scol:scol + ncol], False, lastmm,
                       [CONST, pbufs[pi]], [bf("den")], ii == len(items) - 1)

            pend = None
            for u in all_units:
                pp = stage_s(u)
                if pend is not None:
                    stage_pv(*pend)
                pend = (u, pp)
            stage_pv(*pend)
            c.op("dve", lambda e: e.reciprocal(rden[:, :], denp[:, :]), reads=[bf("den")], writes=[ab["rden"]])
            c.op("dve", lambda e, ch=ch: e.tensor_tensor(hT[:, 4 + ch, :], nump[:, :], rden[:, :], ALU.mult),
                 reads=[bf("num"), ab["rden"]], writes=[bf("hT%d" % (4 + ch))])

    def pool_mixer_prompt(t):
        N = TT
        W = 15 + N
        pa = aview(0, [128, 4, 528], F32)
        pb_ = aview(4224, [128, 4, 528], F32)
        icn = aview(8448, [128, 4, TT], BF16)
        pld = aview(10496, [128, 4, TT], BF16)
        nb = {n: bf("pl_" + n) for n in ["pa", "pb", "icn", "pld"]}
        c.handoff(ARENA_BUFS[0], list(nb.values()))
        ARENA_BUFS[0] = list(nb.values())
        c.dma("pool", icn[:, :, :], d_icnt[t - 4], writes=[nb["icn"]])
        for g in range(4):
            src, srcb = a_ext[:, g, :], bf("a_ext")
            off = 0
            bufs = [(pa[:, g, :], nb["pa"]), (pb_[:, g, :], nb["pb"])]
            for k in range(g + 1):
                sh = 1 << k
                dst, dstb = bufs[k % 2]
                eng = "dve" if (g + k) % 2 == 0 else "pool"
                c.op(eng, lambda e, dst=dst, src=src, off=off, sh=sh: e.tensor_tensor(
                    dst[:, off + sh:W], src[:, off + sh:W], src[:, off:W - sh], ALU.add),
                    reads=[srcb], writes=[dstb])
                src, srcb = dst, dstb
                off += sh
            tmp, tmpb = bufs[(g + 1) % 2]
            c.op("dve", lambda e, tmp=tmp, src=src, g=g: e.tensor_tensor(
                tmp[:, 15:W], src[:, 15:W], icn[:, g, :], ALU.mult), reads=[srcb, nb["icn"]], writes=[tmpb])
            c.op("pool", lambda e, tmp=tmp, g=g: e.tensor_tensor(
                pld[:, g, :], tmp[:, 15:W], a_ext[:, g, 15:W], ALU.subtract),
                reads=[tmpb, bf("a_ext")], writes=[nb["pld"]])
            ps, pb2 = mm_bank()
            mm(ps[:, 0:N], poolw[:, g, :], pld[:, g, :], True, True, [CONST, nb["pld"]], [pb2], True)
            c.op("act", lambda e, g=g, ps=ps: e.activation(hT[:, g, 0:N], ps[:, 0:N], AF.Identity,
                                                           scale=vecs[:, V_PSC + g:V_PSC + g + 1]),
                 reads=[pb2, CONST], writes=[bf("hT%d" % g)])
        if t == 8:
            c.dma("sp", o_poolT, a_ext[:, :, TT:TT + 15], reads=[bf("a_ext")], is_output=True)
        c.op("dve", lambda e: e.tensor_copy(a_ext[:, :, 0:15], a_ext[:, :, TT:TT + 15]),
             reads=[bf("a_ext")], writes=[bf("a_ext")])

    L1_U, L1_VN, L1_VTK, L1_GO, L1_GI, L1_HD = 0, 2048, 4096, 6144, 8192, 10240

    def layer1(kind, t, N):
        sample = kind == "sample"
        halo = kind == "halo"
        uT = aview(L1_U, [128, 4, TT])
        vnT = aview(L1_VN, [128, 4, TT])
        vtk = aview(L1_VTK, [128, 4, TT])
        go = aview(L1_GO, [128, 4, TT])
        gi = aview(L1_GI, [128, 4, TT])
        nb = {n: bf("l1_" + n) for n in ["u", "vn", "vtk", "go", "gi", "hd"]}
        c.handoff(ARENA_BUFS[0], list(nb.values()))
        ARENA_BUFS[0] = list(nb.values())
        if sample:
            hd = aview(L1_HD, [128, 4, 16, 6], F32)
            c.dma("sp", scr[:, 2, 0:128], d_sconv, writes=[bf("scr2")])
            c.op("dve", lambda e: e.tensor_copy(hd[:, :, :, 0:2], scr[:, 2, 0:128].rearrange("p (a b c) -> p a b c", a=4, b=16)),
                 reads=[bf("scr2")], writes=[nb["hd"]])
        else:
            hd = aview(L1_HD, [128, 4, 516], F32)
        ph("L1norm %s" % kind)
        rmsnorm(N, V_NMIX[1], pre=True)
        ph("L1proj")
        dump(9, "h", N)
        if not halo:
            def ev_u(key, oc, ps, pb):
                c.op("act", lambda e: e.activation(uT[:, oc, 0:N], ps, AF.Gelu), reads=[pb], writes=[nb["u"]])
            linear([("od_in", "u")], h_rhs(N), 8, N, ev_u)

            def ev_v(key, oc, ps, pb):
                z_ap, z_b = zb_buf()
                c.op("act", lambda e: e.activation(z_ap[:, 0:N], ps, AF.Gelu), reads=[pb], writes=[z_b])
                mm(auxp[:, 0:N], ones128[:], z_ap[:, 0:N], True, True, [z_b, CONST], [bf("aux")], True)
                vc, vcb = scr_buf()
                c.op("dve", lambda e: e.tensor_tensor(vc[:, 0:N], z_ap[:, 0:N], auxp[:, 0:N], ALU.subtract),
                     reads=[z_b, bf("aux")], writes=[vcb])
                s_ap, s_b = sq_buf()
                c.op("act", lambda e: e.activation(s_ap[:, 0:N], vc[:, 0:N], AF.Square), reads=[vcb], writes=[s_b])
                mm(auxp[:, 0:N], ones128[:], s_ap[:, 0:N], True, True, [s_b, CONST], [bf("aux")], True)
                sd, sdb = scr_buf()
                c.op("act", lambda e: e.activation(sd[:, 0:N], auxp[:, 0:N], AF.Sqrt, bias=EPS, scale=1.0),
                     reads=[bf("aux")], writes=[sdb])
                c.op("dve", lambda e: e.reciprocal(sd[:, 0:N], sd[:, 0:N]), reads=[sdb], writes=[sdb])
                c.op("dve", lambda e: e.tensor_tensor(vc[:, 0:N], vc[:, 0:N], sd[:, 0:N], ALU.mult),
                     reads=[vcb, sdb], writes=[vcb])
                c.op("act", lambda e: e.activation(vnT[:, oc, 0:N], vc[:, 0:N], AF.Identity,
                                                   bias=vecs[:, V_LNB + oc:V_LNB + oc + 1],
                                                   scale=vecs[:, V_LNG + oc:V_LNG + oc + 1]),
                     reads=[vcb, CONST], writes=[nb["vn"]])
            linear([("od_in", "v")], h_rhs(N), 8, N, ev_v)
            if sample:
                c.dma("pool", o_cvsT.rearrange("(ch p) n -> p ch n", p=128), vnT[:, :, 0:N],
                      reads=[nb["vn"]], is_output=True)
            def ev_go(key, oc, ps, pb):
                copy_op("act", go[:, oc, 0:N], ps, [pb], [nb["go"]])
            linear([("od_in", "go")], h_rhs(N), 8, N, ev_go)

        def ev_gi(key, oc, ps, pb):
            copy_op("act", gi[:, oc, 0:N], ps, [pb], [nb["gi"]])
        linear([("od_in", "gi")], h_rhs(N), 8, N, ev_gi)
        if not sample:
            c.op("pool", lambda e: e.tensor_copy(hd[:, :, 0:2], hd_halo[:, :, :]), reads=[bf("hd_halo")], writes=[nb["hd"]])

        def ev_xi(key, oc, ps, pb):
            if sample:
                c.op("dve", lambda e: e.tensor_tensor(hd[:, oc, :, 2:6], ps.rearrange("p (n i) -> p n i", i=4),
                                                      gi[:, oc, 0:N].rearrange("p (n i) -> p n i", i=4), ALU.mult),
                     reads=[pb, nb["gi"]], writes=[nb["hd"]])
            else:
                c.op("dve", lambda e: e.tensor_tensor(hd[:, oc, 2:2 + N], ps, gi[:, oc, 0:N], ALU.mult),
                     reads=[pb, nb["gi"]], writes=[nb["hd"]])
        linear([("od_in", "xi")], h_rhs(N), 8, N, ev_xi)
        if not sample:
            c.op("pool", lambda e: e.tensor_copy(hd_halo[:, :, :], hd[:, :, N:N + 2]), reads=[nb["hd"]], writes=[bf("hd_halo")])
            if t == 8:
                c.dma("sp", o_convT, hd[:, :, N:N + 2], reads=[nb["hd"]], is_output=True)
        else:
            c.op("dve", lambda e: e.tensor_copy(scr[:, 2, 0:128].rearrange("p (a b c) -> p a b c", a=4, b=16), hd[:, :, :, 4:6]),
                 reads=[nb["hd"]], writes=[bf("scr2")])
            c.dma("sp", o_convsT, scr[:, 2, 0:128], reads=[bf("scr2")], is_output=True)
        if halo:
            return
        ph("L1gate")
        for g in range(4):
            tmp, tmpb = scr_buf()
            if not sample:
                th, thb = trp_half()
                for blk in range(4):
                    c.op("pe", lambda e, th=th, blk=blk, g=g: e.transpose(
                        th[:, 128 * blk:128 * blk + 128], vnT[:, g, 128 * blk:128 * blk + 128], ident[:]),
                        reads=[nb["vn"], CONST], writes=[thb], inc=(blk == 3))
                copy_op(ev_eng(), vtk[:, g, :], th, [thb], [nb["vtk"]])
                ps, pb = mm_bank()
                for blk in range(4):
                    mm(ps[:, 128 * blk:128 * blk + 128], vtk[:, g, 128 * blk:128 * blk + 128], wmT[:, g, :],
                       True, True, [nb["vtk"], CONST], [pb], blk == 3)
                c.op("dve", lambda e, tmp=tmp, ps=ps, g=g: e.tensor_tensor(tmp[:, 0:N], ps[:, 0:N], bsT[:, g, :], ALU.add),
                     reads=[pb, CONST], writes=[tmpb])
            else:
                vv = vnT[:, g, 0:N].rearrange("p (n i) -> p n i", i=4)
                tv = tmp[:, 0:N].rearrange("p (n i) -> p n i", i=4)
                for ti in range(4):
                    c.op("act", lambda e, ti=ti, g=g, tv=tv, vv=vv: e.activation(
                        tv[:, :, ti], vv[:, :, 0], AF.Identity, scale=wsS[:, g, 4 * ti:4 * ti + 1]),
                        reads=[nb["vn"], CONST], writes=[tmpb])
                    for si in range(1, ti + 1):
                        c.op("dve", lambda e, ti=ti, si=si, g=g, tv=tv, vv=vv: e.scalar_tensor_tensor(
                            tv[:, :, ti], vv[:, :, si], wsS[:, g, 4 * ti + si:4 * ti + si + 1], tv[:, :, ti],
                            ALU.mult, ALU.add), reads=[nb["vn"], CONST, tmpb], writes=[tmpb])
                c.op("dve", lambda e, tmp=tmp, g=g: e.tensor_tensor(tmp[:, 0:N], tmp[:, 0:N], bsS[:, g, :], ALU.add),
                     reads=[tmpb, CONST], writes=[tmpb])
            c.op("dve", lambda e, tmp=tmp, g=g: e.tensor_tensor(hT[:, g, 0:N], tmp[:, 0:N], uT[:, g, 0:N], ALU.mult),
                 reads=[tmpb, nb["u"]], writes=[bf("hT%d" % g)])

        ph("L1conv")
        for ch in range(4):
            acc, accb = scr_buf()
            cw = lambda j, ch=ch: vecs[:, V_CW + 3 * ch + j:V_CW + 3 * ch + j + 1]
            if sample:
                av = acc[:, 0:N].rearrange("p (n i) -> p n i", i=4)
                hv = lambda j, ch=ch: hd[:, ch, :, j:j + 4]
            else:
                av = acc[:, 0:N]
                hv = lambda j, ch=ch: hd[:, ch, j:j + N]
            c.op("act", lambda e, av=av, hv=hv, cw=cw: e.activation(av, hv(0), AF.Identity, scale=cw(0)),
                 reads=[nb["hd"], CONST], writes=[accb])
            for j in (1, 2):
                c.op("dve", lambda e, av=av, hv=hv, cw=cw, j=j: e.scalar_tensor_tensor(
                    av, hv(j), cw(j), av, ALU.mult, ALU.add), reads=[nb["hd"], CONST, accb], writes=[accb])
            c.op("dve", lambda e, acc=acc, ch=ch: e.tensor_tensor(hT[:, 4 + ch, 0:N], go[:, ch, 0:N], acc[:, 0:N], ALU.mult),
                 reads=[accb, nb["go"]], writes=[bf("hT%d" % (4 + ch))])

        def ev_o(key, oc, ps, pb):
            ch = key[1] * 4 + oc
            c.op("dve", lambda e: e.tensor_tensor(rT[:, ch, 0:N], rT[:, ch, 0:N], ps, ALU.add),
                 reads=[pb, bf("rT%d" % ch)], writes=[bf("rT%d" % ch)])
            norm_acc(ch, N)
        dump(5, "h", N)
        ph("L1wout")
        linear([("od_out", j) for j in range(2)], h_rhs(N), 8, N, ev_o, keep=1)
        dump(6, "r", N)
        ffn_ple(1, N, pTs, d_psT[1] if sample else d_pT[1, :, (t - 4) * TT:(t - 3) * TT])

    def final_out(N, o_ap, col0):
        def out_fn(ch, r_ap, r_b):
            ridx = (rr["scr"] - 1) % 3 if ch == 0 else out_fn.ridx
            out_fn.ridx = ridx
            yi = (ridx + 1 + (ch % 2)) % 3
            y, yb = scr[:, yi, :], bf("scr%d" % yi)
            c.op("dve", lambda e: e.scalar_tensor_tensor(y[:, 0:N], rT[:, ch, 0:N], vecs[:, V_NFIN + ch:V_NFIN + ch + 1],
                                                         r_ap[:, 0:N], ALU.mult, ALU.mult),
                 reads=[bf("rT%d" % ch), r_b, CONST], writes=[yb])
            c.dma("sp", o_ap[128 * ch:128 * ch + 128, col0:col0 + N], y[:, 0:N], reads=[yb], is_output=True)
            x_prefetch_next(ch)
        ph("final")
        rmsnorm(N, V_NFIN, out_fn=out_fn, pre=True)

    def layer0_sample():
        N = NS
        QA = aview(A_QA, [128, 3, 4, TT])
        QB = aview(A_QB, [128, 3, 4, TT])
        VT = aview(A_VT, [128, 4, TT])
        Pb = aview(A_P, [128, 3, TT])
        rden = aview(A_RD, [128, TT], F32)
        ab = {n: bf("ar_" + n) for n in ["QA", "QB", "VT", "P0", "P1", "P2", "rden"]}
        c.handoff(ARENA_BUFS[0], list(ab.values()))
        ARENA_BUFS[0] = list(ab.values())
        kcb = [kview(0, [128, 9, 4, 128]), kview(4608, [128, 9, 4, 128])]
        vcb = [kview(9216, [128, 9, 512]), kview(13824, [128, 9, 512])]
        KTs = kview(18432, [128, 3, 4, NS])
        VsT = kview(19200, [128, 3, 512])
        Pn = kview(20736, [128, 512])
        a_s = kvr[:, 21248:21248 + 2 * 4 * 16 * 19].bitcast(F32).rearrange("p (a b c) -> p a b c", a=4, b=16)
        p1 = kvr[:, 23680:23680 + 2 * 16 * 19].bitcast(F32).rearrange("p (b c) -> p b c", b=16)
        p2 = kvr[:, 24288:24288 + 2 * 16 * 19].bitcast(F32).rearrange("p (b c) -> p b c", b=16)
        plds = kview(24896, [128, 4, NS])
        sb_ = {n: bf("sk_" + n) for n in ["kc0", "kc1", "vc0", "vc1", "KTs", "VsT", "Pn", "a_s", "p1", "p2", "pld"]}
        c.handoff(KV_BUFS, list(sb_.values()))
        c.op("dve", lambda e: e.memset(arena[64:128, 0:6144], 0.0), writes=[ab["QA"]])
        c.op("dve", lambda e: e.memset(arena[0:64, 6144:12288], 0.0), writes=[ab["QB"]])
        c.op("pool", lambda e: e.memset(VsT[:, :, :], 0.0), writes=[sb_["VsT"]])
        c.op("pool", lambda e: e.memset(Pn[:, :], 0.0), writes=[sb_["Pn"]])
        stg = scr[:, 0:2, :].rearrange("p a b -> p (a b)")
        c.dma("sp", stg[:, 0:960], d_spool, writes=[bf("scr0"), bf("scr1")])
        c.op("dve", lambda e: e.tensor_copy(a_s[:, :, :, 0:15], stg[:, 0:960].rearrange("p (a b c) -> p a b c", a=4, b=16)),
             reads=[bf("scr0"), bf("scr1")], writes=[sb_["a_s"]])
        ph("S L0norm")
        rmsnorm(N, V_NMIX[0])
        dump(0, "h", N)
        ph("S L0proj")

        def ev_a(key, oc, ps, pb):
            c.op("act", lambda e: e.copy(a_s[:, oc, :, 15:19], ps.rearrange("p (n i) -> p n i", i=4)),
                 reads=[pb], writes=[sb_["a_s"]])
        linear([("ev_in", "a")], h_rhs(N), 8, N, ev_a)
        for g in range(3):
            def ev_q(key, oc, ps, pb, g=g):
                rope(ps, pb, N, out_halves=(QA[:, g, oc, 0:N], QB[:, g, oc, 0:N]), out_bufs=(ab["QA"], ab["QB"]))
            linear([("ev_in", g, "q")], h_rhs(N), 8, N, ev_q)

            def ev_k(key, oc, ps, pb, g=g):
                rope(ps, pb, N, out_full=KTs[:, g, oc, :], out_bufs=(sb_["KTs"],))
            linear([("ev_in", g, "k")], h_rhs(N), 8, N, ev_k)

            def ev_v(key, oc, ps, pb):
                copy_op(ev_eng(), VT[:, oc, 0:N], ps, [pb], [ab["VT"]])
            linear([("ev_in", g, "v")], h_rhs(N), 8, N, ev_v)
            flush_deferred()
            c.dma("pool", o_kvsT[g, 0].rearrange("(ch p) n -> p ch n", p=128), KTs[:, g, :, :],
                  reads=[sb_["KTs"]], is_output=True)
            c.dma("pool", o_kvsT[g, 1].rearrange("(ch p) n -> p ch n", p=128), VT[:, :, 0:N],
                  reads=[ab["VT"]], is_output=True)
            th, thb = trp_half()
            for ch in range(4):
                c.op("pe", lambda e, th=th, ch=ch: e.transpose(th[0:N, 128 * ch:128 * ch + 128], VT[:, ch, 0:N], ident[:]),
                     reads=[ab["VT"], CONST], writes=[thb], inc=(ch == 3))
            copy_op(ev_eng(), VsT[0:N, g, :], th[0:N, :], [thb], [sb_["VsT"]])
        ph("S att new")
        pbufs = [ab["P0"], ab["P1"], ab["P2"]]
        zero_acc(256)
        for g in range(3):
            sp_, sbk = s_bank()
            for h in range(8):
                ch, hh = h // 2, h % 2
                Q = QA if hh == 0 else QB
                mm(sp_[0:N, 64 * h:64 * h + 64], KTs[:, g, ch, :], Q[:, g, ch, 0:N], True, True,
                   [sb_["KTs"], ab["QA"], ab["QB"]], [sbk], h == 7)
            c.op("act", lambda e, sp_=sp_: e.activation(Pn[0:N, :], sp_[0:N, :], AF.Exp, scale=0.125),
                 reads=[sbk], writes=[sb_["Pn"]])
            c.op("dve", lambda e, g=g: e.tensor_tensor(Pn[0:N, :], Pn[0:N, :], maskSn[0:N, g, :], ALU.mult),
                 reads=[sb_["Pn"], CONST], writes=[sb_["Pn"]])
            for h in range(8):
                ch, hh = h // 2, h % 2
                po = 64 * hh
                mm(nump[po:po + 64, 64 * ch:64 * ch + 64], VsT[:, g, 64 * h:64 * h + 64], Pn[:, 64 * h:64 * h + 64],
                   False, False, [sb_["VsT"], sb_["Pn"]], [bf("num")], False)
                mm(denp[po:po + 64, 64 * ch:64 * ch + 64], ones[:, 0:64], Pn[:, 64 * h:64 * h + 64],
                   False, False, [CONST, sb_["Pn"]], [bf("den")], h == 7)
        ph("S att cache")
        for n in range(16):
            kb_ap, vb_ap = kcb[n % 2], vcb[n % 2]
            kbb, vbb = sb_["kc%d" % (n % 2)], sb_["vc%d" % (n % 2)]
            c.dma("pool", kb_ap.rearrange("p a b c -> p (a b c)"), d_kc[n], writes=[kbb])
            c.dma("pool", vb_ap.rearrange("p a b -> p (a b)"), d_vc[n], writes=[vbb])
            sp_, sbk = s_bank()
            sets = [(0, 0, 4 * n, 4, 0)] + [(1, 1 + i, 4 * n + i, 1, 4 + i) for i in range(4)] + \
                   [(2, 5 + i, 4 * n + i, 1, 8 + i) for i in range(4)]
            for h in range(8):
                ch, hh = h // 2, h % 2
                Q = QA if hh == 0 else QB
                for si, (g, s, q0, nq, k0) in enumerate(sets):
                    mm(sp_[:, 12 * h + k0:12 * h + k0 + nq], kb_ap[:, s, ch, :], Q[:, g, ch, q0:q0 + nq], True, True,
                       [kbb, ab["QA"], ab["QB"]], [sbk], h == 7 and si == 8)
            pi = rr["P"] % 3
            rr["P"] += 1
            P = Pb[:, pi, 0:96]
            c.op("act", lambda e, P=P, sp_=sp_: e.activation(P, sp_[:, 0:96], AF.Exp, scale=0.125),
                 reads=[sbk], writes=[pbufs[pi]])
            c.op("dve", lambda e, P=P: e.tensor_tensor(P, P, maskSc[:, :], ALU.mult),
                 reads=[pbufs[pi], CONST], writes=[pbufs[pi]])
            for h in range(8):
                ch, hh = h // 2, h % 2
                po = 64 * hh
                for si, (g, s, q0, nq, k0) in enumerate(sets):
                    last = (n == 15)
                    mm(nump[po:po + 64, 64 * ch + q0:64 * ch + q0 + nq], vb_ap[:, s, 64 * h:64 * h + 64],
                       P[:, 12 * h + k0:12 * h + k0 + nq], False, last, [vbb, pbufs[pi]], [bf("num")], False)
                    mm(denp[po:po + 64, 64 * ch + q0:64 * ch + q0 + nq], ones[:, 0:64],
                       P[:, 12 * h + k0:12 * h + k0 + nq], False, last, [CONST, pbufs[pi]], [bf("den")],
                       h == 7 and si == 8)
        c.op("dve", lambda e: e.reciprocal(rden[:, 0:256], denp[:, 0:256]), reads=[bf("den")], writes=[ab["rden"]])
        for ch in range(4):
            c.op("dve", lambda e, ch=ch: e.tensor_tensor(hT[:, 4 + ch, 0:N], nump[:, 64 * ch:64 * ch + 64],
                                                         rden[:, 64 * ch:64 * ch + 64], ALU.mult),
                 reads=[bf("num"), ab["rden"]], writes=[bf("hT%d" % (4 + ch))])
        ph("S pool")
        for g in range(4):
            w = 2 << g
            src, srcb = a_s[:, g, :, :], sb_["a_s"]
            off = 0
            bufs = [(p1, sb_["p1"]), (p2, sb_["p2"])]
            for k in range(g + 1):
                sh = 1 << k
                dst, dstb = bufs[k % 2]
                c.op("dve", lambda e, dst=dst, src=src, off=off, sh=sh: e.tensor_tensor(
                    dst[:, :, off + sh:19], src[:, :, off + sh:19], src[:, :, off:19 - sh], ALU.add),
                    reads=[srcb], writes=[dstb])
                src, srcb = dst, dstb
                off += sh
            c.op("dve", lambda e, src=src, g=g, w=w: e.scalar_tensor_tensor(
                plds[:, g, :].rearrange("p (n i) -> p n i", i=4), src[:, :, 15:19], 1.0 / w, a_s[:, g, :, 15:19],
                ALU.mult, ALU.subtract), reads=[srcb, sb_["a_s"]], writes=[sb_["pld"]])
            ps, pb2 = mm_bank()
            mm(ps[:, 0:N], poolw[:, g, :], plds[:, g, :], True, True, [CONST, sb_["pld"]], [pb2], True)
            c.op("act", lambda e, g=g, ps=ps: e.activation(hT[:, g, 0:N], ps[:, 0:N], AF.Identity,
                                                           scale=vecs[:, V_PSC + g:V_PSC + g + 1]),
                 reads=[pb2, CONST], writes=[bf("hT%d" % g)])
        stg = scr[:, 0:2, :].rearrange("p a b -> p (a b)")
        c.op("dve", lambda e: e.tensor_copy(stg[:, 0:960].rearrange("p (a b c) -> p a b c", a=4, b=16), a_s[:, :, :, 4:19]),
             reads=[sb_["a_s"]], writes=[bf("scr0"), bf("scr1")])
        c.dma("sp", o_poolsT, stg[:, 0:960], reads=[bf("scr0"), bf("scr1")], is_output=True)

        def ev_o(key, oc, ps, pb):
            ch = key[1] * 4 + oc
            c.op("dve", lambda e: e.tensor_tensor(rT[:, ch, 0:N], rT[:, ch, 0:N], ps, ALU.add),
                 reads=[pb, bf("rT%d" % ch)], writes=[bf("rT%d" % ch)])
            norm_acc(ch, N)
        ph("S wout")
        dump(1, "h", N)
        linear([("ev_out", j) for j in range(2)], h_rhs(N), 8, N, ev_o, keep=1)
        dump(2, "r", N)
        ffn_ple(0, N, pTs, d_psT[0])

    def _main_loop():
        for ti_, (kind, t) in enumerate(tiles):
            cur["i"] = ti_
            for ch_ in range(8):
                x_load_chunk(ti_, ch_)
            if kind != "sample":
                c.dma("sp", cs[:, :, :], d_cs[:, :, t * TT:(t + 1) * TT], writes=[bf("cs")])
                if kind == "full":
                    c.dma("pool", msk[:, :], d_mask[t - 4], writes=[bf("msk")])
                layer0_prompt(kind, t)
            else:
                c.dma("sp", cs[:, :, 0:NS], d_cs[:, :, NT * TT:NT * TT + NS], writes=[bf("cs")])
                layer0_sample()
            if kind == "kv":
                continue
            if kind == "full" and t == 4:
                layer1("halo", t, 2)
                continue
            if kind == "full":
                layer1("full", t, TT)
                final_out(TT, o_yT, (t - 5) * TT)
            else:
                layer1("sample", t, NS)
                final_out(NS, o_ysT, 0)

    try:
        _main_loop()
    except _Stop:
        pass
    ph("end")
    c.finish()
    c.close()
    return nc, c


_CACHE = {}


def _vec_pc(v):
    return np.ascontiguousarray(v.reshape(-1, 128).T)


def _shared_inputs(inp):
    f = np.float32
    sh = {}
    WB = _wblocks()
    wall = np.zeros((len(WB), 128, 4096), f)
    for i, (key, nm, l, c0, ncols, kcn) in enumerate(WB):
        W = np.asarray(inp[nm][l], dtype=f)[:, c0:c0 + ncols]
        wall[i, :, 0:kcn * ncols] = W.reshape(kcn, 128, ncols).transpose(1, 0, 2).reshape(128, kcn * ncols)
    sh["wall"] = wall
    sh["pool_w"] = np.ascontiguousarray(inp["ev_pool_w"][0].transpose(1, 0, 2), dtype=f)
    ws = np.asarray(inp["od_ws"][0], dtype=f)
    bs = np.asarray(inp["od_bs"][0], dtype=f)
    sh["od_wsT"] = np.ascontiguousarray(ws.transpose(2, 0, 1))
    vec = np.zeros((128, 80), f)
    vec[:, 0:8] = _vec_pc(inp["norm_mix"][0])
    vec[:, 8:16] = _vec_pc(inp["norm_mix"][1])
    vec[:, 16:24] = _vec_pc(inp["norm_ffn"][0])
    vec[:, 24:32] = _vec_pc(inp["norm_ffn"][1])
    vec[:, 32:40] = _vec_pc(inp["norm_ple"][0])
    vec[:, 40:48] = _vec_pc(inp["norm_ple"][1])
    vec[:, 48:56] = _vec_pc(inp["norm_final"])
    vec[:, 56:60] = _vec_pc(inp["ev_pool_scale"][0])
    vec[:, 60:64] = _vec_pc(inp["od_ln_g"][0])
    vec[:, 64:68] = _vec_pc(inp["od_ln_b"][0])
    cw = np.asarray(inp["od_conv_w"][0], dtype=f)
    for ch in range(4):
        for j in range(3):
            vec[:, 68 + 3 * ch + j] = cw[j, ch * 128:(ch + 1) * 128]
    sh["vecs"] = vec
    sh["bsT"] = np.ascontiguousarray(np.broadcast_to(np.tile(bs, (1, 4))[None], (128, 4, 512)), dtype=f)
    sh["bsS"] = np.ascontiguousarray(np.broadcast_to(np.tile(bs[:, 0:4], (1, 16))[None], (128, 4, 64)), dtype=f)
    sh["wsS"] = np.ascontiguousarray(np.broadcast_to(ws[:, 0:4, 0:4].reshape(4, 16)[None], (128, 4, 16)), dtype=f)
    sh["ident"] = np.eye(128, dtype=f)
    rot = np.zeros((128, 128), f)
    for m in range(128):
        k = m + 32 if (m % 64) < 32 else m - 32
        rot[k, m] = 1.0
    sh["rotm"] = rot
    sidx = np.arange(128)
    sh["trimask"] = (sidx[:, None] <= sidx[None, :]).astype(f)
    msc = np.ones((128, 96), f)
    for h in range(8):
        for k in range(4):
            msc[:, 12 * h + k] = (sidx >= k).astype(f)
    sh["maskSc"] = msc
    msn = np.zeros((128, 3, 512), f)
    row = np.arange(64)
    col = np.arange(64)
    same = (row[:, None] // 4) == (col[None, :] // 4)
    m0 = same & ((row[:, None] % 4) <= (col[None, :] % 4))
    m1 = same & ((row[:, None] % 4) == (col[None, :] % 4))
    for h in range(8):
        msn[0:64, 0, 64 * h:64 * h + 64] = m0
        msn[0:64, 1, 64 * h:64 * h + 64] = m1
        msn[0:64, 2, 64 * h:64 * h + 64] = m1
    sh["maskSn"] = msn
    return sh


def _rope_tables(pos):
    f = np.float32
    half = 32
    inv = np.power(f(10000.0), -np.arange(half, dtype=f) / f(half)).astype(f)
    ang = pos.astype(f)[None, :] * inv[:, None]
    cos = np.cos(ang).astype(f)
    sin = np.sin(ang).astype(f)
    p = np.arange(128)
    fi = p % 32
    sign = np.where((p % 64) < 32, -1.0, 1.0).astype(f)
    out = np.empty((128, 2, pos.shape[0]), f)
    out[:, 0, :] = cos[fi]
    out[:, 1, :] = sin[fi] * sign[:, None]
    return out


def _core_masks(s):
    f = np.float32
    k = np.arange(128)[:, None]
    q128 = np.arange(128)[None, :]
    prev_m = (k >= q128).astype(f)
    cur_m = (k <= q128).astype(f)
    out = np.zeros((5, 128, MASKW), f)
    for ti in range(5):
        tau = 4 + ti
        valid = lambda tp: 1.0 if (s - 2560 + 512 * tp) >= 0 else 0.0
        m = out[ti]
        for qb in range(4):
            m[:, (qb * 2) * 128:(qb * 2 + 1) * 128] = prev_m * (valid(tau - 1) if qb == 0 else 1.0)
            m[:, (qb * 2 + 1) * 128:(qb * 2 + 2) * 128] = cur_m
        for r4 in range(4):
            m[:, 1024 + (r4 * 2) * 128:1024 + (r4 * 2 + 1) * 128] = prev_m * valid(tau - 1)
            m[:, 1024 + (r4 * 2 + 1) * 128:1024 + (r4 * 2 + 2) * 128] = cur_m
        i32 = np.arange(32)[None, :]
        blk = np.zeros((128, 32), f)
        for slot in range(4):
            tp = [x for x in range(tau - 4, tau) if x % 4 == slot][0]
            u = np.arange(32)[:, None]
            rel = 32 * (tau - tp) + i32 - u
            blk[32 * slot:32 * slot + 32, :] = ((rel <= 128) & (rel >= 0)).astype(f) * valid(tp)
        for r in range(16):
            m[:, 2048 + 32 * r:2048 + 32 * r + 32] = blk
        kk = np.arange(128)[:, None]
        for r4 in range(4):
            for rho in range(4):
                mb = ((kk % 4) == rho) & ((kk // 4) <= i32)
                c0 = 2560 + (4 * r4 + rho) * 32
                m[:, c0:c0 + 32] = mb.astype(f)
    return out


def _core_icnt(s):
    f = np.float32
    out = np.ones((5, 128, 4, TT), f)
    for ti in range(5):
        pos = s - 2560 + 512 * (4 + ti) + np.arange(TT)
        for g in range(4):
            w = 2 << g
            cnt = np.minimum(w, np.maximum(pos, 0) + 1).astype(f)
            out[ti, :, g, :] = (f(1.0) / cnt)[None, :]
    return out


def _gather_cache(inp, n):
    f = np.float32
    kc = np.empty((128, 9, 4, 128), f)
    vc = np.empty((128, 9, 512), f)
    sets = [(inp["cache_kv_w128"], np.arange(128))]
    for i in range(4):
        sets.append((inp["cache_kv_w512"], i + 4 * np.arange(128)))
    for i in range(4):
        sets.append((inp["cache_kv_w2048"], i + 16 * np.arange(128)))
    for si, (arr, rows) in enumerate(sets):
        blk = np.asarray(arr[0, n, rows], dtype=f)
        kk = blk[:, 0].reshape(128, 512)
        vv = blk[:, 1].reshape(128, 512)
        kc[:, si, :, :] = kk.T.reshape(4, 128, 128).transpose(1, 0, 2)
        vc[:, si, :] = vv
    return kc.reshape(128, 9 * 4 * 128), vc.reshape(128, 9 * 512)


def kernel(**inp):
    f = np.float32
    inp = {k: np.asarray(v) for k, v in inp.items()}
    if "prog" not in _CACHE:
        _CACHE["prog"] = build_program()
    nc, _ctx = _CACHE["prog"]
    sh = _shared_inputs(inp)
    xp = np.asarray(inp["x_prompt"], dtype=f)
    pp = np.asarray(inp["p_prompt"], dtype=f)
    xs = np.asarray(inp["x_sample"], dtype=f)
    ps_ = np.asarray(inp["p_sample"], dtype=f)
    in_maps = []
    for core in (_DBG.get("cores") or range(NCORES)):
        b, s = core // 4, (core % 4) * SEG
        m = dict(sh)
        tok = s - 2560 + np.arange(NT * TT)
        xT = np.zeros((1024, NT * TT), f)
        ok = tok >= 0
        xT[:, ok] = xp[b, tok[ok]].T
        m["xT"] = xT
        tokp = s - 512 + np.arange(5 * TT)
        pT = np.zeros((2, 256, 5 * TT), f)
        okp = tokp >= 0
        pT[:, :, okp] = pp[:, b, tokp[okp]].transpose(0, 2, 1)
        m["pT"] = pT
        pos = np.concatenate([np.maximum(tok, 0), 2048 + (np.arange(NS) % 4)])
        m["cs"] = _rope_tables(pos)
        m["mask"] = _core_masks(s)
        m["icnt"] = _core_icnt(s)
        n0 = 16 * core
        m["xsT"] = np.ascontiguousarray(xs[n0:n0 + 16].reshape(NS, 1024).T)
        m["psT"] = np.ascontiguousarray(ps_[:, n0:n0 + 16].reshape(2, NS, 256).transpose(0, 2, 1))
        kcs, vcs = [], []
        for j in range(16):
            a, b_ = _gather_cache(inp, n0 + j)
            kcs.append(a)
            vcs.append(b_)
        m["kc"] = np.stack(kcs)
        m["vc"] = np.stack(vcs)
        sp = np.asarray(inp["state_pool"][0, n0:n0 + 16], dtype=f)
        m["spool"] = np.ascontiguousarray(sp.reshape(16, 15, 4, 128).transpose(3, 2, 0, 1)).reshape(128, 960)
        sc = np.asarray(inp["state_conv"][0, n0:n0 + 16], dtype=f)
        m["sconv"] = np.ascontiguousarray(sc.reshape(16, 2, 4, 128).transpose(3, 2, 0, 1)).reshape(128, 128)
        in_maps.append(m)
    if _DBG.get("cores"):
        res = run_bass_kernel_spmd(nc, in_maps, core_ids=list(range(len(in_maps))))
        return res.results
    res = run_bass_kernel_spmd(nc, in_maps, core_ids=list(range(NCORES)))
    R = res.results
    y_prompt = np.empty((2, SEQ, 1024), f)
    y_sample = np.empty((128, 4, 1024), f)
    kvp = [np.empty((1, 2, w, 2, 8, 64), f) for w in (128, 512, 2048)]
    kvs = [np.empty((1, 128, 4, 2, 8, 64), f) for _ in range(3)]
    pool_p = np.empty((1, 2, 15, 512), f)
    pool_s = np.empty((1, 128, 15, 512), f)
    conv_p = np.empty((1, 2, 2, 512), f)
    conv_s = np.empty((1, 128, 2, 512), f)
    cv_s = np.empty((1, 128, 4, 512), f)
    for core in range(NCORES):
        r = R[core]
        b, s = core // 4, (core % 4) * SEG
        n0 = 16 * core
        y_prompt[b, s:s + SEG] = r["o_yT"].T
        y_sample[n0:n0 + 16] = r["o_ysT"].T.reshape(16, 4, 1024)
        for g in range(3):
            t = r["o_kvsT"][g]
            kvs[g][0, n0:n0 + 16] = t.transpose(2, 0, 1).reshape(16, 4, 2, 8, 64)
        pool_s[0, n0:n0 + 16] = r["o_poolsT"].reshape(128, 4, 16, 15).transpose(2, 3, 1, 0).reshape(16, 15, 512)
        conv_s[0, n0:n0 + 16] = r["o_convsT"].reshape(128, 4, 16, 2).transpose(2, 3, 1, 0).reshape(16, 2, 512)
        cv_s[0, n0:n0 + 16] = r["o_cvsT"].T.reshape(16, 4, 512)
        if core % 4 == 3:
            for g, w in enumerate((128, 512, 2048)):
                t = r["o_kvT"][g][:, :, SEG - w:]
                kvp[g][0, b] = t.transpose(2, 0, 1).reshape(w, 2, 8, 64)
            pool_p[0, b] = r["o_poolT"].transpose(2, 1, 0).reshape(15, 512)
            conv_p[0, b] = r["o_convT"].transpose(2, 1, 0).reshape(2, 512)
    return (y_prompt, y_sample, kvp[0], kvp[1], kvp[2], kvs[0], kvs[1], kvs[2],
            pool_p, pool_s, conv_p, conv_s, cv_s)
```

```python
import numpy as np
import concourse.bass as bass
import concourse.mybir as mybir
from concourse.bass_utils import run_bass_kernel_spmd

F32 = mybir.dt.float32
BF16 = mybir.dt.bfloat16
ALU = mybir.AluOpType
AF = mybir.ActivationFunctionType

NCORES = 8
SEQ = 8192
SEG = 2048
TT = 512
NT = 9
NS = 64
EPS = 1e-6
MASKW = 3072


class Buf:
    __slots__ = ("name", "w", "r")

    def __init__(self, name=""):
        self.name = name
        self.w = None
        self.r = {}


class Eng:
    def __init__(self, key, sem):
        self.key = key
        self.sem = sem
        self.count = 0
        self.waited = {}
        self.prog = []


class Ctx:
    COMPUTE = ("pe", "act", "dve", "pool")

    def __init__(self, nc, n_dma_sems=8):
        self.nc = nc
        self.sems = {}
        self.engs = {}
        self._stack = []
        for k in ("pe", "act", "dve", "pool", "sp"):
            self._sem("e_" + k)
            self.engs[k] = Eng(k, "e_" + k)
        self.dma_pool = {}
        self.dma_rr = {}
        self.dma_val = {}
        for q in ("sp", "act", "pool"):
            names = []
            for i in range(n_dma_sems):
                nm = "d_%s_%d" % (q, i)
                self._sem(nm)
                names.append(nm)
                self.dma_val[nm] = 0
            self.dma_pool[q] = names
            self.dma_rr[q] = 0
        self.out_events = []
        self.n_wait = 0
        self.n_op = 0

    def _sem(self, name):
        cm = self.nc.semaphore(name)
        h = cm.__enter__()
        self._stack.append(cm)
        self.sems[name] = h
        return h

    def sb(self, name, shape, dtype):
        cm = self.nc.sbuf_tensor("sb_" + name, shape, dtype)
        t = cm.__enter__()
        self._stack.append(cm)
        return t

    def ps(self, name, shape, dtype):
        cm = self.nc.psum_tensor("ps_" + name, shape, dtype)
        t = cm.__enter__()
        self._stack.append(cm)
        return t

    def _deps(self, reads, writes):
        deps = {}
        for b in reads:
            if b.w is not None:
                k, v = b.w
                if deps.get(k, 0) < v:
                    deps[k] = v
        for b in writes:
            if b.w is not None:
                k, v = b.w
                if deps.get(k, 0) < v:
                    deps[k] = v
            for k, v in b.r.items():
                if deps.get(k, 0) < v:
                    deps[k] = v
        return deps

    def _emit_waits(self, e, deps, skip_own=False):
        for k, v in deps.items():
            if skip_own and k == e.sem:
                continue
            if e.waited.get(k, 0) >= v:
                continue
            e.waited[k] = v
            e.prog.append(("wait", self.sems[k], v))
            self.n_wait += 1

    def _record(self, ev, reads, writes):
        k, v = ev
        for b in reads:
            if b.r.get(k, 0) < v:
                b.r[k] = v
        for b in writes:
            b.w = ev
            b.r = {}

    def op(self, eng, fn, reads=(), writes=(), inc=True):
        e = self.engs[eng]
        deps = self._deps(reads, writes)
        self._emit_waits(e, deps, skip_own=(eng == "pe"))
        if inc:
            e.count += 1
            ev = (e.sem, e.count)
            e.prog.append(("op", fn, self.sems[e.sem], 1))
        else:
            ev = (e.sem, e.count + 1)
            e.prog.append(("op", fn, None, 0))
        self._record(ev, reads, writes)
        self.n_op += 1
        return ev

    def dma(self, q, out, in_, reads=(), writes=(), is_output=False):
        e = self.engs[q]
        deps = self._deps(reads, writes)
        names = self.dma_pool[q]
        nm = names[self.dma_rr[q] % len(names)]
        self.dma_rr[q] += 1
        prev = self.dma_val[nm]
        if prev > 0 and deps.get(nm, 0) < prev:
            deps[nm] = prev
        self._emit_waits(e, deps)
        val = prev + 16
        self.dma_val[nm] = val
        ev = (nm, val)

        def fn(engine, out=out, in_=in_):
            return engine.dma_start(out=out, in_=in_)
        e.prog.append(("op", fn, self.sems[nm], 16))
        self._record(ev, reads, writes)
        if is_output:
            self.out_events.append(ev)
        self.n_op += 1
        return ev

    def handoff(self, old, new):
        merged = {}
        for b in old:
            if b.w is not None:
                k, v = b.w
                if merged.get(k, 0) < v:
                    merged[k] = v
            for k, v in b.r.items():
                if merged.get(k, 0) < v:
                    merged[k] = v
        for b in new:
            b.w = None
            b.r = dict(merged)

    def finish(self):
        e = self.engs["sp"]
        deps = {}
        for k, v in self.out_events:
            deps[k] = max(deps.get(k, 0), v)
        for kk in self.COMPUTE:
            ee = self.engs[kk]
            if ee.count > 0:
                deps[ee.sem] = ee.count
        for nm, v in self.dma_val.items():
            if v > 0:
                deps[nm] = max(deps.get(nm, 0), v)
        self._emit_waits(e, deps)
        nc = self.nc
        engs = self.engs

        def replay(engine, prog):
            for item in prog:
                if item[0] == "wait":
                    engine.wait_ge(item[1], item[2])
                else:
                    _, fn, sem, n = item
                    ins = fn(engine)
                    if sem is not None:
                        ins.then_inc(sem, n)

        with nc.Block() as block:
            @block.sync
            def _(eng):
                replay(eng, engs["sp"].prog)

            @block.tensor
            def _(eng):
                replay(eng, engs["pe"].prog)

            @block.scalar
            def _(eng):
                replay(eng, engs["act"].prog)

            @block.vector
            def _(eng):
                replay(eng, engs["dve"].prog)

            @block.gpsimd
            def _(eng):
                replay(eng, engs["pool"].prog)

    def close(self):
        while self._stack:
            cm = self._stack.pop()
            cm.__exit__(None, None, None)


def _wblocks():
    L = [(("ev_in", "a"), "ev_w_in", 0, 0, 512, 8)]
    for g in range(3):
        base = 512 + g * 1536
        for i, nm in enumerate(("q", "k", "v")):
            L.append((("ev_in", g, nm), "ev_w_in", 0, base + 512 * i, 512, 8))
    for j in range(2):
        L.append((("ev_out", j), "ev_w_out", 0, 512 * j, 512, 8))
    names = ["u", "v", "go", "gi", "xi"]
    for j in range(5):
        L.append((("od_in", names[j]), "od_w_in", 0, 512 * j, 512, 8))
    for j in range(2):
        L.append((("od_out", j), "od_w_out", 0, 512 * j, 512, 8))
    for l in range(2):
        for j in range(8):
            L.append((("w1", l, j), "ffn_w1", l, 512 * j, 512, 8))
        for j in range(8):
            L.append((("w2", l, j), "ffn_w2", l, 128 * j, 128, 32))
        L.append((("proj", l), "ple_w_proj", l, 0, 1024, 2))
        for j in range(2):
            L.append((("gate", l, j), "ple_w_gate", l, 512 * j, 512, 8))
    return L


_DBG = {}


class _Stop(Exception):
    pass


PHASES = []


def ck(name):
    if _DBG.get("stop") == name:
        raise _Stop()


def build_program():
    nc = bass.Bass("TRN2", target_bir_lowering=False)
    c = Ctx(nc)

    def din(name, shape):
        return nc.dram_tensor(name, list(shape), F32, kind="ExternalInput").ap()

    def dout(name, shape):
        return nc.dram_tensor(name, list(shape), F32, kind="ExternalOutput").ap()

    WB = _wblocks()
    WIDX = {k[0]: i for i, k in enumerate(WB)}
    d_wall = din("wall", [len(WB), 128, 4096])
    d_pool_w = din("pool_w", [128, 4, 128])
    d_wsT = din("od_wsT", [128, 4, 128])
    d_vecs = din("vecs", [128, 80])
    d_bsT = din("bsT", [128, 4, 512])
    d_bsS = din("bsS", [128, 4, 64])
    d_wsS = din("wsS", [128, 4, 16])
    d_ident = din("ident", [128, 128])
    d_rotm = din("rotm", [128, 128])
    d_tri = din("trimask", [128, 128])
    d_maskSc = din("maskSc", [128, 96])
    d_maskSn = din("maskSn", [128, 3, 512])
    d_xT = din("xT", [1024, NT * TT])
    d_pT = din("pT", [2, 256, 5 * TT])
    d_cs = din("cs", [128, 2, NT * TT + NS])
    d_mask = din("mask", [5, 128, MASKW])
    d_icnt = din("icnt", [5, 128, 4, TT])
    d_xsT = din("xsT", [1024, NS])
    d_psT = din("psT", [2, 256, NS])
    d_kc = din("kc", [16, 128, 9 * 4 * 128])
    d_vc = din("vc", [16, 128, 9 * 512])
    d_spool = din("spool", [128, 960])
    d_sconv = din("sconv", [128, 128])
    o_yT = dout("o_yT", [1024, SEG])
    o_ysT = dout("o_ysT", [1024, NS])
    o_kvT = dout("o_kvT", [3, 2, 512, SEG])
    o_kvsT = dout("o_kvsT", [3, 2, 512, NS])
    o_poolT = dout("o_poolT", [128, 4, 15])
    o_poolsT = dout("o_poolsT", [128, 960])
    o_convT = dout("o_convT", [128, 4, 2])
    o_convsT = dout("o_convsT", [128, 128])
    o_cvsT = dout("o_cvsT", [512, NS])
    o_dbg = dout("o_dbg", [12, 1024, NS]) if _DBG.get("dump") else None

    rT = c.sb("rT", [128, 8, TT], F32)
    hT = c.sb("hT", [128, 8, TT], BF16)
    sq = c.sb("sq", [128, 2, TT], BF16)
    zb = c.sb("zb", [128, 2, TT], BF16)
    scr = c.sb("scr", [128, 3, TT], F32)
    cs = c.sb("cs_s", [128, 2, TT], F32)
    a_ext = c.sb("a_ext", [128, 4, 528], F32)
    pTs = c.sb("pTs_s", [128, 2, TT], BF16)
    msk = c.sb("msk_s", [128, MASKW], BF16)
    NW = 4
    wsl = c.sb("wsl", [128, NW, 4096], BF16)
    ident = c.sb("ident", [128, 128], BF16)
    rotm = c.sb("rotm_s", [128, 128], BF16)
    ones = c.sb("ones", [128, 128], BF16)
    onesn = c.sb("onesn", [128, 128], BF16)
    ones128 = c.sb("ones128", [128, 128], BF16)
    zeros = c.sb("zeros", [128, 128], BF16)
    vecs = c.sb("vecs_s", [128, 80], F32)
    bsT = c.sb("bsT_s", [128, 4, TT], BF16)
    bsS = c.sb("bsS_s", [128, 4, NS], F32)
    wsS = c.sb("wsS_s", [128, 4, 16], F32)
    wmT = c.sb("wmT", [128, 4, 128], BF16)
    wsf = c.sb("wsf", [128, 4, 128], F32)
    trim = c.sb("trim", [128, 128], F32)
    poolw = c.sb("poolw", [128, 4, 128], BF16)
    hd_halo = c.sb("hd_halo", [128, 4, 2], F32)
    maskSc = c.sb("maskSc_s", [128, 96], BF16)
    maskSn = c.sb("maskSn_s", [128, 3, 512], BF16)
    arena = c.sb("arena", [128, 16896], BF16)
    kvr = c.sb("kvr", [128, 36864], BF16)

    mmps = [c.ps("mm0", [128, 512], F32), c.ps("mm1", [128, 512], F32)]
    sps = [c.ps("s0", [128, 512], F32), c.ps("s1", [128, 512], F32)]
    nump = c.ps("nump", [128, 512], F32)
    denp = c.ps("denp", [128, 512], F32)
    auxp = c.ps("auxp", [128, 512], F32)
    trp = c.ps("trp", [128, 1024], BF16)

    del PHASES[:]

    def ph(name):
        PHASES.append((name, sum(1 for it in c.engs["pe"].prog if it[0] == "op")))

    B = {}

    def bf(name):
        if name not in B:
            B[name] = Buf(name)
        return B[name]

    def aview(off, shape, dtype=BF16):
        n = 1
        for s in shape[1:]:
            n *= s
        if dtype == F32:
            ap = arena[:, off:off + 2 * n].bitcast(F32)
        else:
            ap = arena[:, off:off + n]
        if len(shape) == 3:
            ap = ap.rearrange("p (a b) -> p a b", a=shape[1])
        elif len(shape) == 4:
            ap = ap.rearrange("p (a b c) -> p a b c", a=shape[1], b=shape[2])
        return ap

    def kview(off, shape):
        n = 1
        for s in shape[1:]:
            n *= s
        ap = kvr[:, off:off + n]
        if len(shape) == 3:
            ap = ap.rearrange("p (a b) -> p a b", a=shape[1])
        elif len(shape) == 4:
            ap = ap.rearrange("p (a b c) -> p a b c", a=shape[1], b=shape[2])
        return ap

    KT0 = kview(0, [128, 4, 2 * TT])
    KT1 = kview(4096, [128, 4, 2 * TT])
    KT2 = kview(8192, [128, 4, 5 * TT])
    VK0 = kview(18432, [128, 2, 4, 512])
    VK1 = kview(22528, [128, 2, 4, 512])
    V2A = kview(26624, [128, 16, 512])
    V2B = kview(34816, [128, 4, 512])

    wseq = []

    def wblock(key):
        i = WIDX[key]
        _, _, _, _, ncols, kcn = WB[i]
        wseq.append((key, i, kcn, ncols))

    def sched_l0(kind, t):
        if kind == "kv":
            if t == 3:
                wblock(("ev_in", "a"))
                gs = (0, 1, 2)
            else:
                gs = (2,)
            for g in gs:
                wblock(("ev_in", g, "k"))
                wblock(("ev_in", g, "v"))
            return
        wblock(("ev_in", "a"))
        for g in range(3):
            for nm in ("q", "k", "v"):
                wblock(("ev_in", g, nm))
        for j in range(2):
            wblock(("ev_out", j))
        sched_ffn_ple(0)

    def sched_ffn_ple(l):
        for j in range(8):
            wblock(("w1", l, j))
        for j in range(8):
            wblock(("w2", l, j))
        wblock(("proj", l))
        for j in range(2):
            wblock(("gate", l, j))

    def sched_l1(kind):
        names = ["u", "v", "go", "gi", "xi"]
        if kind == "halo":
            for j in (3, 4):
                wblock(("od_in", names[j]))
            return
        for j in range(5):
            wblock(("od_in", names[j]))
        for j in range(2):
            wblock(("od_out", j))
        sched_ffn_ple(1)

    tiles = [("kv", t) for t in range(4)] + [("full", t) for t in range(4, 9)] + [("sample", 9)]
    if _DBG.get("tiles") is not None:
        tiles = [tiles[i] for i in _DBG["tiles"]]
    for kind, t in tiles:
        if kind == "kv":
            sched_l0("kv", t)
        else:
            sched_l0("full", t)
            if kind == "full" and t == 4:
                sched_l1("halo")
            else:
                sched_l1("full")

    wstate = {"issued": 0, "next": 0}

    def w_issue_upto(n):
        while wstate["issued"] < min(n, len(wseq)):
            i = wstate["issued"]
            key, bi, kcn, ncols = wseq[i]
            slot = i % NW
            c.dma("pool", wsl[:, slot, 0:kcn * ncols], d_wall[bi, :, 0:kcn * ncols], writes=[bf("w%d" % slot)])
            wstate["issued"] += 1

    def getw(key, oldest=None):
        i = wstate["next"]
        assert wseq[i][0] == key, (wseq[i][0], key)
        _, bi, kcn, ncols = wseq[i]
        w_issue_upto((i if oldest is None else oldest) + NW - 1)
        slot = i % NW
        wstate["next"] += 1
        view = wsl[:, slot, 0:kcn * ncols].rearrange("p (k n) -> p k n", k=kcn)
        return view, bf("w%d" % slot)

    def w_advance():
        w_issue_upto(wstate["next"] + NW)

    rr = {"mm": 0, "s": 0, "sq": 0, "zb": 0, "scr": 0, "P": 0, "ev": 0, "trp": 0}

    WIDE = [(mmps[0], "mm0"), (mmps[1], "mm1"), (sps[0], "s0"), (sps[1], "s1"), (nump, "num"), (denp, "den")]
    SB4 = [(sps[0], "s0"), (sps[1], "s1"), (mmps[0], "mm0"), (mmps[1], "mm1")]

    def mm_bank():
        i = rr["mm"] % len(WIDE)
        rr["mm"] += 1
        return WIDE[i][0], bf(WIDE[i][1])

    def s_bank():
        i = rr["s"] % len(SB4)
        rr["s"] += 1
        return SB4[i][0], bf(SB4[i][1])

    def scr_buf():
        i = rr["scr"] % 3
        rr["scr"] += 1
        return scr[:, i, :], bf("scr%d" % i)

    def zb_buf():
        i = rr["zb"] % 2
        rr["zb"] += 1
        return zb[:, i, :], bf("zb%d" % i)

    def sq_buf():
        i = rr["sq"] % 2
        rr["sq"] += 1
        return sq[:, i, :], bf("sq%d" % i)

    def trp_half():
        i = rr["trp"] % 2
        rr["trp"] += 1
        return trp[:, 512 * i:512 * i + 512], bf("trp")

    def ev_eng():
        rr["ev"] += 1
        if _DBG.get("evac"):
            return _DBG["evac"]
        return "act" if rr["ev"] % 2 else "dve"

    def copy_op(eng, out, in_, reads, writes):
        if eng == "act":
            c.op("act", lambda e: e.copy(out, in_), reads=reads, writes=writes)
        else:
            c.op(eng, lambda e: e.tensor_copy(out, in_), reads=reads, writes=writes)

    def mm(ps_ap, lhsT, rhs, start, stop, reads, writes, inc):
        c.op("pe", lambda e: e.matmul(ps_ap, lhsT, rhs, start=start, stop=stop),
             reads=reads, writes=writes, inc=inc)

    c.dma("pool", ident[:], d_ident, writes=[bf("const")])
    c.dma("pool", rotm[:], d_rotm, writes=[bf("const")])
    c.dma("pool", poolw[:], d_pool_w, writes=[bf("const")])
    c.dma("pool", bsT[:], d_bsT, writes=[bf("const")])
    c.dma("pool", maskSc[:], d_maskSc, writes=[bf("const")])
    c.dma("pool", maskSn[:], d_maskSn, writes=[bf("const")])
    c.dma("sp", vecs[:], d_vecs, writes=[bf("const")])
    c.dma("sp", bsS[:], d_bsS, writes=[bf("const")])
    c.dma("sp", wsS[:], d_wsS, writes=[bf("const")])
    c.dma("sp", wsf[:], d_wsT, writes=[bf("wsf")])
    c.dma("sp", trim[:], d_tri, writes=[bf("trim")])
    c.op("dve", lambda e: e.memset(ones[:], 1.0), writes=[bf("const")])
    c.op("dve", lambda e: e.memset(onesn[:], 1.0 / 1024.0), writes=[bf("const")])
    c.op("dve", lambda e: e.memset(ones128[:], 1.0 / 128.0), writes=[bf("const")])
    c.op("dve", lambda e: e.memset(zeros[:], 0.0), writes=[bf("const")])
    for g in range(4):
        c.op("dve", lambda e, g=g: e.tensor_tensor(wmT[:, g, :], wsf[:, g, :], trim[:], ALU.mult),
             reads=[bf("wsf"), bf("trim")], writes=[bf("const")])
    c.op("dve", lambda e: e.memset(hd_halo[:], 0.0), writes=[bf("hd_halo")])
    c.op("dve", lambda e: e.memset(a_ext[:], 0.0), writes=[bf("a_ext")])
    CONST = bf("const")
    V_NMIX, V_NFFN, V_NPLE, V_NFIN = (0, 8), (16, 24), (32, 40), 48
    V_PSC, V_LNG, V_LNB, V_CW = 56, 60, 64, 68

    w_issue_upto(NW)

    deferred = []

    def flush_deferred(keep=0, everything=False):
        if everything:
            while deferred:
                deferred.pop(0)()
            return
        for _ in range(max(0, len(deferred) - keep)):
            deferred.pop(0)()

    def dump(i, which, N):
        if o_dbg is None or N != NS:
            return
        dst = o_dbg[i].rearrange("(ch p) n -> p ch n", p=128)
        if which == "h":
            c.dma("pool", dst, hT[:, :, 0:N], reads=[bf("hT%d" % ch) for ch in range(8)], is_output=True)
        else:
            c.dma("sp", dst, rT[:, :, 0:N], reads=[bf("rT%d" % ch) for ch in range(8)], is_output=True)

    def norm_acc(ch, N, defer=True):
        s_ap, s_b = sq_buf()
        c.op("act", lambda e, ch=ch, s_ap=s_ap: e.activation(s_ap[:, 0:N], rT[:, ch, 0:N], AF.Square),
             reads=[bf("rT%d" % ch)], writes=[s_b])
        def later():
            mm(auxp[:, 0:N], onesn[:], s_ap[:, 0:N], ch == 0, ch == 7, [s_b, CONST], [bf("aux")], True)
        if defer:
            deferred.append(later)
        else:
            later()

    def rmsnorm(N, vcol, out_fn=None, pre=False):
        ss = auxp[:, 0:N]
        if not pre:
            for ch in range(8):
                norm_acc(ch, N, defer=False)
        flush_deferred(everything=True)
        r_ap, r_b = scr_buf()
        c.op("act", lambda e: e.activation(r_ap[:, 0:N], ss, AF.Sqrt, bias=EPS, scale=1.0),
             reads=[bf("aux")], writes=[r_b])
        c.op("dve", lambda e: e.reciprocal(r_ap[:, 0:N], r_ap[:, 0:N]), reads=[r_b], writes=[r_b])
        for ch in range(8):
            if out_fn is None:
                if ch % 2 == 0:
                    c.op("dve", lambda e, ch=ch: e.scalar_tensor_tensor(
                        hT[:, ch, 0:N], rT[:, ch, 0:N], vecs[:, vcol + ch:vcol + ch + 1], r_ap[:, 0:N],
                        ALU.mult, ALU.mult), reads=[bf("rT%d" % ch), r_b, CONST], writes=[bf("hT%d" % ch)])
                else:
                    tz, tzb = zb_buf()
                    c.op("pool", lambda e, ch=ch, tz=tz: e.tensor_tensor(tz[:, 0:N], rT[:, ch, 0:N], r_ap[:, 0:N], ALU.mult),
                         reads=[bf("rT%d" % ch), r_b], writes=[tzb])
                    c.op("act", lambda e, ch=ch, tz=tz: e.activation(hT[:, ch, 0:N], tz[:, 0:N], AF.Identity,
                                                                     scale=vecs[:, vcol + ch:vcol + ch + 1]),
                         reads=[tzb, CONST], writes=[bf("hT%d" % ch)])
            else:
                out_fn(ch, r_ap, r_b)

    def linear(keys, rhs_fn, nk, N, evac, keep=0):
        for key in keys:
            wv, wb = getw(key)
            ncols = wv.shape[2]
            for oc in range(ncols // 128):
                ps, pb = mm_bank()
                for kc in range(nk):
                    rhs, rb = rhs_fn(kc)
                    mm(ps[:, 0:N], wv[:, kc, 128 * oc:128 * oc + 128], rhs, kc == 0, kc == nk - 1,
                       [wb, rb], [pb], kc == nk - 1)
                flush_deferred(keep)
                evac(key, oc, ps[:, 0:N], pb)
            w_advance()

    def h_rhs(N):
        return lambda kc: (hT[:, kc, 0:N], bf("hT%d" % kc))

    def zero_acc(ncol):
        mm(nump[:, 0:ncol], zeros[:], bsT[:, 0, 0:ncol], True, False, [CONST], [bf("num")], False)
        mm(denp[:, 0:ncol], zeros[:], bsT[:, 0, 0:ncol], True, False, [CONST], [bf("den")], True)

    def rope(ps, pb, N, out_full=None, out_halves=None, out_bufs=()):
        z_ap, z_b = zb_buf()
        copy_op("act", z_ap[:, 0:N], ps, [pb], [z_b])

        def stage2():
            rp, rpb = mm_bank()
            mm(rp[:, 0:N], rotm[:], z_ap[:, 0:N], True, True, [z_b, CONST], [rpb], True)
            t1, t1b = scr_buf()
            t2, t2b = scr_buf()
            c.op("dve", lambda e: e.tensor_tensor(t1[:, 0:N], z_ap[:, 0:N], cs[:, 0, 0:N], ALU.mult),
                 reads=[z_b, bf("cs")], writes=[t1b])
            c.op("dve", lambda e: e.tensor_tensor(t2[:, 0:N], rp[:, 0:N], cs[:, 1, 0:N], ALU.mult),
                 reads=[rpb, bf("cs")], writes=[t2b])
            if out_full is not None:
                c.op("pool", lambda e: e.tensor_tensor(out_full, t1[:, 0:N], t2[:, 0:N], ALU.add),
                     reads=[t1b, t2b], writes=list(out_bufs))
            else:
                oa, ob = out_halves
                c.op("pool", lambda e: e.tensor_tensor(oa[0:64, :], t1[0:64, 0:N], t2[0:64, 0:N], ALU.add),
                     reads=[t1b, t2b], writes=[out_bufs[0]])
                c.op("pool", lambda e: e.tensor_tensor(ob[64:128, :], t1[64:128, 0:N], t2[64:128, 0:N], ALU.add),
                     reads=[t1b, t2b], writes=[out_bufs[1]])
        deferred.append(stage2)

    def ffn_ple(l, N, p_ap, p_src):
        c.dma("pool", p_ap[:, :, 0:N], p_src.rearrange("(k p) n -> p k n", p=128), writes=[bf("pTs")])
        hid = aview(0, [128, 32, TT])
        hb = [bf("hid%d" % f) for f in range(32)]
        c.handoff(ARENA_BUFS[0], hb)
        ARENA_BUFS[0] = hb
        ph("ffn%d" % l)
        rmsnorm(N, V_NFFN[l], pre=True)

        def ev1(key, oc, ps, pb):
            f = key[2] * 4 + oc
            r_ap, r_b = zb_buf()
            if f % 2 == 0:
                c.op("act", lambda e: e.activation(r_ap[:, 0:N], ps, AF.Relu), reads=[pb], writes=[r_b])
            else:
                c.op("dve", lambda e: e.tensor_scalar(r_ap[:, 0:N], ps, 0.0, None, ALU.max), reads=[pb], writes=[r_b])
            c.op("pool", lambda e: e.tensor_tensor(hid[:, f, 0:N], r_ap[:, 0:N], r_ap[:, 0:N], ALU.mult),
                 reads=[r_b], writes=[hb[f]])
        linear([("w1", l, j) for j in range(8)], h_rhs(N), 8, N, ev1)

        def ev2(key, oc, ps, pb):
            ch = key[2]
            c.op("dve", lambda e: e.tensor_tensor(rT[:, ch, 0:N], rT[:, ch, 0:N], ps, ALU.add),
                 reads=[pb, bf("rT%d" % ch)], writes=[bf("rT%d" % ch)])
            norm_acc(ch, N)
        linear([("w2", l, j) for j in range(8)], lambda kc: (hid[:, kc, 0:N], hb[kc]), 32, N, ev2)
        dump(3 + 4 * l, "r", N)
        ph("ple%d" % l)
        rmsnorm(N, V_NPLE[l], pre=True)
        ip = wstate["next"]
        wp, wpb = getw(("proj", l))
        pend_sq = [None]
        for j in range(2):
            wg, wgb = getw(("gate", l, j), oldest=ip)
            for oc in range(4):
                ch = 4 * j + oc
                psg, pgb = mm_bank()
                for kc in range(8):
                    mm(psg[:, 0:N], wg[:, kc, 128 * oc:128 * oc + 128], hT[:, kc, 0:N], kc == 0, kc == 7,
                       [wgb, bf("hT%d" % kc)], [pgb], kc == 7)
                psp, ppb = mm_bank()
                for kc in range(2):
                    mm(psp[:, 0:N], wp[:, kc, 128 * ch:128 * ch + 128], p_ap[:, kc, 0:N], kc == 0, kc == 1,
                       [wpb, bf("pTs")], [ppb], kc == 1)
                flush_deferred()
                g_ap, g_b = scr_buf()
                c.op("act", lambda e, g_ap=g_ap, psg=psg: e.activation(g_ap[:, 0:N], psg[:, 0:N], AF.Sigmoid),
                     reads=[pgb], writes=[g_b])
                if pend_sq[0] is not None:
                    norm_acc(pend_sq[0], N)
                t_ap, t_b = scr_buf()
                c.op("dve", lambda e, g_ap=g_ap, t_ap=t_ap, psp=psp: e.tensor_tensor(
                    t_ap[:, 0:N], g_ap[:, 0:N], psp[:, 0:N], ALU.mult), reads=[g_b, ppb], writes=[t_b])
                c.op("pool", lambda e, t_ap=t_ap, ch=ch: e.tensor_tensor(
                    rT[:, ch, 0:N], rT[:, ch, 0:N], t_ap[:, 0:N], ALU.add),
                    reads=[t_b, bf("rT%d" % ch)], writes=[bf("rT%d" % ch)])
                pend_sq[0] = ch
        norm_acc(pend_sq[0], N)
        w_advance()
        dump(4 + 4 * l, "r", N)

    ARENA_BUFS = [[bf("arena_init")]]
    RT = [bf("rT%d" % ch) for ch in range(8)]
    cur = {"i": 0}
    xloaded = set()

    def x_load_chunk(i, ch):
        if (i, ch) in xloaded:
            return
        xloaded.add((i, ch))
        kind_, t_ = tiles[i]
        if kind_ == "sample":
            c.dma("sp", rT[:, ch, 0:NS], d_xsT[128 * ch:128 * ch + 128, :], writes=[RT[ch]])
        else:
            c.dma("sp", rT[:, ch, :], d_xT[128 * ch:128 * ch + 128, t_ * TT:(t_ + 1) * TT], writes=[RT[ch]])

    def x_prefetch_next(ch=None):
        i = cur["i"] + 1
        if i >= len(tiles):
            return
        for k in ([ch] if ch is not None else range(8)):
            x_load_chunk(i, k)
    KV_BUFS = [bf("kv_KT0"), bf("kv_KT1"), bf("kv_KT2r"), bf("kv_KT2c"), bf("kv_VK0"), bf("kv_VK1"),
               bf("kv_V2A"), bf("kv_V2B")]

    A_QA, A_QB = 0, 6144
    A_VT = 12288
    A_P = 14336
    A_RD = 15872

    def layer0_prompt(kind, t):
        N = TT
        full = kind == "full"
        main = full and t >= 5
        cur, prev = t % 2, (t + 1) % 2
        tok0 = (t - 5) * TT
        QA = aview(A_QA, [128, 3, 4, TT])
        QB = aview(A_QB, [128, 3, 4, TT])
        VT = aview(A_VT, [128, 4, TT])
        Pb = aview(A_P, [128, 3, TT])
        rden = aview(A_RD, [128, TT], F32)
        ab = {n: bf("ar_" + n) for n in ["QA", "QB", "VT", "P0", "P1", "P2", "rden"]}
        c.handoff(ARENA_BUFS[0], list(ab.values()))
        ARENA_BUFS[0] = list(ab.values())
        if full:
            c.op("dve", lambda e: e.memset(arena[64:128, 0:6144], 0.0), writes=[ab["QA"]])
            c.op("dve", lambda e: e.memset(arena[0:64, 6144:12288], 0.0), writes=[ab["QB"]])
        ph("L0norm %s%d" % (kind, t))
        ck("pre")
        rmsnorm(N, V_NMIX[0])
        ck("norm")
        if not full:
            x_prefetch_next()
        ph("L0proj")
        gs = (0, 1, 2) if (full or t == 3) else (2,)
        KTs = [KT0, KT1, KT2]
        kbufs = [bf("kv_KT0"), bf("kv_KT1"), bf("kv_KT2c")]

        def kslot(g, ch):
            if g == 2:
                return KT2[:, ch, 4 * TT:5 * TT]
            return KTs[g][:, ch, cur * TT:(cur + 1) * TT]

        if full or t == 3:
            def ev_a(key, oc, ps, pb):
                copy_op("act", a_ext[:, oc, 15:15 + N], ps, [pb], [bf("a_ext")])
            linear([("ev_in", "a")], h_rhs(N), 8, N, ev_a)
        for g in gs:
            if full:
                def ev_q(key, oc, ps, pb, g=g):
                    rope(ps, pb, N, out_halves=(QA[:, g, oc, :], QB[:, g, oc, :]), out_bufs=(ab["QA"], ab["QB"]))
                linear([("ev_in", g, "q")], h_rhs(N), 8, N, ev_q)

            def ev_k(key, oc, ps, pb, g=g):
                rope(ps, pb, N, out_full=kslot(g, oc), out_bufs=(kbufs[g],))
            linear([("ev_in", g, "k")], h_rhs(N), 8, N, ev_k)
            ck("k")

            def ev_v(key, oc, ps, pb):
                copy_op(ev_eng(), VT[:, oc, :], ps, [pb], [ab["VT"]])
            linear([("ev_in", g, "v")], h_rhs(N), 8, N, ev_v)
            flush_deferred(everything=True)
            ck("v")
            if main:
                ko = o_kvT[g, 0, :, tok0:tok0 + TT].rearrange("(ch p) n -> p ch n", p=128)
                ksrc = KT2[:, :, 4 * TT:5 * TT] if g == 2 else KTs[g][:, :, cur * TT:(cur + 1) * TT]
                c.dma("pool", ko, ksrc, reads=[kbufs[g]], is_output=True)
                vo = o_kvT[g, 1, :, tok0:tok0 + TT].rearrange("(ch p) n -> p ch n", p=128)
                c.dma("pool", vo, VT[:, :, :], reads=[ab["VT"]], is_output=True)
            for b in range(4):
                th, thb = trp_half()
                for ch in range(4):
                    src = VT[:, ch, 128 * b:128 * b + 128] if g == 0 else VT[:, ch, b:TT:4]
                    c.op("pe", lambda e, th=th, ch=ch, src=src: e.transpose(th[:, 128 * ch:128 * ch + 128], src, ident[:]),
                         reads=[ab["VT"], CONST], writes=[thb], inc=(ch == 3))
                ck("trb%d" % b)
                if g == 0:
                    copy_op(ev_eng(), VK0[:, cur, b, :], th, [thb], [bf("kv_VK0")])
                elif g == 1:
                    copy_op(ev_eng(), VK1[:, cur, b, :], th, [thb], [bf("kv_VK1")])
                else:
                    copy_op(ev_eng(), V2B[:, b, :], th, [thb], [bf("kv_V2B")])
                ck("evb%d" % b)
        ph("L0att")
        ck("tr")
        if full:
            attention_prompt(t, QA, QB, Pb, rden, ab)
        ph("L0ring")
        ck("att")
        s = t % 4
        c.op("act", lambda e: e.copy(KT2[:, :, s * TT:(s + 1) * TT], KT2[:, :, 4 * TT:5 * TT]),
             reads=[bf("kv_KT2c")], writes=[bf("kv_KT2r")])
        for j in range(4):
            c.dma("sp", V2A[32 * s:32 * s + 32, 4 * j:4 * j + 4, :], V2B[j:128:4, :, :],
                  reads=[bf("kv_V2B")], writes=[bf("kv_V2A")])
        ck("ring")
        if not full:
            if t == 3:
                c.op("dve", lambda e: e.tensor_copy(a_ext[:, :, 0:15], a_ext[:, :, TT:TT + 15]),
                     reads=[bf("a_ext")], writes=[bf("a_ext")])
            return
        ph("L0pool")
        pool_mixer_prompt(t)
        ph("L0wout")
        if t == 4:
            HB = [bf("hT%d" % ch) for ch in range(8)]
            c.op("dve", lambda e: e.tensor_copy(hT[:, :, 0:2], hT[:, :, TT - 2:TT]), reads=HB, writes=HB)
            c.op("dve", lambda e: e.tensor_copy(rT[:, :, 0:2], rT[:, :, TT - 2:TT]), reads=RT, writes=RT)
            N = 2
        def ev_o(key, oc, ps, pb):
            ch = key[1] * 4 + oc
            c.op("dve", lambda e: e.tensor_tensor(rT[:, ch, 0:N], rT[:, ch, 0:N], ps, ALU.add),
                 reads=[pb, bf("rT%d" % ch)], writes=[bf("rT%d" % ch)])
            norm_acc(ch, N)
        linear([("ev_out", j) for j in range(2)], h_rhs(N), 8, N, ev_o, keep=1)
        if t == 4:
            ffn_ple(0, N, pTs, d_pT[0, :, TT - 2:TT])
        else:
            ffn_ple(0, N, pTs, d_pT[0, :, (t - 4) * TT:(t - 3) * TT])

    def attention_prompt(t, QA, QB, Pb, rden, ab):
        cur, prev = t % 2, (t + 1) % 2
        pbufs = [ab["P0"], ab["P1"], ab["P2"]]
        kb0, kb1, kb2r, kb2c = bf("kv_KT0"), bf("kv_KT1"), bf("kv_KT2r"), bf("kv_KT2c")
        vb0, vb1, vb2a, vb2b = bf("kv_VK0"), bf("kv_VK1"), bf("kv_V2A"), bf("kv_V2B")
        for ch in range(4):
            zero_acc(TT)
            all_units = []
            for hh in range(2):
                Q = QA if hh == 0 else QB
                qb_ = ab["QA"] if hh == 0 else ab["QB"]
                po = 64 * hh
                fo = 128 * ch + 64 * hh
                units = []
                for half in range(2):
                    items = []
                    for qi in range(2):
                        qb = 2 * half + qi
                        q_ap = Q[:, 0, ch, 128 * qb:128 * qb + 128]
                        if qb == 0:
                            kp = KT0[:, ch, prev * TT + 384:prev * TT + 512]
                            vp = VK0[:, prev, 3, fo:fo + 64]
                        else:
                            kp = KT0[:, ch, cur * TT + 128 * (qb - 1):cur * TT + 128 * qb]
                            vp = VK0[:, cur, qb - 1, fo:fo + 64]
                        kc_ = KT0[:, ch, cur * TT + 128 * qb:cur * TT + 128 * qb + 128]
                        vc_ = VK0[:, cur, qb, fo:fo + 64]
                        oc_ = (128 * qb, 128 * qb + 128, 1)
                        items.append((kp, kb0, q_ap, (2 * qi) * 128, 128, vp, vb0, oc_, False))
                        items.append((kc_, kb0, q_ap, (2 * qi + 1) * 128, 128, vc_, vb0, oc_, False))
                    units.append((items, half * 512))
                for half in range(2):
                    items = []
                    for qi in range(2):
                        r4 = 2 * half + qi
                        q_ap = Q[:, 1, ch, r4:TT:4]
                        kp = KT1[:, ch, prev * TT + r4:prev * TT + TT:4]
                        kc_ = KT1[:, ch, cur * TT + r4:cur * TT + TT:4]
                        vp = VK1[:, prev, r4, fo:fo + 64]
                        vc_ = VK1[:, cur, r4, fo:fo + 64]
                        oc_ = (r4, TT, 4)
                        items.append((kp, kb1, q_ap, (2 * qi) * 128, 128, vp, vb1, oc_, False))
                        items.append((kc_, kb1, q_ap, (2 * qi + 1) * 128, 128, vc_, vb1, oc_, False))
                    units.append((items, 1024 + half * 512))
                items = []
                for r in range(16):
                    items.append((KT2[:, ch, r:4 * TT:16], kb2r, Q[:, 2, ch, r:TT:16], 32 * r, 32,
                                  V2A[:, r, fo:fo + 64], vb2a, (r, TT, 16), False))
                units.append((items, 2048))
                items = []
                for r4 in range(4):
                    for rho in range(4):
                        r = r4 + 4 * rho
                        items.append((KT2[:, ch, 4 * TT + r4:5 * TT:4], kb2c, Q[:, 2, ch, r:TT:16],
                                      (4 * r4 + rho) * 32, 32, V2B[:, r4, fo:fo + 64], vb2b, (r, TT, 16), False))
                units.append((items, 2560))
                if t == 4:
                    def need(it):
                        o0, o1, os_ = it[7]
                        cols = range(o0, o1, os_)
                        return 510 in cols or 511 in cols
                    units = [([it for it in items if need(it)], mcol) for items, mcol in units]
                    units = [u for u in units if u[0]]
                for ui, (items, mcol) in enumerate(units):
                    all_units.append((items, mcol, qb_, po, ui == len(units) - 1 and hh == 1))

            def stage_s(u):
                items, mcol, qb_, po, last_unit = u
                sp_, sb_ = s_bank()
                for ii, (k_ap, kb_, q_ap, scol, ncol, v_ap, vb_, oc_, st) in enumerate(items):
                    mm(sp_[:, scol:scol + ncol], k_ap, q_ap, True, True, [kb_, qb_], [sb_], ii == len(items) - 1)
                pi = rr["P"] % 3
                rr["P"] += 1
                P = Pb[:, pi, :]
                c.op("act", lambda e, P=P, sp_=sp_: e.activation(P, sp_[:, :], AF.Exp, scale=0.125),
                     reads=[sb_], writes=[pbufs[pi]])
                c.op("dve", lambda e, P=P, mcol=mcol: e.tensor_tensor(P, P, msk[:, mcol:mcol + 512], ALU.mult),
                     reads=[pbufs[pi], bf("msk")], writes=[pbufs[pi]])
                return (P, pi)

            def stage_pv(u, pp):
                items, mcol, qb_, po, last_unit = u
                P, pi = pp
                for ii, (k_ap, kb_, q_ap, scol, ncol, v_ap, vb_, oc_, st) in enumerate(items):
                    lastmm = last_unit and ii == len(items) - 1
                    o0, o1, os_ = oc_
                    mm(nump[po:po + 64, o0:o1:os_], v_ap, P[:, scol:scol + ncol], False, lastmm,
                       [vb_, pbufs[pi]], [bf("num")], False)
                    mm(denp[po:po + 64, o0:o1:os_], ones[:, 0:64], P[:, scol:scol + ncol], False, lastmm,
                       [CONST, pbufs[pi]], [bf("den")], ii == len(items) - 1)

            pend = None
            for u in all_units:
                pp = stage_s(u)
                if pend is not None:
                    stage_pv(*pend)
                pend = (u, pp)
            stage_pv(*pend)
            c.op("dve", lambda e: e.reciprocal(rden[:, :], denp[:, :]), reads=[bf("den")], writes=[ab["rden"]])
            c.op("dve", lambda e, ch=ch: e.tensor_tensor(hT[:, 4 + ch, :], nump[:, :], rden[:, :], ALU.mult),
                 reads=[bf("num"), ab["rden"]], writes=[bf("hT%d" % (4 + ch))])

    def pool_mixer_prompt(t):
        N = TT
        W = 15 + N
        pa = aview(0, [128, 4, 528], F32)
        pb_ = aview(4224, [128, 4, 528], F32)
        icn = aview(8448, [128, 4, TT], BF16)
        pld = aview(10496, [128, 4, TT], BF16)
        nb = {n: bf("pl_" + n) for n in ["pa", "pb", "icn", "pld"]}
        c.handoff(ARENA_BUFS[0], list(nb.values()))
        ARENA_BUFS[0] = list(nb.values())
        c.dma("pool", icn[:, :, :], d_icnt[t - 4], writes=[nb["icn"]])
        for g in range(4):
            src, srcb = a_ext[:, g, :], bf("a_ext")
            off = 0
            bufs = [(pa[:, g, :], nb["pa"]), (pb_[:, g, :], nb["pb"])]
            for k in range(g + 1):
                sh = 1 << k
                dst, dstb = bufs[k % 2]
                eng = "dve" if (g + k) % 2 == 0 else "pool"
                c.op(eng, lambda e, dst=dst, src=src, off=off, sh=sh: e.tensor_tensor(
                    dst[:, off + sh:W], src[:, off + sh:W], src[:, off:W - sh], ALU.add),
                    reads=[srcb], writes=[dstb])
                src, srcb = dst, dstb
                off += sh
            tmp, tmpb = bufs[(g + 1) % 2]
            c.op("dve", lambda e, tmp=tmp, src=src, g=g: e.tensor_tensor(
                tmp[:, 15:W], src[:, 15:W], icn[:, g, :], ALU.mult), reads=[srcb, nb["icn"]], writes=[tmpb])
            c.op("pool", lambda e, tmp=tmp, g=g: e.tensor_tensor(
                pld[:, g, :], tmp[:, 15:W], a_ext[:, g, 15:W], ALU.subtract),
                reads=[tmpb, bf("a_ext")], writes=[nb["pld"]])
            ps, pb2 = mm_bank()
            mm(ps[:, 0:N], poolw[:, g, :], pld[:, g, :], True, True, [CONST, nb["pld"]], [pb2], True)
            c.op("act", lambda e, g=g, ps=ps: e.activation(hT[:, g, 0:N], ps[:, 0:N], AF.Identity,
                                                           scale=vecs[:, V_PSC + g:V_PSC + g + 1]),
                 reads=[pb2, CONST], writes=[bf("hT%d" % g)])
        if t == 8:
            c.dma("sp", o_poolT, a_ext[:, :, TT:TT + 15], reads=[bf("a_ext")], is_output=True)
        c.op("dve", lambda e: e.tensor_copy(a_ext[:, :, 0:15], a_ext[:, :, TT:TT + 15]),
             reads=[bf("a_ext")], writes=[bf("a_ext")])

    L1_U, L1_VN, L1_VTK, L1_GO, L1_GI, L1_HD = 0, 2048, 4096, 6144, 8192, 10240

    def layer1(kind, t, N):
        sample = kind == "sample"
        halo = kind == "halo"
        uT = aview(L1_U, [128, 4, TT])
        vnT = aview(L1_VN, [128, 4, TT])
        vtk = aview(L1_VTK, [128, 4, TT])
        go = aview(L1_GO, [128, 4, TT])
        gi = aview(L1_GI, [128, 4, TT])
        nb = {n: bf("l1_" + n) for n in ["u", "vn", "vtk", "go", "gi", "hd"]}
        c.handoff(ARENA_BUFS[0], list(nb.values()))
        ARENA_BUFS[0] = list(nb.values())
        if sample:
            hd = aview(L1_HD, [128, 4, 16, 6], F32)
            c.dma("sp", scr[:, 2, 0:128], d_sconv, writes=[bf("scr2")])
            c.op("dve", lambda e: e.tensor_copy(hd[:, :, :, 0:2], scr[:, 2, 0:128].rearrange("p (a b c) -> p a b c", a=4, b=16)),
                 reads=[bf("scr2")], writes=[nb["hd"]])
        else:
            hd = aview(L1_HD, [128, 4, 516], F32)
        ph("L1norm %s" % kind)
        rmsnorm(N, V_NMIX[1], pre=True)
        ph("L1proj")
        dump(9, "h", N)
        if not halo:
            def ev_u(key, oc, ps, pb):
                c.op("act", lambda e: e.activation(uT[:, oc, 0:N], ps, AF.Gelu), reads=[pb], writes=[nb["u"]])
            linear([("od_in", "u")], h_rhs(N), 8, N, ev_u)

            def ev_v(key, oc, ps, pb):
                z_ap, z_b = zb_buf()
                c.op("act", lambda e: e.activation(z_ap[:, 0:N], ps, AF.Gelu), reads=[pb], writes=[z_b])

                def stage_b():
                    m_ps, m_b = mm_bank()
                    mm(m_ps[:, 0:N], ones128[:], z_ap[:, 0:N], True, True, [z_b, CONST], [m_b], True)
                    vc, vcb = scr_buf()
                    c.op("dve", lambda e: e.tensor_tensor(vc[:, 0:N], z_ap[:, 0:N], m_ps[:, 0:N], ALU.subtract),
                         reads=[z_b, m_b], writes=[vcb])
                    s_ap, s_b = sq_buf()
                    c.op("act", lambda e: e.activation(s_ap[:, 0:N], vc[:, 0:N], AF.Square), reads=[vcb], writes=[s_b])

                    def stage_c():
                        v_ps, v_b = mm_bank()
                        mm(v_ps[:, 0:N], ones128[:], s_ap[:, 0:N], True, True, [s_b, CONST], [v_b], True)
                        sd, sdb = scr_buf()
                        c.op("act", lambda e: e.activation(sd[:, 0:N], v_ps[:, 0:N], AF.Sqrt, bias=EPS, scale=1.0),
                             reads=[v_b], writes=[sdb])
                        c.op("dve", lambda e: e.reciprocal(sd[:, 0:N], sd[:, 0:N]), reads=[sdb], writes=[sdb])
                        c.op("dve", lambda e: e.tensor_tensor(vc[:, 0:N], vc[:, 0:N], sd[:, 0:N], ALU.mult),
                             reads=[vcb, sdb], writes=[vcb])
                        c.op("act", lambda e: e.activation(vnT[:, oc, 0:N], vc[:, 0:N], AF.Identity,
                                                           bias=vecs[:, V_LNB + oc:V_LNB + oc + 1],
                                                           scale=vecs[:, V_LNG + oc:V_LNG + oc + 1]),
                             reads=[vcb, CONST], writes=[nb["vn"]])
                    deferred.append(stage_c)
                deferred.append(stage_b)
            linear([("od_in", "v")], h_rhs(N), 8, N, ev_v)
            def ev_go(key, oc, ps, pb):
                copy_op("act", go[:, oc, 0:N], ps, [pb], [nb["go"]])
            linear([("od_in", "go")], h_rhs(N), 8, N, ev_go)

        def ev_gi(key, oc, ps, pb):
            copy_op("act", gi[:, oc, 0:N], ps, [pb], [nb["gi"]])
        linear([("od_in", "gi")], h_rhs(N), 8, N, ev_gi)
        if not sample:
            c.op("pool", lambda e: e.tensor_copy(hd[:, :, 0:2], hd_halo[:, :, :]), reads=[bf("hd_halo")], writes=[nb["hd"]])

        def ev_xi(key, oc, ps, pb):
            if sample:
                c.op("dve", lambda e: e.tensor_tensor(hd[:, oc, :, 2:6], ps.rearrange("p (n i) -> p n i", i=4),
                                                      gi[:, oc, 0:N].rearrange("p (n i) -> p n i", i=4), ALU.mult),
                     reads=[pb, nb["gi"]], writes=[nb["hd"]])
            else:
                c.op("dve", lambda e: e.tensor_tensor(hd[:, oc, 2:2 + N], ps, gi[:, oc, 0:N], ALU.mult),
                     reads=[pb, nb["gi"]], writes=[nb["hd"]])
        linear([("od_in", "xi")], h_rhs(N), 8, N, ev_xi)
        if not sample:
            c.op("pool", lambda e: e.tensor_copy(hd_halo[:, :, :], hd[:, :, N:N + 2]), reads=[nb["hd"]], writes=[bf("hd_halo")])
            if t == 8:
                c.dma("sp", o_convT, hd[:, :, N:N + 2], reads=[nb["hd"]], is_output=True)
        else:
            c.op("dve", lambda e: e.tensor_copy(scr[:, 2, 0:128].rearrange("p (a b c) -> p a b c", a=4, b=16), hd[:, :, :, 4:6]),
                 reads=[nb["hd"]], writes=[bf("scr2")])
            c.dma("sp", o_convsT, scr[:, 2, 0:128], reads=[bf("scr2")], is_output=True)
        if halo:
            return
        ph("L1gate")
        flush_deferred(everything=True)
        if sample:
            c.dma("pool", o_cvsT.rearrange("(ch p) n -> p ch n", p=128), vnT[:, :, 0:N],
                  reads=[nb["vn"]], is_output=True)
        for g in range(4):
            tmp, tmpb = scr_buf()
            if not sample:
                th, thb = trp_half()
                for blk in range(4):
                    c.op("pe", lambda e, th=th, blk=blk, g=g: e.transpose(
                        th[:, 128 * blk:128 * blk + 128], vnT[:, g, 128 * blk:128 * blk + 128], ident[:]),
                        reads=[nb["vn"], CONST], writes=[thb], inc=(blk == 3))
                copy_op(ev_eng(), vtk[:, g, :], th, [thb], [nb["vtk"]])
                ps, pb = mm_bank()
                for blk in range(4):
                    mm(ps[:, 128 * blk:128 * blk + 128], vtk[:, g, 128 * blk:128 * blk + 128], wmT[:, g, :],
                       True, True, [nb["vtk"], CONST], [pb], blk == 3)
                c.op("dve", lambda e, tmp=tmp, ps=ps, g=g: e.tensor_tensor(tmp[:, 0:N], ps[:, 0:N], bsT[:, g, :], ALU.add),
                     reads=[pb, CONST], writes=[tmpb])
            else:
                vv = vnT[:, g, 0:N].rearrange("p (n i) -> p n i", i=4)
                tv = tmp[:, 0:N].rearrange("p (n i) -> p n i", i=4)
                for ti in range(4):
                    c.op("act", lambda e, ti=ti, g=g, tv=tv, vv=vv: e.activation(
                        tv[:, :, ti], vv[:, :, 0], AF.Identity, scale=wsS[:, g, 4 * ti:4 * ti + 1]),
                        reads=[nb["vn"], CONST], writes=[tmpb])
                    for si in range(1, ti + 1):
                        c.op("dve", lambda e, ti=ti, si=si, g=g, tv=tv, vv=vv: e.scalar_tensor_tensor(
                            tv[:, :, ti], vv[:, :, si], wsS[:, g, 4 * ti + si:4 * ti + si + 1], tv[:, :, ti],
                            ALU.mult, ALU.add), reads=[nb["vn"], CONST, tmpb], writes=[tmpb])
                c.op("dve", lambda e, tmp=tmp, g=g: e.tensor_tensor(tmp[:, 0:N], tmp[:, 0:N], bsS[:, g, :], ALU.add),
                     reads=[tmpb, CONST], writes=[tmpb])
            c.op("dve", lambda e, tmp=tmp, g=g: e.tensor_tensor(hT[:, g, 0:N], tmp[:, 0:N], uT[:, g, 0:N], ALU.mult),
                 reads=[tmpb, nb["u"]], writes=[bf("hT%d" % g)])

        ph("L1conv")
        for ch in range(4):
            acc, accb = scr_buf()
            cw = lambda j, ch=ch: vecs[:, V_CW + 3 * ch + j:V_CW + 3 * ch + j + 1]
            if sample:
                av = acc[:, 0:N].rearrange("p (n i) -> p n i", i=4)
                hv = lambda j, ch=ch: hd[:, ch, :, j:j + 4]
            else:
                av = acc[:, 0:N]
                hv = lambda j, ch=ch: hd[:, ch, j:j + N]
            c.op("act", lambda e, av=av, hv=hv, cw=cw: e.activation(av, hv(0), AF.Identity, scale=cw(0)),
                 reads=[nb["hd"], CONST], writes=[accb])
            for j in (1, 2):
                c.op("dve", lambda e, av=av, hv=hv, cw=cw, j=j: e.scalar_tensor_tensor(
                    av, hv(j), cw(j), av, ALU.mult, ALU.add), reads=[nb["hd"], CONST, accb], writes=[accb])
            c.op("dve", lambda e, acc=acc, ch=ch: e.tensor_tensor(hT[:, 4 + ch, 0:N], go[:, ch, 0:N], acc[:, 0:N], ALU.mult),
                 reads=[accb, nb["go"]], writes=[bf("hT%d" % (4 + ch))])

        def ev_o(key, oc, ps, pb):
            ch = key[1] * 4 + oc
            c.op("dve", lambda e: e.tensor_tensor(rT[:, ch, 0:N], rT[:, ch, 0:N], ps, ALU.add),
                 reads=[pb, bf("rT%d" % ch)], writes=[bf("rT%d" % ch)])
            norm_acc(ch, N)
        dump(5, "h", N)
        ph("L1wout")
        linear([("od_out", j) for j in range(2)], h_rhs(N), 8, N, ev_o, keep=1)
        dump(6, "r", N)
        ffn_ple(1, N, pTs, d_psT[1] if sample else d_pT[1, :, (t - 4) * TT:(t - 3) * TT])

    def final_out(N, o_ap, col0):
        def out_fn(ch, r_ap, r_b):
            ridx = (rr["scr"] - 1) % 3 if ch == 0 else out_fn.ridx
            out_fn.ridx = ridx
            yi = (ridx + 1 + (ch % 2)) % 3
            y, yb = scr[:, yi, :], bf("scr%d" % yi)
            c.op("dve", lambda e: e.scalar_tensor_tensor(y[:, 0:N], rT[:, ch, 0:N], vecs[:, V_NFIN + ch:V_NFIN + ch + 1],
                                                         r_ap[:, 0:N], ALU.mult, ALU.mult),
                 reads=[bf("rT%d" % ch), r_b, CONST], writes=[yb])
            c.dma("sp", o_ap[128 * ch:128 * ch + 128, col0:col0 + N], y[:, 0:N], reads=[yb], is_output=True)
            x_prefetch_next(ch)
        ph("final")
        rmsnorm(N, V_NFIN, out_fn=out_fn, pre=True)

    def layer0_sample():
        N = NS
        QA = aview(A_QA, [128, 3, 4, TT])
        QB = aview(A_QB, [128, 3, 4, TT])
        VT = aview(A_VT, [128, 4, TT])
        Pb = aview(A_P, [128, 3, TT])
        rden = aview(A_RD, [128, TT], F32)
        ab = {n: bf("ar_" + n) for n in ["QA", "QB", "VT", "P0", "P1", "P2", "rden"]}
        c.handoff(ARENA_BUFS[0], list(ab.values()))
        ARENA_BUFS[0] = list(ab.values())
        kcb = [kview(0, [128, 9, 4, 128]), kview(4608, [128, 9, 4, 128])]
        vcb = [kview(9216, [128, 9, 512]), kview(13824, [128, 9, 512])]
        KTs = kview(18432, [128, 3, 4, NS])
        VsT = kview(19200, [128, 3, 512])
        Pn = kview(20736, [128, 512])
        a_s = kvr[:, 21248:21248 + 2 * 4 * 16 * 19].bitcast(F32).rearrange("p (a b c) -> p a b c", a=4, b=16)
        p1 = kvr[:, 23680:23680 + 2 * 16 * 19].bitcast(F32).rearrange("p (b c) -> p b c", b=16)
        p2 = kvr[:, 24288:24288 + 2 * 16 * 19].bitcast(F32).rearrange("p (b c) -> p b c", b=16)
        plds = kview(24896, [128, 4, NS])
        sb_ = {n: bf("sk_" + n) for n in ["kc0", "kc1", "vc0", "vc1", "KTs", "VsT", "Pn", "a_s", "p1", "p2", "pld"]}
        c.handoff(KV_BUFS, list(sb_.values()))
        c.op("dve", lambda e: e.memset(arena[64:128, 0:6144], 0.0), writes=[ab["QA"]])
        c.op("dve", lambda e: e.memset(arena[0:64, 6144:12288], 0.0), writes=[ab["QB"]])
        c.op("pool", lambda e: e.memset(VsT[:, :, :], 0.0), writes=[sb_["VsT"]])
        c.op("pool", lambda e: e.memset(Pn[:, :], 0.0), writes=[sb_["Pn"]])
        stg = scr[:, 0:2, :].rearrange("p a b -> p (a b)")
        c.dma("sp", stg[:, 0:960], d_spool, writes=[bf("scr0"), bf("scr1")])
        c.op("dve", lambda e: e.tensor_copy(a_s[:, :, :, 0:15], stg[:, 0:960].rearrange("p (a b c) -> p a b c", a=4, b=16)),
             reads=[bf("scr0"), bf("scr1")], writes=[sb_["a_s"]])
        ph("S L0norm")
        rmsnorm(N, V_NMIX[0])
        dump(0, "h", N)
        ph("S L0proj")

        def ev_a(key, oc, ps, pb):
            c.op("act", lambda e: e.copy(a_s[:, oc, :, 15:19], ps.rearrange("p (n i) -> p n i", i=4)),
                 reads=[pb], writes=[sb_["a_s"]])
        linear([("ev_in", "a")], h_rhs(N), 8, N, ev_a)
        for g in range(3):
            def ev_q(key, oc, ps, pb, g=g):
                rope(ps, pb, N, out_halves=(QA[:, g, oc, 0:N], QB[:, g, oc, 0:N]), out_bufs=(ab["QA"], ab["QB"]))
            linear([("ev_in", g, "q")], h_rhs(N), 8, N, ev_q)

            def ev_k(key, oc, ps, pb, g=g):
                rope(ps, pb, N, out_full=KTs[:, g, oc, :], out_bufs=(sb_["KTs"],))
            linear([("ev_in", g, "k")], h_rhs(N), 8, N, ev_k)

            def ev_v(key, oc, ps, pb):
                copy_op(ev_eng(), VT[:, oc, 0:N], ps, [pb], [ab["VT"]])
            linear([("ev_in", g, "v")], h_rhs(N), 8, N, ev_v)
            flush_deferred(everything=True)
            c.dma("pool", o_kvsT[g, 0].rearrange("(ch p) n -> p ch n", p=128), KTs[:, g, :, :],
                  reads=[sb_["KTs"]], is_output=True)
            c.dma("pool", o_kvsT[g, 1].rearrange("(ch p) n -> p ch n", p=128), VT[:, :, 0:N],
                  reads=[ab["VT"]], is_output=True)
            th, thb = trp_half()
            for ch in range(4):
                c.op("pe", lambda e, th=th, ch=ch: e.transpose(th[0:N, 128 * ch:128 * ch + 128], VT[:, ch, 0:N], ident[:]),
                     reads=[ab["VT"], CONST], writes=[thb], inc=(ch == 3))
            copy_op(ev_eng(), VsT[0:N, g, :], th[0:N, :], [thb], [sb_["VsT"]])
        ph("S att new")
        pbufs = [ab["P0"], ab["P1"], ab["P2"]]
        zero_acc(256)
        for g in range(3):
            sp_, sbk = s_bank()
            for h in range(8):
                ch, hh = h // 2, h % 2
                Q = QA if hh == 0 else QB
                mm(sp_[0:N, 64 * h:64 * h + 64], KTs[:, g, ch, :], Q[:, g, ch, 0:N], True, True,
                   [sb_["KTs"], ab["QA"], ab["QB"]], [sbk], h == 7)
            c.op("act", lambda e, sp_=sp_: e.activation(Pn[0:N, :], sp_[0:N, :], AF.Exp, scale=0.125),
                 reads=[sbk], writes=[sb_["Pn"]])
            c.op("dve", lambda e, g=g: e.tensor_tensor(Pn[0:N, :], Pn[0:N, :], maskSn[0:N, g, :], ALU.mult),
                 reads=[sb_["Pn"], CONST], writes=[sb_["Pn"]])
            for h in range(8):
                ch, hh = h // 2, h % 2
                po = 64 * hh
                mm(nump[po:po + 64, 64 * ch:64 * ch + 64], VsT[:, g, 64 * h:64 * h + 64], Pn[:, 64 * h:64 * h + 64],
                   False, False, [sb_["VsT"], sb_["Pn"]], [bf("num")], False)
                mm(denp[po:po + 64, 64 * ch:64 * ch + 64], ones[:, 0:64], Pn[:, 64 * h:64 * h + 64],
                   False, False, [CONST, sb_["Pn"]], [bf("den")], h == 7)
        ph("S att cache")
        for n in range(16):
            kb_ap, vb_ap = kcb[n % 2], vcb[n % 2]
            kbb, vbb = sb_["kc%d" % (n % 2)], sb_["vc%d" % (n % 2)]
            c.dma("pool", kb_ap.rearrange("p a b c -> p (a b c)"), d_kc[n], writes=[kbb])
            c.dma("pool", vb_ap.rearrange("p a b -> p (a b)"), d_vc[n], writes=[vbb])
            sp_, sbk = s_bank()
            sets = [(0, 0, 4 * n, 4, 0)] + [(1, 1 + i, 4 * n + i, 1, 4 + i) for i in range(4)] + \
                   [(2, 5 + i, 4 * n + i, 1, 8 + i) for i in range(4)]
            for h in range(8):
                ch, hh = h // 2, h % 2
                Q = QA if hh == 0 else QB
                for si, (g, s, q0, nq, k0) in enumerate(sets):
                    mm(sp_[:, 12 * h + k0:12 * h + k0 + nq], kb_ap[:, s, ch, :], Q[:, g, ch, q0:q0 + nq], True, True,
                       [kbb, ab["QA"], ab["QB"]], [sbk], h == 7 and si == 8)
            pi = rr["P"] % 3
            rr["P"] += 1
            P = Pb[:, pi, 0:96]
            c.op("act", lambda e, P=P, sp_=sp_: e.activation(P, sp_[:, 0:96], AF.Exp, scale=0.125),
                 reads=[sbk], writes=[pbufs[pi]])
            c.op("dve", lambda e, P=P: e.tensor_tensor(P, P, maskSc[:, :], ALU.mult),
                 reads=[pbufs[pi], CONST], writes=[pbufs[pi]])
            for h in range(8):
                ch, hh = h // 2, h % 2
                po = 64 * hh
                for si, (g, s, q0, nq, k0) in enumerate(sets):
                    last = (n == 15)
                    mm(nump[po:po + 64, 64 * ch + q0:64 * ch + q0 + nq], vb_ap[:, s, 64 * h:64 * h + 64],
                       P[:, 12 * h + k0:12 * h + k0 + nq], False, last, [vbb, pbufs[pi]], [bf("num")], False)
                    mm(denp[po:po + 64, 64 * ch + q0:64 * ch + q0 + nq], ones[:, 0:64],
                       P[:, 12 * h + k0:12 * h + k0 + nq], False, last, [CONST, pbufs[pi]], [bf("den")],
                       h == 7 and si == 8)
        c.op("dve", lambda e: e.reciprocal(rden[:, 0:256], denp[:, 0:256]), reads=[bf("den")], writes=[ab["rden"]])
        for ch in range(4):
            c.op("dve", lambda e, ch=ch: e.tensor_tensor(hT[:, 4 + ch, 0:N], nump[:, 64 * ch:64 * ch + 64],
                                                         rden[:, 64 * ch:64 * ch + 64], ALU.mult),
                 reads=[bf("num"), ab["rden"]], writes=[bf("hT%d" % (4 + ch))])
        ph("S pool")
        for g in range(4):
            w = 2 << g
            src, srcb = a_s[:, g, :, :], sb_["a_s"]
            off = 0
            bufs = [(p1, sb_["p1"]), (p2, sb_["p2"])]
            for k in range(g + 1):
                sh = 1 << k
                dst, dstb = bufs[k % 2]
                c.op("dve", lambda e, dst=dst, src=src, off=off, sh=sh: e.tensor_tensor(
                    dst[:, :, off + sh:19], src[:, :, off + sh:19], src[:, :, off:19 - sh], ALU.add),
                    reads=[srcb], writes=[dstb])
                src, srcb = dst, dstb
                off += sh
            c.op("dve", lambda e, src=src, g=g, w=w: e.scalar_tensor_tensor(
                plds[:, g, :].rearrange("p (n i) -> p n i", i=4), src[:, :, 15:19], 1.0 / w, a_s[:, g, :, 15:19],
                ALU.mult, ALU.subtract), reads=[srcb, sb_["a_s"]], writes=[sb_["pld"]])
            ps, pb2 = mm_bank()
            mm(ps[:, 0:N], poolw[:, g, :], plds[:, g, :], True, True, [CONST, sb_["pld"]], [pb2], True)
            c.op("act", lambda e, g=g, ps=ps: e.activation(hT[:, g, 0:N], ps[:, 0:N], AF.Identity,
                                                           scale=vecs[:, V_PSC + g:V_PSC + g + 1]),
                 reads=[pb2, CONST], writes=[bf("hT%d" % g)])
        stg = scr[:, 0:2, :].rearrange("p a b -> p (a b)")
        c.op("dve", lambda e: e.tensor_copy(stg[:, 0:960].rearrange("p (a b c) -> p a b c", a=4, b=16), a_s[:, :, :, 4:19]),
             reads=[sb_["a_s"]], writes=[bf("scr0"), bf("scr1")])
        c.dma("sp", o_poolsT, stg[:, 0:960], reads=[bf("scr0"), bf("scr1")], is_output=True)

        def ev_o(key, oc, ps, pb):
            ch = key[1] * 4 + oc
            c.op("dve", lambda e: e.tensor_tensor(rT[:, ch, 0:N], rT[:, ch, 0:N], ps, ALU.add),
                 reads=[pb, bf("rT%d" % ch)], writes=[bf("rT%d" % ch)])
            norm_acc(ch, N)
        ph("S wout")
        dump(1, "h", N)
        linear([("ev_out", j) for j in range(2)], h_rhs(N), 8, N, ev_o, keep=1)
        dump(2, "r", N)
        ffn_ple(0, N, pTs, d_psT[0])

    def _main_loop():
        for ti_, (kind, t) in enumerate(tiles):
            cur["i"] = ti_
            for ch_ in range(8):
                x_load_chunk(ti_, ch_)
            if kind != "sample":
                c.dma("sp", cs[:, :, :], d_cs[:, :, t * TT:(t + 1) * TT], writes=[bf("cs")])
                if kind == "full":
                    c.dma("pool", msk[:, :], d_mask[t - 4], writes=[bf("msk")])
                layer0_prompt(kind, t)
            else:
                c.dma("sp", cs[:, :, 0:NS], d_cs[:, :, NT * TT:NT * TT + NS], writes=[bf("cs")])
                layer0_sample()
            if kind == "kv":
                continue
            if kind == "full" and t == 4:
                layer1("halo", t, 2)
                continue
            if kind == "full":
                layer1("full", t, TT)
                final_out(TT, o_yT, (t - 5) * TT)
            else:
                layer1("sample", t, NS)
                final_out(NS, o_ysT, 0)

    try:
        _main_loop()
    except _Stop:
        pass
    ph("end")
    c.finish()
    c.close()
    return nc, c


_CACHE = {}


def _vec_pc(v):
    return np.ascontiguousarray(v.reshape(-1, 128).T)


def _shared_inputs(inp):
    f = np.float32
    sh = {}
    WB = _wblocks()
    wall = np.zeros((len(WB), 128, 4096), f)
    for i, (key, nm, l, c0, ncols, kcn) in enumerate(WB):
        W = np.asarray(inp[nm][l], dtype=f)[:, c0:c0 + ncols]
        wall[i, :, 0:kcn * ncols] = W.reshape(kcn, 128, ncols).transpose(1, 0, 2).reshape(128, kcn * ncols)
    sh["wall"] = wall
    sh["pool_w"] = np.ascontiguousarray(inp["ev_pool_w"][0].transpose(1, 0, 2), dtype=f)
    ws = np.asarray(inp["od_ws"][0], dtype=f)
    bs = np.asarray(inp["od_bs"][0], dtype=f)
    sh["od_wsT"] = np.ascontiguousarray(ws.transpose(2, 0, 1))
    vec = np.zeros((128, 80), f)
    vec[:, 0:8] = _vec_pc(inp["norm_mix"][0])
    vec[:, 8:16] = _vec_pc(inp["norm_mix"][1])
    vec[:, 16:24] = _vec_pc(inp["norm_ffn"][0])
    vec[:, 24:32] = _vec_pc(inp["norm_ffn"][1])
    vec[:, 32:40] = _vec_pc(inp["norm_ple"][0])
    vec[:, 40:48] = _vec_pc(inp["norm_ple"][1])
    vec[:, 48:56] = _vec_pc(inp["norm_final"])
    vec[:, 56:60] = _vec_pc(inp["ev_pool_scale"][0])
    vec[:, 60:64] = _vec_pc(inp["od_ln_g"][0])
    vec[:, 64:68] = _vec_pc(inp["od_ln_b"][0])
    cw = np.asarray(inp["od_conv_w"][0], dtype=f)
    for ch in range(4):
        for j in range(3):
            vec[:, 68 + 3 * ch + j] = cw[j, ch * 128:(ch + 1) * 128]
    sh["vecs"] = vec
    sh["bsT"] = np.ascontiguousarray(np.broadcast_to(np.tile(bs, (1, 4))[None], (128, 4, 512)), dtype=f)
    sh["bsS"] = np.ascontiguousarray(np.broadcast_to(np.tile(bs[:, 0:4], (1, 16))[None], (128, 4, 64)), dtype=f)
    sh["wsS"] = np.ascontiguousarray(np.broadcast_to(ws[:, 0:4, 0:4].reshape(4, 16)[None], (128, 4, 16)), dtype=f)
    sh["ident"] = np.eye(128, dtype=f)
    rot = np.zeros((128, 128), f)
    for m in range(128):
        k = m + 32 if (m % 64) < 32 else m - 32
        rot[k, m] = 1.0
    sh["rotm"] = rot
    sidx = np.arange(128)
    sh["trimask"] = (sidx[:, None] <= sidx[None, :]).astype(f)
    msc = np.ones((128, 96), f)
    for h in range(8):
        for k in range(4):
            msc[:, 12 * h + k] = (sidx >= k).astype(f)
    sh["maskSc"] = msc
    msn = np.zeros((128, 3, 512), f)
    row = np.arange(64)
    col = np.arange(64)
    same = (row[:, None] // 4) == (col[None, :] // 4)
    m0 = same & ((row[:, None] % 4) <= (col[None, :] % 4))
    m1 = same & ((row[:, None] % 4) == (col[None, :] % 4))
    for h in range(8):
        msn[0:64, 0, 64 * h:64 * h + 64] = m0
        msn[0:64, 1, 64 * h:64 * h + 64] = m1
        msn[0:64, 2, 64 * h:64 * h + 64] = m1
    sh["maskSn"] = msn
    return sh


def _rope_tables(pos):
    f = np.float32
    half = 32
    inv = np.power(f(10000.0), -np.arange(half, dtype=f) / f(half)).astype(f)
    ang = pos.astype(f)[None, :] * inv[:, None]
    cos = np.cos(ang).astype(f)
    sin = np.sin(ang).astype(f)
    p = np.arange(128)
    fi = p % 32
    sign = np.where((p % 64) < 32, -1.0, 1.0).astype(f)
    out = np.empty((128, 2, pos.shape[0]), f)
    out[:, 0, :] = cos[fi]
    out[:, 1, :] = sin[fi] * sign[:, None]
    return out


def _core_masks(s):
    f = np.float32
    k = np.arange(128)[:, None]
    q128 = np.arange(128)[None, :]
    prev_m = (k >= q128).astype(f)
    cur_m = (k <= q128).astype(f)
    out = np.zeros((5, 128, MASKW), f)
    for ti in range(5):
        tau = 4 + ti
        valid = lambda tp: 1.0 if (s - 2560 + 512 * tp) >= 0 else 0.0
        m = out[ti]
        for qb in range(4):
            m[:, (qb * 2) * 128:(qb * 2 + 1) * 128] = prev_m * (valid(tau - 1) if qb == 0 else 1.0)
            m[:, (qb * 2 + 1) * 128:(qb * 2 + 2) * 128] = cur_m
        for r4 in range(4):
            m[:, 1024 + (r4 * 2) * 128:1024 + (r4 * 2 + 1) * 128] = prev_m * valid(tau - 1)
            m[:, 1024 + (r4 * 2 + 1) * 128:1024 + (r4 * 2 + 2) * 128] = cur_m
        i32 = np.arange(32)[None, :]
        blk = np.zeros((128, 32), f)
        for slot in range(4):
            tp = [x for x in range(tau - 4, tau) if x % 4 == slot][0]
            u = np.arange(32)[:, None]
            rel = 32 * (tau - tp) + i32 - u
            blk[32 * slot:32 * slot + 32, :] = ((rel <= 128) & (rel >= 0)).astype(f) * valid(tp)
        for r in range(16):
            m[:, 2048 + 32 * r:2048 + 32 * r + 32] = blk
        kk = np.arange(128)[:, None]
        for r4 in range(4):
            for rho in range(4):
                mb = ((kk % 4) == rho) & ((kk // 4) <= i32)
                c0 = 2560 + (4 * r4 + rho) * 32
                m[:, c0:c0 + 32] = mb.astype(f)
    return out


def _core_icnt(s):
    f = np.float32
    out = np.ones((5, 128, 4, TT), f)
    for ti in range(5):
        pos = s - 2560 + 512 * (4 + ti) + np.arange(TT)
        for g in range(4):
            w = 2 << g
            cnt = np.minimum(w, np.maximum(pos, 0) + 1).astype(f)
            out[ti, :, g, :] = (f(1.0) / cnt)[None, :]
    return out


def _gather_cache(inp, n):
    f = np.float32
    kc = np.empty((128, 9, 4, 128), f)
    vc = np.empty((128, 9, 512), f)
    sets = [(inp["cache_kv_w128"], np.arange(128))]
    for i in range(4):
        sets.append((inp["cache_kv_w512"], i + 4 * np.arange(128)))
    for i in range(4):
        sets.append((inp["cache_kv_w2048"], i + 16 * np.arange(128)))
    for si, (arr, rows) in enumerate(sets):
        blk = np.asarray(arr[0, n, rows], dtype=f)
        kk = blk[:, 0].reshape(128, 512)
        vv = blk[:, 1].reshape(128, 512)
        kc[:, si, :, :] = kk.T.reshape(4, 128, 128).transpose(1, 0, 2)
        vc[:, si, :] = vv
    return kc.reshape(128, 9 * 4 * 128), vc.reshape(128, 9 * 512)


def kernel(**inp):
    f = np.float32
    inp = {k: np.asarray(v) for k, v in inp.items()}
    if "prog" not in _CACHE:
        _CACHE["prog"] = build_program()
    nc, _ctx = _CACHE["prog"]
    sh = _shared_inputs(inp)
    xp = np.asarray(inp["x_prompt"], dtype=f)
    pp = np.asarray(inp["p_prompt"], dtype=f)
    xs = np.asarray(inp["x_sample"], dtype=f)
    ps_ = np.asarray(inp["p_sample"], dtype=f)
    in_maps = []
    for core in (_DBG.get("cores") or range(NCORES)):
        b, s = core // 4, (core % 4) * SEG
        m = dict(sh)
        tok = s - 2560 + np.arange(NT * TT)
        xT = np.zeros((1024, NT * TT), f)
        ok = tok >= 0
        xT[:, ok] = xp[b, tok[ok]].T
        m["xT"] = xT
        tokp = s - 512 + np.arange(5 * TT)
        pT = np.zeros((2, 256, 5 * TT), f)
        okp = tokp >= 0
        pT[:, :, okp] = pp[:, b, tokp[okp]].transpose(0, 2, 1)
        m["pT"] = pT
        pos = np.concatenate([np.maximum(tok, 0), 2048 + (np.arange(NS) % 4)])
        m["cs"] = _rope_tables(pos)
        m["mask"] = _core_masks(s)
        m["icnt"] = _core_icnt(s)
        n0 = 16 * core
        m["xsT"] = np.ascontiguousarray(xs[n0:n0 + 16].reshape(NS, 1024).T)
        m["psT"] = np.ascontiguousarray(ps_[:, n0:n0 + 16].reshape(2, NS, 256).transpose(0, 2, 1))
        kcs, vcs = [], []
        for j in range(16):
            a, b_ = _gather_cache(inp, n0 + j)
            kcs.append(a)
            vcs.append(b_)
        m["kc"] = np.stack(kcs)
        m["vc"] = np.stack(vcs)
        sp = np.asarray(inp["state_pool"][0, n0:n0 + 16], dtype=f)
        m["spool"] = np.ascontiguousarray(sp.reshape(16, 15, 4, 128).transpose(3, 2, 0, 1)).reshape(128, 960)
        sc = np.asarray(inp["state_conv"][0, n0:n0 + 16], dtype=f)
        m["sconv"] = np.ascontiguousarray(sc.reshape(16, 2, 4, 128).transpose(3, 2, 0, 1)).reshape(128, 128)
        in_maps.append(m)
    if _DBG.get("cores"):
        res = run_bass_kernel_spmd(nc, in_maps, core_ids=list(range(len(in_maps))))
        return res.results
    res = run_bass_kernel_spmd(nc, in_maps, core_ids=list(range(NCORES)))
    R = res.results
    y_prompt = np.empty((2, SEQ, 1024), f)
    y_sample = np.empty((128, 4, 1024), f)
    kvp = [np.empty((1, 2, w, 2, 8, 64), f) for w in (128, 512, 2048)]
    kvs = [np.empty((1, 128, 4, 2, 8, 64), f) for _ in range(3)]
    pool_p = np.empty((1, 2, 15, 512), f)
    pool_s = np.empty((1, 128, 15, 512), f)
    conv_p = np.empty((1, 2, 2, 512), f)
    conv_s = np.empty((1, 128, 2, 512), f)
    cv_s = np.empty((1, 128, 4, 512), f)
    for core in range(NCORES):
        r = R[core]
        b, s = core // 4, (core % 4) * SEG
        n0 = 16 * core
        y_prompt[b, s:s + SEG] = r["o_yT"].T
        y_sample[n0:n0 + 16] = r["o_ysT"].T.reshape(16, 4, 1024)
        for g in range(3):
            t = r["o_kvsT"][g]
            kvs[g][0, n0:n0 + 16] = t.transpose(2, 0, 1).reshape(16, 4, 2, 8, 64)
        pool_s[0, n0:n0 + 16] = r["o_poolsT"].reshape(128, 4, 16, 15).transpose(2, 3, 1, 0).reshape(16, 15, 512)
        conv_s[0, n0:n0 + 16] = r["o_convsT"].reshape(128, 4, 16, 2).transpose(2, 3, 1, 0).reshape(16, 2, 512)
        cv_s[0, n0:n0 + 16] = r["o_cvsT"].T.reshape(16, 4, 512)
        if core % 4 == 3:
            for g, w in enumerate((128, 512, 2048)):
                t = r["o_kvT"][g][:, :, SEG - w:]
                kvp[g][0, b] = t.transpose(2, 0, 1).reshape(w, 2, 8, 64)
            pool_p[0, b] = r["o_poolT"].transpose(2, 1, 0).reshape(15, 512)
            conv_p[0, b] = r["o_convT"].transpose(2, 1, 0).reshape(2, 512)
    return (y_prompt, y_sample, kvp[0], kvp[1], kvp[2], kvs[0], kvs[1], kvs[2],
            pool_p, pool_s, conv_p, conv_s, cv_s)
```

```python
import numpy as np
import concourse.bass as bass
import concourse.mybir as mybir
from concourse.bass_utils import run_bass_kernel_spmd

F32 = mybir.dt.float32
BF16 = mybir.dt.bfloat16
ALU = mybir.AluOpType
AF = mybir.ActivationFunctionType

NCORES = 8
SEQ = 8192
SEG = 2048
TT = 512
NT = 9
NS = 64
EPS = 1e-6
MASKW = 3072


class Buf:
    __slots__ = ("name", "w", "r")

    def __init__(self, name=""):
        self.name = name
        self.w = None
        self.r = {}


class Eng:
    def __init__(self, key, sem):
        self.key = key
        self.sem = sem
        self.count = 0
        self.waited = {}
        self.prog = []


class Ctx:
    COMPUTE = ("pe", "act", "dve", "pool")

    def __init__(self, nc, n_dma_sems=8):
        self.nc = nc
        self.sems = {}
        self.engs = {}
        self._stack = []
        for k in ("pe", "act", "dve", "pool", "sp"):
            self._sem("e_" + k)
            self.engs[k] = Eng(k, "e_" + k)
        self.dma_pool = {}
        self.dma_rr = {}
        self.dma_val = {}
        for q in ("sp", "act", "pool"):
            names = []
            for i in range(n_dma_sems):
                nm = "d_%s_%d" % (q, i)
                self._sem(nm)
                names.append(nm)
                self.dma_val[nm] = 0
            self.dma_pool[q] = names
            self.dma_rr[q] = 0
        self.out_events = []
        self.n_wait = 0
        self.n_op = 0

    def _sem(self, name):
        cm = self.nc.semaphore(name)
        h = cm.__enter__()
        self._stack.append(cm)
        self.sems[name] = h
        return h

    def sb(self, name, shape, dtype):
        cm = self.nc.sbuf_tensor("sb_" + name, shape, dtype)
        t = cm.__enter__()
        self._stack.append(cm)
        return t

    def ps(self, name, shape, dtype):
        cm = self.nc.psum_tensor("ps_" + name, shape, dtype)
        t = cm.__enter__()
        self._stack.append(cm)
        return t

    def _deps(self, reads, writes):
        deps = {}
        for b in reads:
            if b.w is not None:
                k, v = b.w
                if deps.get(k, 0) < v:
                    deps[k] = v
        for b in writes:
            if b.w is not None:
                k, v = b.w
                if deps.get(k, 0) < v:
                    deps[k] = v
            for k, v in b.r.items():
                if deps.get(k, 0) < v:
                    deps[k] = v
        return deps

    def _emit_waits(self, e, deps, skip_own=False):
        for k, v in deps.items():
            if skip_own and k == e.sem:
                continue
            if e.waited.get(k, 0) >= v:
                continue
            e.waited[k] = v
            e.prog.append(("wait", self.sems[k], v))
            self.n_wait += 1

    def _record(self, ev, reads, writes):
        k, v = ev
        for b in reads:
            if b.r.get(k, 0) < v:
                b.r[k] = v
        for b in writes:
            b.w = ev
            b.r = {}

    def op(self, eng, fn, reads=(), writes=(), inc=True):
        e = self.engs[eng]
        deps = self._deps(reads, writes)
        self._emit_waits(e, deps, skip_own=(eng == "pe"))
        if inc:
            e.count += 1
            ev = (e.sem, e.count)
            e.prog.append(("op", fn, self.sems[e.sem], 1))
        else:
            ev = (e.sem, e.count + 1)
            e.prog.append(("op", fn, None, 0))
        self._record(ev, reads, writes)
        self.n_op += 1
        return ev

    def dma(self, q, out, in_, reads=(), writes=(), is_output=False):
        e = self.engs[q]
        deps = self._deps(reads, writes)
        names = self.dma_pool[q]
        nm = names[self.dma_rr[q] % len(names)]
        self.dma_rr[q] += 1
        prev = self.dma_val[nm]
        if prev > 0 and deps.get(nm, 0) < prev:
            deps[nm] = prev
        self._emit_waits(e, deps)
        val = prev + 16
        self.dma_val[nm] = val
        ev = (nm, val)

        def fn(engine, out=out, in_=in_):
            return engine.dma_start(out=out, in_=in_)
        e.prog.append(("op", fn, self.sems[nm], 16))
        self._record(ev, reads, writes)
        if is_output:
            self.out_events.append(ev)
        self.n_op += 1
        return ev

    def handoff(self, old, new):
        merged = {}
        for b in old:
            if b.w is not None:
                k, v = b.w
                if merged.get(k, 0) < v:
                    merged[k] = v
            for k, v in b.r.items():
                if merged.get(k, 0) < v:
                    merged[k] = v
        for b in new:
            b.w = None
            b.r = dict(merged)

    def finish(self):
        e = self.engs["sp"]
        deps = {}
        for k, v in self.out_events:
            deps[k] = max(deps.get(k, 0), v)
        for kk in self.COMPUTE:
            ee = self.engs[kk]
            if ee.count > 0:
                deps[ee.sem] = ee.count
        for nm, v in self.dma_val.items():
            if v > 0:
                deps[nm] = max(deps.get(nm, 0), v)
        self._emit_waits(e, deps)
        nc = self.nc
        engs = self.engs

        def replay(engine, prog):
            for item in prog:
                if item[0] == "wait":
                    engine.wait_ge(item[1], item[2])
                else:
                    _, fn, sem, n = item
                    ins = fn(engine)
                    if sem is not None:
                        ins.then_inc(sem, n)

        with nc.Block() as block:
            @block.sync
            def _(eng):
                replay(eng, engs["sp"].prog)

            @block.tensor
            def _(eng):
                replay(eng, engs["pe"].prog)

            @block.scalar
            def _(eng):
                replay(eng, engs["act"].prog)

            @block.vector
            def _(eng):
                replay(eng, engs["dve"].prog)

            @block.gpsimd
            def _(eng):
                replay(eng, engs["pool"].prog)

    def close(self):
        while self._stack:
            cm = self._stack.pop()
            cm.__exit__(None, None, None)


def _wblocks():
    L = [(("ev_in", "a"), "ev_w_in", 0, 0, 512, 8)]
    for g in range(3):
        base = 512 + g * 1536
        for i, nm in enumerate(("q", "k", "v")):
            L.append((("ev_in", g, nm), "ev_w_in", 0, base + 512 * i, 512, 8))
    for j in range(2):
        L.append((("ev_out", j), "ev_w_out", 0, 512 * j, 512, 8))
    names = ["u", "v", "go", "gi", "xi"]
    for j in range(5):
        L.append((("od_in", names[j]), "od_w_in", 0, 512 * j, 512, 8))
    for j in range(2):
        L.append((("od_out", j), "od_w_out", 0, 512 * j, 512, 8))
    for l in range(2):
        for j in range(8):
            L.append((("w1", l, j), "ffn_w1", l, 512 * j, 512, 8))
        for j in range(8):
            L.append((("w2", l, j), "ffn_w2", l, 128 * j, 128, 32))
        L.append((("proj", l), "ple_w_proj", l, 0, 1024, 2))
        for j in range(2):
            L.append((("gate", l, j), "ple_w_gate", l, 512 * j, 512, 8))
    return L


_DBG = {}


class _Stop(Exception):
    pass


PHASES = []


def ck(name):
    if _DBG.get("stop") == name:
        raise _Stop()


def build_program():
    nc = bass.Bass("TRN2", target_bir_lowering=False)
    c = Ctx(nc)

    def din(name, shape):
        return nc.dram_tensor(name, list(shape), F32, kind="ExternalInput").ap()

    def dout(name, shape):
        return nc.dram_tensor(name, list(shape), F32, kind="ExternalOutput").ap()

    WB = _wblocks()
    WIDX = {k[0]: i for i, k in enumerate(WB)}
    d_wall = din("wall", [len(WB), 128, 4096])
    d_wbf = nc.dram_tensor("wbf_scratch", [len(WB), 128, 4096], BF16).ap()
    wconv = set()
    d_pool_w = din("pool_w", [128, 4, 128])
    d_wsT = din("od_wsT", [128, 4, 128])
    d_vecs = din("vecs", [128, 80])
    d_bsT = din("bsT", [128, 4, 512])
    d_bsS = din("bsS", [128, 4, 64])
    d_wsS = din("wsS", [128, 4, 16])
    d_ident = din("ident", [128, 128])
    d_rotm = din("rotm", [128, 128])
    d_tri = din("trimask", [128, 128])
    d_maskSc = din("maskSc", [128, 96])
    d_maskSn = din("maskSn", [128, 3, 512])
    d_xT = din("xT", [1024, NT * TT])
    d_pT = din("pT", [2, 256, 5 * TT])
    d_cs = din("cs", [128, 2, NT * TT + NS])
    d_mask = din("mask", [5, 128, MASKW])
    d_icnt = din("icnt", [5, 128, 4, TT])
    d_xsT = din("xsT", [1024, NS])
    d_psT = din("psT", [2, 256, NS])
    d_kc = din("kc", [16, 128, 9 * 4 * 128])
    d_vc = din("vc", [16, 128, 9 * 512])
    d_spool = din("spool", [128, 960])
    d_sconv = din("sconv", [128, 128])
    o_yT = dout("o_yT", [1024, SEG])
    o_ysT = dout("o_ysT", [1024, NS])
    o_kvT = dout("o_kvT", [3, 2, 512, SEG])
    o_kvsT = dout("o_kvsT", [3, 2, 512, NS])
    o_poolT = dout("o_poolT", [128, 4, 15])
    o_poolsT = dout("o_poolsT", [128, 960])
    o_convT = dout("o_convT", [128, 4, 2])
    o_convsT = dout("o_convsT", [128, 128])
    o_cvsT = dout("o_cvsT", [512, NS])
    o_dbg = dout("o_dbg", [12, 1024, NS]) if _DBG.get("dump") else None

    rT = c.sb("rT", [128, 8, TT], F32)
    hT = c.sb("hT", [128, 8, TT], BF16)
    sq = c.sb("sq", [128, 2, TT], BF16)
    zb = c.sb("zb", [128, 2, TT], BF16)
    scr = c.sb("scr", [128, 3, TT], F32)
    cs = c.sb("cs_s", [128, 2, TT], F32)
    a_ext = c.sb("a_ext", [128, 4, 528], F32)
    pTs = c.sb("pTs_s", [128, 2, TT], BF16)
    msk = c.sb("msk_s", [128, MASKW], BF16)
    NW = 4
    wsl = c.sb("wsl", [128, NW, 4096], BF16)
    ident = c.sb("ident", [128, 128], BF16)
    rotm = c.sb("rotm_s", [128, 128], BF16)
    ones = c.sb("ones", [128, 128], BF16)
    onesn = c.sb("onesn", [128, 128], BF16)
    ones128 = c.sb("ones128", [128, 128], BF16)
    zeros = c.sb("zeros", [128, 128], BF16)
    vecs = c.sb("vecs_s", [128, 80], F32)
    bsT = c.sb("bsT_s", [128, 4, TT], BF16)
    bsS = c.sb("bsS_s", [128, 4, NS], F32)
    wsS = c.sb("wsS_s", [128, 4, 16], F32)
    wmT = c.sb("wmT", [128, 4, 128], BF16)
    wsf = c.sb("wsf", [128, 4, 128], F32)
    trim = c.sb("trim", [128, 128], F32)
    poolw = c.sb("poolw", [128, 4, 128], BF16)
    hd_halo = c.sb("hd_halo", [128, 4, 2], F32)
    maskSc = c.sb("maskSc_s", [128, 96], BF16)
    maskSn = c.sb("maskSn_s", [128, 3, 512], BF16)
    arena = c.sb("arena", [128, 16896], BF16)
    kvr = c.sb("kvr", [128, 36864], BF16)

    mmps = [c.ps("mm0", [128, 512], F32), c.ps("mm1", [128, 512], F32)]
    sps = [c.ps("s0", [128, 512], F32), c.ps("s1", [128, 512], F32)]
    nump = c.ps("nump", [128, 512], F32)
    denp = c.ps("denp", [128, 512], F32)
    auxp = c.ps("auxp", [128, 512], F32)
    trp = c.ps("trp", [128, 1024], BF16)

    del PHASES[:]

    def ph(name):
        PHASES.append((name, sum(1 for it in c.engs["pe"].prog if it[0] == "op")))

    B = {}

    def bf(name):
        if name not in B:
            B[name] = Buf(name)
        return B[name]

    def aview(off, shape, dtype=BF16):
        n = 1
        for s in shape[1:]:
            n *= s
        if dtype == F32:
            ap = arena[:, off:off + 2 * n].bitcast(F32)
        else:
            ap = arena[:, off:off + n]
        if len(shape) == 3:
            ap = ap.rearrange("p (a b) -> p a b", a=shape[1])
        elif len(shape) == 4:
            ap = ap.rearrange("p (a b c) -> p a b c", a=shape[1], b=shape[2])
        return ap

    def kview(off, shape):
        n = 1
        for s in shape[1:]:
            n *= s
        ap = kvr[:, off:off + n]
        if len(shape) == 3:
            ap = ap.rearrange("p (a b) -> p a b", a=shape[1])
        elif len(shape) == 4:
            ap = ap.rearrange("p (a b c) -> p a b c", a=shape[1], b=shape[2])
        return ap

    KT0 = kview(0, [128, 4, 2 * TT])
    KT1 = kview(4096, [128, 4, 2 * TT])
    KT2 = kview(8192, [128, 4, 5 * TT])
    VK0 = kview(18432, [128, 2, 4, 512])
    VK1 = kview(22528, [128, 2, 4, 512])
    V2A = kview(26624, [128, 16, 512])
    V2B = kview(34816, [128, 4, 512])

    wseq = []

    def wblock(key):
        i = WIDX[key]
        _, _, _, _, ncols, kcn = WB[i]
        wseq.append((key, i, kcn, ncols))

    def sched_l0(kind, t):
        if kind == "kv":
            if t == 3:
                wblock(("ev_in", "a"))
                gs = (0, 1, 2)
            else:
                gs = (2,)
            for g in gs:
                wblock(("ev_in", g, "k"))
                wblock(("ev_in", g, "v"))
            return
        wblock(("ev_in", "a"))
        for g in range(3):
            for nm in ("q", "k", "v"):
                wblock(("ev_in", g, nm))
        for j in range(2):
            wblock(("ev_out", j))
        sched_ffn_ple(0)

    def sched_ffn_ple(l):
        for j in range(8):
            wblock(("w1", l, j))
        for j in range(8):
            wblock(("w2", l, j))
        wblock(("proj", l))
        for j in range(2):
            wblock(("gate", l, j))

    def sched_l1(kind):
        names = ["u", "v", "go", "gi", "xi"]
        if kind == "halo":
            for j in (3, 4):
                wblock(("od_in", names[j]))
            return
        for j in range(5):
            wblock(("od_in", names[j]))
        for j in range(2):
            wblock(("od_out", j))
        sched_ffn_ple(1)

    tiles = [("kv", t) for t in range(4)] + [("full", t) for t in range(4, 9)] + [("sample", 9)]
    if _DBG.get("tiles") is not None:
        tiles = [tiles[i] for i in _DBG["tiles"]]
    for kind, t in tiles:
        if kind == "kv":
            sched_l0("kv", t)
        else:
            sched_l0("full", t)
            if kind == "full" and t == 4:
                sched_l1("halo")
            else:
                sched_l1("full")

    wstate = {"issued": 0, "next": 0}

    def w_issue_upto(n):
        while wstate["issued"] < min(n, len(wseq)):
            i = wstate["issued"]
            key, bi, kcn, ncols = wseq[i]
            slot = i % NW
            if bi not in wconv:
                wconv.add(bi)
                c.dma("pool", wsl[:, slot, 0:kcn * ncols], d_wall[bi, :, 0:kcn * ncols], writes=[bf("w%d" % slot)])
                c.dma("sp", d_wbf[bi, :, 0:kcn * ncols], wsl[:, slot, 0:kcn * ncols],
                      reads=[bf("w%d" % slot)], writes=[bf("wbf%d" % bi)])
            else:
                c.dma("sp", wsl[:, slot, 0:kcn * ncols], d_wbf[bi, :, 0:kcn * ncols],
                      reads=[bf("wbf%d" % bi)], writes=[bf("w%d" % slot)])
            wstate["issued"] += 1

    def getw(key, oldest=None):
        i = wstate["next"]
        assert wseq[i][0] == key, (wseq[i][0], key)
        _, bi, kcn, ncols = wseq[i]
        w_issue_upto((i if oldest is None else oldest) + NW - 1)
        slot = i % NW
        wstate["next"] += 1
        view = wsl[:, slot, 0:kcn * ncols].rearrange("p (k n) -> p k n", k=kcn)
        return view, bf("w%d" % slot)

    def w_advance():
        w_issue_upto(wstate["next"] + NW)

    rr = {"mm": 0, "s": 0, "sq": 0, "zb": 0, "scr": 0, "P": 0, "ev": 0, "trp": 0}

    WIDE = [(mmps[0], "mm0"), (mmps[1], "mm1"), (sps[0], "s0"), (sps[1], "s1"), (nump, "num"), (denp, "den")]
    SB4 = [(sps[0], "s0"), (sps[1], "s1"), (mmps[0], "mm0"), (mmps[1], "mm1")]

    def mm_bank():
        i = rr["mm"] % len(WIDE)
        rr["mm"] += 1
        return WIDE[i][0], bf(WIDE[i][1])

    def s_bank():
        i = rr["s"] % len(SB4)
        rr["s"] += 1
        return SB4[i][0], bf(SB4[i][1])

    def scr_buf():
        i = rr["scr"] % 3
        rr["scr"] += 1
        return scr[:, i, :], bf("scr%d" % i)

    def zb_buf():
        i = rr["zb"] % 2
        rr["zb"] += 1
        return zb[:, i, :], bf("zb%d" % i)

    def sq_buf():
        i = rr["sq"] % 2
        rr["sq"] += 1
        return sq[:, i, :], bf("sq%d" % i)

    def trp_half():
        i = rr["trp"] % 2
        rr["trp"] += 1
        return trp[:, 512 * i:512 * i + 512], bf("trp")

    def ev_eng():
        rr["ev"] += 1
        if _DBG.get("evac"):
            return _DBG["evac"]
        return "act" if rr["ev"] % 2 else "dve"

    def copy_op(eng, out, in_, reads, writes):
        if eng == "act":
            c.op("act", lambda e: e.copy(out, in_), reads=reads, writes=writes)
        else:
            c.op(eng, lambda e: e.tensor_copy(out, in_), reads=reads, writes=writes)

    def mm(ps_ap, lhsT, rhs, start, stop, reads, writes, inc):
        c.op("pe", lambda e: e.matmul(ps_ap, lhsT, rhs, start=start, stop=stop),
             reads=reads, writes=writes, inc=inc)

    c.dma("pool", ident[:], d_ident, writes=[bf("const")])
    c.dma("pool", rotm[:], d_rotm, writes=[bf("const")])
    c.dma("pool", poolw[:], d_pool_w, writes=[bf("const")])
    c.dma("pool", bsT[:], d_bsT, writes=[bf("const")])
    c.dma("pool", maskSc[:], d_maskSc, writes=[bf("const")])
    c.dma("pool", maskSn[:], d_maskSn, writes=[bf("const")])
    c.dma("sp", vecs[:], d_vecs, writes=[bf("const")])
    c.dma("sp", bsS[:], d_bsS, writes=[bf("const")])
    c.dma("sp", wsS[:], d_wsS, writes=[bf("const")])
    c.dma("sp", wsf[:], d_wsT, writes=[bf("wsf")])
    c.dma("sp", trim[:], d_tri, writes=[bf("trim")])
    c.op("dve", lambda e: e.memset(ones[:], 1.0), writes=[bf("const")])
    c.op("dve", lambda e: e.memset(onesn[:], 1.0 / 1024.0), writes=[bf("const")])
    c.op("dve", lambda e: e.memset(ones128[:], 1.0 / 128.0), writes=[bf("const")])
    c.op("dve", lambda e: e.memset(zeros[:], 0.0), writes=[bf("const")])
    for g in range(4):
        c.op("dve", lambda e, g=g: e.tensor_tensor(wmT[:, g, :], wsf[:, g, :], trim[:], ALU.mult),
             reads=[bf("wsf"), bf("trim")], writes=[bf("const")])
    c.op("dve", lambda e: e.memset(hd_halo[:], 0.0), writes=[bf("hd_halo")])
    c.op("dve", lambda e: e.memset(a_ext[:], 0.0), writes=[bf("a_ext")])
    CONST = bf("const")
    V_NMIX, V_NFFN, V_NPLE, V_NFIN = (0, 8), (16, 24), (32, 40), 48
    V_PSC, V_LNG, V_LNB, V_CW = 56, 60, 64, 68

    w_issue_upto(NW)

    deferred = []

    def flush_deferred(keep=0, everything=False):
        if everything:
            while deferred:
                deferred.pop(0)()
            return
        for _ in range(max(0, len(deferred) - keep)):
            deferred.pop(0)()

    def dump(i, which, N):
        if o_dbg is None or N != NS:
            return
        dst = o_dbg[i].rearrange("(ch p) n -> p ch n", p=128)
        if which == "h":
            c.dma("pool", dst, hT[:, :, 0:N], reads=[bf("hT%d" % ch) for ch in range(8)], is_output=True)
        else:
            c.dma("sp", dst, rT[:, :, 0:N], reads=[bf("rT%d" % ch) for ch in range(8)], is_output=True)

    def norm_acc(ch, N, defer=True):
        s_ap, s_b = sq_buf()
        c.op("act", lambda e, ch=ch, s_ap=s_ap: e.activation(s_ap[:, 0:N], rT[:, ch, 0:N], AF.Square),
             reads=[bf("rT%d" % ch)], writes=[s_b])
        def later():
            mm(auxp[:, 0:N], onesn[:], s_ap[:, 0:N], ch == 0, ch == 7, [s_b, CONST], [bf("aux")], True)
        if defer:
            deferred.append(later)
        else:
            later()

    def rmsnorm(N, vcol, out_fn=None, pre=False):
        ss = auxp[:, 0:N]
        if not pre:
            for ch in range(8):
                norm_acc(ch, N, defer=False)
        flush_deferred(everything=True)
        r_ap, r_b = scr_buf()
        c.op("act", lambda e: e.activation(r_ap[:, 0:N], ss, AF.Sqrt, bias=EPS, scale=1.0),
             reads=[bf("aux")], writes=[r_b])
        c.op("dve", lambda e: e.reciprocal(r_ap[:, 0:N], r_ap[:, 0:N]), reads=[r_b], writes=[r_b])
        for ch in range(8):
            if out_fn is None:
                if ch % 2 == 0:
                    c.op("dve", lambda e, ch=ch: e.scalar_tensor_tensor(
                        hT[:, ch, 0:N], rT[:, ch, 0:N], vecs[:, vcol + ch:vcol + ch + 1], r_ap[:, 0:N],
                        ALU.mult, ALU.mult), reads=[bf("rT%d" % ch), r_b, CONST], writes=[bf("hT%d" % ch)])
                else:
                    tz, tzb = zb_buf()
                    c.op("pool", lambda e, ch=ch, tz=tz: e.tensor_tensor(tz[:, 0:N], rT[:, ch, 0:N], r_ap[:, 0:N], ALU.mult),
                         reads=[bf("rT%d" % ch), r_b], writes=[tzb])
                    c.op("act", lambda e, ch=ch, tz=tz: e.activation(hT[:, ch, 0:N], tz[:, 0:N], AF.Identity,
                                                                     scale=vecs[:, vcol + ch:vcol + ch + 1]),
                         reads=[tzb, CONST], writes=[bf("hT%d" % ch)])
            else:
                out_fn(ch, r_ap, r_b)

    def linear(keys, rhs_fn, nk, N, evac, keep=0):
        for key in keys:
            wv, wb = getw(key)
            ncols = wv.shape[2]
            for oc in range(ncols // 128):
                ps, pb = mm_bank()
                for kc in range(nk):
                    rhs, rb = rhs_fn(kc)
                    mm(ps[:, 0:N], wv[:, kc, 128 * oc:128 * oc + 128], rhs, kc == 0, kc == nk - 1,
                       [wb, rb], [pb], kc == nk - 1)
                flush_deferred(keep)
                evac(key, oc, ps[:, 0:N], pb)
            w_advance()

    def h_rhs(N):
        return lambda kc: (hT[:, kc, 0:N], bf("hT%d" % kc))

    def zero_acc(ncol):
        mm(nump[:, 0:ncol], zeros[:], bsT[:, 0, 0:ncol], True, False, [CONST], [bf("num")], False)
        mm(denp[:, 0:ncol], zeros[:], bsT[:, 0, 0:ncol], True, False, [CONST], [bf("den")], True)

    def rope(ps, pb, N, out_full=None, out_halves=None, out_bufs=()):
        z_ap, z_b = zb_buf()
        copy_op("act", z_ap[:, 0:N], ps, [pb], [z_b])

        def stage2():
            rp, rpb = mm_bank()
            mm(rp[:, 0:N], rotm[:], z_ap[:, 0:N], True, True, [z_b, CONST], [rpb], True)
            t1, t1b = scr_buf()
            t2, t2b = scr_buf()
            c.op("dve", lambda e: e.tensor_tensor(t1[:, 0:N], z_ap[:, 0:N], cs[:, 0, 0:N], ALU.mult),
                 reads=[z_b, bf("cs")], writes=[t1b])
            c.op("dve", lambda e: e.tensor_tensor(t2[:, 0:N], rp[:, 0:N], cs[:, 1, 0:N], ALU.mult),
                 reads=[rpb, bf("cs")], writes=[t2b])
            if out_full is not None:
                c.op("pool", lambda e: e.tensor_tensor(out_full, t1[:, 0:N], t2[:, 0:N], ALU.add),
                     reads=[t1b, t2b], writes=list(out_bufs))
            else:
                oa, ob = out_halves
                c.op("pool", lambda e: e.tensor_tensor(oa[0:64, :], t1[0:64, 0:N], t2[0:64, 0:N], ALU.add),
                     reads=[t1b, t2b], writes=[out_bufs[0]])
                c.op("pool", lambda e: e.tensor_tensor(ob[64:128, :], t1[64:128, 0:N], t2[64:128, 0:N], ALU.add),
                     reads=[t1b, t2b], writes=[out_bufs[1]])
        deferred.append(stage2)

    def ffn_ple(l, N, p_ap, p_src):
        c.dma("pool", p_ap[:, :, 0:N], p_src.rearrange("(k p) n -> p k n", p=128), writes=[bf("pTs")])
        hid = aview(0, [128, 32, TT])
        hb = [bf("hid%d" % f) for f in range(32)]
        c.handoff(ARENA_BUFS[0], hb)
        ARENA_BUFS[0] = hb
        ph("ffn%d" % l)
        rmsnorm(N, V_NFFN[l], pre=True)

        def ev1(key, oc, ps, pb):
            f = key[2] * 4 + oc
            r_ap, r_b = zb_buf()
            if f % 2 == 0:
                c.op("act", lambda e: e.activation(r_ap[:, 0:N], ps, AF.Relu), reads=[pb], writes=[r_b])
            else:
                c.op("dve", lambda e: e.tensor_scalar(r_ap[:, 0:N], ps, 0.0, None, ALU.max), reads=[pb], writes=[r_b])
            c.op("pool", lambda e: e.tensor_tensor(hid[:, f, 0:N], r_ap[:, 0:N], r_ap[:, 0:N], ALU.mult),
                 reads=[r_b], writes=[hb[f]])
        linear([("w1", l, j) for j in range(8)], h_rhs(N), 8, N, ev1)

        def ev2(key, oc, ps, pb):
            ch = key[2]
            c.op("dve", lambda e: e.tensor_tensor(rT[:, ch, 0:N], rT[:, ch, 0:N], ps, ALU.add),
                 reads=[pb, bf("rT%d" % ch)], writes=[bf("rT%d" % ch)])
            norm_acc(ch, N)
        linear([("w2", l, j) for j in range(8)], lambda kc: (hid[:, kc, 0:N], hb[kc]), 32, N, ev2)
        dump(3 + 4 * l, "r", N)
        ph("ple%d" % l)
        rmsnorm(N, V_NPLE[l], pre=True)
        ip = wstate["next"]
        wp, wpb = getw(("proj", l))
        pend_sq = [None]
        for j in range(2):
            wg, wgb = getw(("gate", l, j), oldest=ip)
            for oc in range(4):
                ch = 4 * j + oc
                psg, pgb = mm_bank()
                for kc in range(8):
                    mm(psg[:, 0:N], wg[:, kc, 128 * oc:128 * oc + 128], hT[:, kc, 0:N], kc == 0, kc == 7,
                       [wgb, bf("hT%d" % kc)], [pgb], kc == 7)
                psp, ppb = mm_bank()
                for kc in range(2):
                    mm(psp[:, 0:N], wp[:, kc, 128 * ch:128 * ch + 128], p_ap[:, kc, 0:N], kc == 0, kc == 1,
                       [wpb, bf("pTs")], [ppb], kc == 1)
                flush_deferred()
                g_ap, g_b = scr_buf()
                c.op("act", lambda e, g_ap=g_ap, psg=psg: e.activation(g_ap[:, 0:N], psg[:, 0:N], AF.Sigmoid),
                     reads=[pgb], writes=[g_b])
                if pend_sq[0] is not None:
                    norm_acc(pend_sq[0], N)
                t_ap, t_b = scr_buf()
                c.op("dve", lambda e, g_ap=g_ap, t_ap=t_ap, psp=psp: e.tensor_tensor(
                    t_ap[:, 0:N], g_ap[:, 0:N], psp[:, 0:N], ALU.mult), reads=[g_b, ppb], writes=[t_b])
                c.op("pool", lambda e, t_ap=t_ap, ch=ch: e.tensor_tensor(
                    rT[:, ch, 0:N], rT[:, ch, 0:N], t_ap[:, 0:N], ALU.add),
                    reads=[t_b, bf("rT%d" % ch)], writes=[bf("rT%d" % ch)])
                pend_sq[0] = ch
        norm_acc(pend_sq[0], N)
        w_advance()
        dump(4 + 4 * l, "r", N)

    ARENA_BUFS = [[bf("arena_init")]]
    RT = [bf("rT%d" % ch) for ch in range(8)]
    cur = {"i": 0}
    xloaded = set()

    def x_load_chunk(i, ch):
        if (i, ch) in xloaded:
            return
        xloaded.add((i, ch))
        kind_, t_ = tiles[i]
        if kind_ == "sample":
            c.dma("sp", rT[:, ch, 0:NS], d_xsT[128 * ch:128 * ch + 128, :], writes=[RT[ch]])
        else:
            c.dma("sp", rT[:, ch, :], d_xT[128 * ch:128 * ch + 128, t_ * TT:(t_ + 1) * TT], writes=[RT[ch]])

    def x_prefetch_next(ch=None):
        i = cur["i"] + 1
        if i >= len(tiles):
            return
        for k in ([ch] if ch is not None else range(8)):
            x_load_chunk(i, k)
    KV_BUFS = [bf("kv_KT0"), bf("kv_KT1"), bf("kv_KT2r"), bf("kv_KT2c"), bf("kv_VK0"), bf("kv_VK1"),
               bf("kv_V2A"), bf("kv_V2B")]

    A_QA, A_QB = 0, 6144
    A_VT = 12288
    A_P = 14336
    A_RD = 15872

    def layer0_prompt(kind, t):
        N = TT
        full = kind == "full"
        main = full and t >= 5
        cur, prev = t % 2, (t + 1) % 2
        tok0 = (t - 5) * TT
        QA = aview(A_QA, [128, 3, 4, TT])
        QB = aview(A_QB, [128, 3, 4, TT])
        VT = aview(A_VT, [128, 4, TT])
        Pb = aview(A_P, [128, 3, TT])
        rden = aview(A_RD, [128, TT], F32)
        ab = {n: bf("ar_" + n) for n in ["QA", "QB", "VT", "P0", "P1", "P2", "rden"]}
        c.handoff(ARENA_BUFS[0], list(ab.values()))
        ARENA_BUFS[0] = list(ab.values())
        if full:
            c.op("dve", lambda e: e.memset(arena[64:128, 0:6144], 0.0), writes=[ab["QA"]])
            c.op("dve", lambda e: e.memset(arena[0:64, 6144:12288], 0.0), writes=[ab["QB"]])
        ph("L0norm %s%d" % (kind, t))
        ck("pre")
        rmsnorm(N, V_NMIX[0])
        ck("norm")
        if not full:
            x_prefetch_next()
        ph("L0proj")
        gs = (0, 1, 2) if (full or t == 3) else (2,)
        KTs = [KT0, KT1, KT2]
        kbufs = [bf("kv_KT0"), bf("kv_KT1"), bf("kv_KT2c")]

        def kslot(g, ch):
            if g == 2:
                return KT2[:, ch, 4 * TT:5 * TT]
            return KTs[g][:, ch, cur * TT:(cur + 1) * TT]

        if full or t == 3:
            def ev_a(key, oc, ps, pb):
                copy_op("act", a_ext[:, oc, 15:15 + N], ps, [pb], [bf("a_ext")])
            linear([("ev_in", "a")], h_rhs(N), 8, N, ev_a)
        for g in gs:
            if full:
                def ev_q(key, oc, ps, pb, g=g):
                    rope(ps, pb, N, out_halves=(QA[:, g, oc, :], QB[:, g, oc, :]), out_bufs=(ab["QA"], ab["QB"]))
                linear([("ev_in", g, "q")], h_rhs(N), 8, N, ev_q)

            def ev_k(key, oc, ps, pb, g=g):
                rope(ps, pb, N, out_full=kslot(g, oc), out_bufs=(kbufs[g],))
            linear([("ev_in", g, "k")], h_rhs(N), 8, N, ev_k)
            ck("k")

            def ev_v(key, oc, ps, pb):
                copy_op(ev_eng(), VT[:, oc, :], ps, [pb], [ab["VT"]])
            linear([("ev_in", g, "v")], h_rhs(N), 8, N, ev_v)
            flush_deferred(everything=True)
            ck("v")
            if main:
                ko = o_kvT[g, 0, :, tok0:tok0 + TT].rearrange("(ch p) n -> p ch n", p=128)
                ksrc = KT2[:, :, 4 * TT:5 * TT] if g == 2 else KTs[g][:, :, cur * TT:(cur + 1) * TT]
                c.dma("pool", ko, ksrc, reads=[kbufs[g]], is_output=True)
                vo = o_kvT[g, 1, :, tok0:tok0 + TT].rearrange("(ch p) n -> p ch n", p=128)
                c.dma("pool", vo, VT[:, :, :], reads=[ab["VT"]], is_output=True)
            for b in range(4):
                th, thb = trp_half()
                for ch in range(4):
                    src = VT[:, ch, 128 * b:128 * b + 128] if g == 0 else VT[:, ch, b:TT:4]
                    c.op("pe", lambda e, th=th, ch=ch, src=src: e.transpose(th[:, 128 * ch:128 * ch + 128], src, ident[:]),
                         reads=[ab["VT"], CONST], writes=[thb], inc=(ch == 3))
                ck("trb%d" % b)
                if g == 0:
                    copy_op(ev_eng(), VK0[:, cur, b, :], th, [thb], [bf("kv_VK0")])
                elif g == 1:
                    copy_op(ev_eng(), VK1[:, cur, b, :], th, [thb], [bf("kv_VK1")])
                else:
                    copy_op(ev_eng(), V2B[:, b, :], th, [thb], [bf("kv_V2B")])
                ck("evb%d" % b)
        ph("L0att")
        ck("tr")
        if full:
            attention_prompt(t, QA, QB, Pb, rden, ab)
        ph("L0ring")
        ck("att")
        s = t % 4
        c.op("act", lambda e: e.copy(KT2[:, :, s * TT:(s + 1) * TT], KT2[:, :, 4 * TT:5 * TT]),
             reads=[bf("kv_KT2c")], writes=[bf("kv_KT2r")])
        for j in range(4):
            c.dma("sp", V2A[32 * s:32 * s + 32, 4 * j:4 * j + 4, :], V2B[j:128:4, :, :],
                  reads=[bf("kv_V2B")], writes=[bf("kv_V2A")])
        ck("ring")
        if not full:
            if t == 3:
                c.op("dve", lambda e: e.tensor_copy(a_ext[:, :, 0:15], a_ext[:, :, TT:TT + 15]),
                     reads=[bf("a_ext")], writes=[bf("a_ext")])
            return
        ph("L0pool")
        pool_mixer_prompt(t)
        ph("L0wout")
        if t == 4:
            HB = [bf("hT%d" % ch) for ch in range(8)]
            c.op("dve", lambda e: e.tensor_copy(hT[:, :, 0:2], hT[:, :, TT - 2:TT]), reads=HB, writes=HB)
            c.op("dve", lambda e: e.tensor_copy(rT[:, :, 0:2], rT[:, :, TT - 2:TT]), reads=RT, writes=RT)
            N = 2
        def ev_o(key, oc, ps, pb):
            ch = key[1] * 4 + oc
            c.op("dve", lambda e: e.tensor_tensor(rT[:, ch, 0:N], rT[:, ch, 0:N], ps, ALU.add),
                 reads=[pb, bf("rT%d" % ch)], writes=[bf("rT%d" % ch)])
            norm_acc(ch, N)
        linear([("ev_out", j) for j in range(2)], h_rhs(N), 8, N, ev_o, keep=1)
        if t == 4:
            ffn_ple(0, N, pTs, d_pT[0, :, TT - 2:TT])
        else:
            ffn_ple(0, N, pTs, d_pT[0, :, (t - 4) * TT:(t - 3) * TT])

    def attention_prompt(t, QA, QB, Pb, rden, ab):
        cur, prev = t % 2, (t + 1) % 2
        pbufs = [ab["P0"], ab["P1"], ab["P2"]]
        kb0, kb1, kb2r, kb2c = bf("kv_KT0"), bf("kv_KT1"), bf("kv_KT2r"), bf("kv_KT2c")
        vb0, vb1, vb2a, vb2b = bf("kv_VK0"), bf("kv_VK1"), bf("kv_V2A"), bf("kv_V2B")
        for ch in range(4):
            zero_acc(TT)
            all_units = []
            for hh in range(2):
                Q = QA if hh == 0 else QB
                qb_ = ab["QA"] if hh == 0 else ab["QB"]
                po = 64 * hh
                fo = 128 * ch + 64 * hh
                units = []
                for half in range(2):
                    items = []
                    for qi in range(2):
                        qb = 2 * half + qi
                        q_ap = Q[:, 0, ch, 128 * qb:128 * qb + 128]
                        if qb == 0:
                            kp = KT0[:, ch, prev * TT + 384:prev * TT + 512]
                            vp = VK0[:, prev, 3, fo:fo + 64]
                        else:
                            kp = KT0[:, ch, cur * TT + 128 * (qb - 1):cur * TT + 128 * qb]
                            vp = VK0[:, cur, qb - 1, fo:fo + 64]
                        kc_ = KT0[:, ch, cur * TT + 128 * qb:cur * TT + 128 * qb + 128]
                        vc_ = VK0[:, cur, qb, fo:fo + 64]
                        oc_ = (128 * qb, 128 * qb + 128, 1)
                        items.append((kp, kb0, q_ap, (2 * qi) * 128, 128, vp, vb0, oc_, False))
                        items.append((kc_, kb0, q_ap, (2 * qi + 1) * 128, 128, vc_, vb0, oc_, False))
                    units.append((items, half * 512))
                for half in range(2):
                    items = []
                    for qi in range(2):
                        r4 = 2 * half + qi
                        q_ap = Q[:, 1, ch, r4:TT:4]
                        kp = KT1[:, ch, prev * TT + r4:prev * TT + TT:4]
                        kc_ = KT1[:, ch, cur * TT + r4:cur * TT + TT:4]
                        vp = VK1[:, prev, r4, fo:fo + 64]
                        vc_ = VK1[:, cur, r4, fo:fo + 64]
                        oc_ = (r4, TT, 4)
                        items.append((kp, kb1, q_ap, (2 * qi) * 128, 128, vp, vb1, oc_, False))
                        items.append((kc_, kb1, q_ap, (2 * qi + 1) * 128, 128, vc_, vb1, oc_, False))
                    units.append((items, 1024 + half * 512))
                items = []
                for r in range(16):
                    items.append((KT2[:, ch, r:4 * TT:16], kb2r, Q[:, 2, ch, r:TT:16], 32 * r, 32,
                                  V2A[:, r, fo:fo + 64], vb2a, (r, TT, 16), False))
                units.append((items, 2048))
                items = []
                for r4 in range(4):
                    for rho in range(4):
                        r = r4 + 4 * rho
                        items.append((KT2[:, ch, 4 * TT + r4:5 * TT:4], kb2c, Q[:, 2, ch, r:TT:16],
                                      (4 * r4 + rho) * 32, 32, V2B[:, r4, fo:fo + 64], vb2b, (r, TT, 16), False))
                units.append((items, 2560))
                if t == 4:
                    def need(it):
                        o0, o1, os_ = it[7]
                        cols = range(o0, o1, os_)
                        return 510 in cols or 511 in cols
                    units = [([it for it in items if need(it)], mcol) for items, mcol in units]
                    units = [u for u in units if u[0]]
                for ui, (items, mcol) in enumerate(units):
                    all_units.append((items, mcol, qb_, po, ui == len(units) - 1 and hh == 1))

            def stage_s(u):
                items, mcol, qb_, po, last_unit = u
                sp_, sb_ = s_bank()
                for ii, (k_ap, kb_, q_ap, scol, ncol, v_ap, vb_, oc_, st) in enumerate(items):
                    mm(sp_[:, scol:scol + ncol], k_ap, q_ap, True, True, [kb_, qb_], [sb_], ii == len(items) - 1)
                pi = rr["P"] % 3
                rr["P"] += 1
                P = Pb[:, pi, :]
                c.op("act", lambda e, P=P, sp_=sp_: e.activation(P, sp_[:, :], AF.Exp, scale=0.125),
                     reads=[sb_], writes=[pbufs[pi]])
                c.op("dve", lambda e, P=P, mcol=mcol: e.tensor_tensor(P, P, msk[:, mcol:mcol + 512], ALU.mult),
                     reads=[pbufs[pi], bf("msk")], writes=[pbufs[pi]])
                return (P, pi)

            def stage_pv(u, pp):
                items, mcol, qb_, po, last_unit = u
                P, pi = pp
                for ii, (k_ap, kb_, q_ap, scol, ncol, v_ap, vb_, oc_, st) in enumerate(items):
                    lastmm = last_unit and ii == len(items) - 1
                    o0, o1, os_ = oc_
                    mm(nump[po:po + 64, o0:o1:os_], v_ap, P[:, scol:scol + ncol], False, lastmm,
                       [vb_, pbufs[pi]], [bf("num")], False)
                    mm(denp[po:po + 64, o0:o1:os_], ones[:, 0:64], P[:, scol:scol + ncol], False, lastmm,
                       [CONST, pbufs[pi]], [bf("den")], ii == len(items) - 1)

            pend = None
            for u in all_units:
                pp = stage_s(u)
                if pend is not None:
                    stage_pv(*pend)
                pend = (u, pp)
            stage_pv(*pend)
            c.op("dve", lambda e: e.reciprocal(rden[:, :], denp[:, :]), reads=[bf("den")], writes=[ab["rden"]])
            c.op("dve", lambda e, ch=ch: e.tensor_tensor(hT[:, 4 + ch, :], nump[:, :], rden[:, :], ALU.mult),
                 reads=[bf("num"), ab["rden"]], writes=[bf("hT%d" % (4 + ch))])

    def pool_mixer_prompt(t):
        N = TT
        W = 15 + N
        pa = aview(0, [128, 4, 528], F32)
        pb_ = aview(4224, [128, 4, 528], F32)
        icn = aview(8448, [128, 4, TT], BF16)
        pld = aview(10496, [128, 4, TT], BF16)
        nb = {n: bf("pl_" + n) for n in ["pa", "pb", "icn", "pld"]}
        c.handoff(ARENA_BUFS[0], list(nb.values()))
        ARENA_BUFS[0] = list(nb.values())
        c.dma("pool", icn[:, :, :], d_icnt[t - 4], writes=[nb["icn"]])
        for g in range(4):
            src, srcb = a_ext[:, g, :], bf("a_ext")
            off = 0
            bufs = [(pa[:, g, :], nb["pa"]), (pb_[:, g, :], nb["pb"])]
            for k in range(g + 1):
                sh = 1 << k
                dst, dstb = bufs[k % 2]
                eng = "dve" if (g + k) % 2 == 0 else "pool"
                c.op(eng, lambda e, dst=dst, src=src, off=off, sh=sh: e.tensor_tensor(
                    dst[:, off + sh:W], src[:, off + sh:W], src[:, off:W - sh], ALU.add),
                    reads=[srcb], writes=[dstb])
                src, srcb = dst, dstb
                off += sh
            tmp, tmpb = bufs[(g + 1) % 2]
            c.op("dve", lambda e, tmp=tmp, src=src, g=g: e.tensor_tensor(
                tmp[:, 15:W], src[:, 15:W], icn[:, g, :], ALU.mult), reads=[srcb, nb["icn"]], writes=[tmpb])
            c.op("pool", lambda e, tmp=tmp, g=g: e.tensor_tensor(
                pld[:, g, :], tmp[:, 15:W], a_ext[:, g, 15:W], ALU.subtract),
                reads=[tmpb, bf("a_ext")], writes=[nb["pld"]])
            ps, pb2 = mm_bank()
            mm(ps[:, 0:N], poolw[:, g, :], pld[:, g, :], True, True, [CONST, nb["pld"]], [pb2], True)
            c.op("act", lambda e, g=g, ps=ps: e.activation(hT[:, g, 0:N], ps[:, 0:N], AF.Identity,
                                                           scale=vecs[:, V_PSC + g:V_PSC + g + 1]),
                 reads=[pb2, CONST], writes=[bf("hT%d" % g)])
        if t == 8:
            c.dma("sp", o_poolT, a_ext[:, :, TT:TT + 15], reads=[bf("a_ext")], is_output=True)
        c.op("dve", lambda e: e.tensor_copy(a_ext[:, :, 0:15], a_ext[:, :, TT:TT + 15]),
             reads=[bf("a_ext")], writes=[bf("a_ext")])

    L1_U, L1_VN, L1_VTK, L1_GO, L1_GI, L1_HD = 0, 2048, 4096, 6144, 8192, 10240

    def layer1(kind, t, N):
        sample = kind == "sample"
        halo = kind == "halo"
        uT = aview(L1_U, [128, 4, TT])
        vnT = aview(L1_VN, [128, 4, TT])
        vtk = aview(L1_VTK, [128, 4, TT])
        go = aview(L1_GO, [128, 4, TT])
        gi = aview(L1_GI, [128, 4, TT])
        nb = {n: bf("l1_" + n) for n in ["u", "vn", "vtk", "go", "gi", "hd"]}
        c.handoff(ARENA_BUFS[0], list(nb.values()))
        ARENA_BUFS[0] = list(nb.values())
        if sample:
            hd = aview(L1_HD, [128, 4, 16, 6], F32)
            c.dma("sp", scr[:, 2, 0:128], d_sconv, writes=[bf("scr2")])
            c.op("dve", lambda e: e.tensor_copy(hd[:, :, :, 0:2], scr[:, 2, 0:128].rearrange("p (a b c) -> p a b c", a=4, b=16)),
                 reads=[bf("scr2")], writes=[nb["hd"]])
        else:
            hd = aview(L1_HD, [128, 4, 516], F32)
        ph("L1norm %s" % kind)
        rmsnorm(N, V_NMIX[1], pre=True)
        ph("L1proj")
        dump(9, "h", N)
        if not halo:
            def ev_u(key, oc, ps, pb):
                c.op("act", lambda e: e.activation(uT[:, oc, 0:N], ps, AF.Gelu), reads=[pb], writes=[nb["u"]])
            linear([("od_in", "u")], h_rhs(N), 8, N, ev_u)

            def ev_v(key, oc, ps, pb):
                z_ap, z_b = zb_buf()
                c.op("act", lambda e: e.activation(z_ap[:, 0:N], ps, AF.Gelu), reads=[pb], writes=[z_b])

                def stage_b():
                    m_ps, m_b = mm_bank()
                    mm(m_ps[:, 0:N], ones128[:], z_ap[:, 0:N], True, True, [z_b, CONST], [m_b], True)
                    vc, vcb = scr_buf()
                    c.op("dve", lambda e: e.tensor_tensor(vc[:, 0:N], z_ap[:, 0:N], m_ps[:, 0:N], ALU.subtract),
                         reads=[z_b, m_b], writes=[vcb])
                    s_ap, s_b = sq_buf()
                    c.op("act", lambda e: e.activation(s_ap[:, 0:N], vc[:, 0:N], AF.Square), reads=[vcb], writes=[s_b])

                    def stage_c():
                        v_ps, v_b = mm_bank()
                        mm(v_ps[:, 0:N], ones128[:], s_ap[:, 0:N], True, True, [s_b, CONST], [v_b], True)
                        sd, sdb = scr_buf()
                        c.op("act", lambda e: e.activation(sd[:, 0:N], v_ps[:, 0:N], AF.Sqrt, bias=EPS, scale=1.0),
                             reads=[v_b], writes=[sdb])
                        c.op("dve", lambda e: e.reciprocal(sd[:, 0:N], sd[:, 0:N]), reads=[sdb], writes=[sdb])
                        c.op("dve", lambda e: e.tensor_tensor(vc[:, 0:N], vc[:, 0:N], sd[:, 0:N], ALU.mult),
                             reads=[vcb, sdb], writes=[vcb])
                        c.op("act", lambda e: e.activation(vnT[:, oc, 0:N], vc[:, 0:N], AF.Identity,
                                                           bias=vecs[:, V_LNB + oc:V_LNB + oc + 1],
                                                           scale=vecs[:, V_LNG + oc:V_LNG + oc + 1]),
                             reads=[vcb, CONST], writes=[nb["vn"]])
                    deferred.append(stage_c)
                deferred.append(stage_b)
            linear([("od_in", "v")], h_rhs(N), 8, N, ev_v)
            def ev_go(key, oc, ps, pb):
                copy_op("act", go[:, oc, 0:N], ps, [pb], [nb["go"]])
            linear([("od_in", "go")], h_rhs(N), 8, N, ev_go)

        def ev_gi(key, oc, ps, pb):
            copy_op("act", gi[:, oc, 0:N], ps, [pb], [nb["gi"]])
        linear([("od_in", "gi")], h_rhs(N), 8, N, ev_gi)
        if not sample:
            c.op("pool", lambda e: e.tensor_copy(hd[:, :, 0:2], hd_halo[:, :, :]), reads=[bf("hd_halo")], writes=[nb["hd"]])

        def ev_xi(key, oc, ps, pb):
            if sample:
                c.op("dve", lambda e: e.tensor_tensor(hd[:, oc, :, 2:6], ps.rearrange("p (n i) -> p n i", i=4),
                                                      gi[:, oc, 0:N].rearrange("p (n i) -> p n i", i=4), ALU.mult),
                     reads=[pb, nb["gi"]], writes=[nb["hd"]])
            else:
                c.op("dve", lambda e: e.tensor_tensor(hd[:, oc, 2:2 + N], ps, gi[:, oc, 0:N], ALU.mult),
                     reads=[pb, nb["gi"]], writes=[nb["hd"]])
        linear([("od_in", "xi")], h_rhs(N), 8, N, ev_xi)
        if not sample:
            c.op("pool", lambda e: e.tensor_copy(hd_halo[:, :, :], hd[:, :, N:N + 2]), reads=[nb["hd"]], writes=[bf("hd_halo")])
            if t == 8:
                c.dma("sp", o_convT, hd[:, :, N:N + 2], reads=[nb["hd"]], is_output=True)
        else:
            c.op("dve", lambda e: e.tensor_copy(scr[:, 2, 0:128].rearrange("p (a b c) -> p a b c", a=4, b=16), hd[:, :, :, 4:6]),
                 reads=[nb["hd"]], writes=[bf("scr2")])
            c.dma("sp", o_convsT, scr[:, 2, 0:128], reads=[bf("scr2")], is_output=True)
        if halo:
            return
        ph("L1gate")
        flush_deferred(everything=True)
        if sample:
            c.dma("pool", o_cvsT.rearrange("(ch p) n -> p ch n", p=128), vnT[:, :, 0:N],
                  reads=[nb["vn"]], is_output=True)
        for g in range(4):
            tmp, tmpb = scr_buf()
            if not sample:
                th, thb = trp_half()
                for blk in range(4):
                    c.op("pe", lambda e, th=th, blk=blk, g=g: e.transpose(
                        th[:, 128 * blk:128 * blk + 128], vnT[:, g, 128 * blk:128 * blk + 128], ident[:]),
                        reads=[nb["vn"], CONST], writes=[thb], inc=(blk == 3))
                copy_op(ev_eng(), vtk[:, g, :], th, [thb], [nb["vtk"]])
                ps, pb = mm_bank()
                for blk in range(4):
                    mm(ps[:, 128 * blk:128 * blk + 128], vtk[:, g, 128 * blk:128 * blk + 128], wmT[:, g, :],
                       True, True, [nb["vtk"], CONST], [pb], blk == 3)
                c.op("dve", lambda e, tmp=tmp, ps=ps, g=g: e.tensor_tensor(tmp[:, 0:N], ps[:, 0:N], bsT[:, g, :], ALU.add),
                     reads=[pb, CONST], writes=[tmpb])
            else:
                vv = vnT[:, g, 0:N].rearrange("p (n i) -> p n i", i=4)
                tv = tmp[:, 0:N].rearrange("p (n i) -> p n i", i=4)
                for ti in range(4):
                    c.op("act", lambda e, ti=ti, g=g, tv=tv, vv=vv: e.activation(
                        tv[:, :, ti], vv[:, :, 0], AF.Identity, scale=wsS[:, g, 4 * ti:4 * ti + 1]),
                        reads=[nb["vn"], CONST], writes=[tmpb])
                    for si in range(1, ti + 1):
                        c.op("dve", lambda e, ti=ti, si=si, g=g, tv=tv, vv=vv: e.scalar_tensor_tensor(
                            tv[:, :, ti], vv[:, :, si], wsS[:, g, 4 * ti + si:4 * ti + si + 1], tv[:, :, ti],
                            ALU.mult, ALU.add), reads=[nb["vn"], CONST, tmpb], writes=[tmpb])
                c.op("dve", lambda e, tmp=tmp, g=g: e.tensor_tensor(tmp[:, 0:N], tmp[:, 0:N], bsS[:, g, :], ALU.add),
                     reads=[tmpb, CONST], writes=[tmpb])
            c.op("dve", lambda e, tmp=tmp, g=g: e.tensor_tensor(hT[:, g, 0:N], tmp[:, 0:N], uT[:, g, 0:N], ALU.mult),
                 reads=[tmpb, nb["u"]], writes=[bf("hT%d" % g)])

        ph("L1conv")
        for ch in range(4):
            acc, accb = scr_buf()
            cw = lambda j, ch=ch: vecs[:, V_CW + 3 * ch + j:V_CW + 3 * ch + j + 1]
            if sample:
                av = acc[:, 0:N].rearrange("p (n i) -> p n i", i=4)
                hv = lambda j, ch=ch: hd[:, ch, :, j:j + 4]
            else:
                av = acc[:, 0:N]
                hv = lambda j, ch=ch: hd[:, ch, j:j + N]
            c.op("act", lambda e, av=av, hv=hv, cw=cw: e.activation(av, hv(0), AF.Identity, scale=cw(0)),
                 reads=[nb["hd"], CONST], writes=[accb])
            for j in (1, 2):
                c.op("dve", lambda e, av=av, hv=hv, cw=cw, j=j: e.scalar_tensor_tensor(
                    av, hv(j), cw(j), av, ALU.mult, ALU.add), reads=[nb["hd"], CONST, accb], writes=[accb])
            c.op("dve", lambda e, acc=acc, ch=ch: e.tensor_tensor(hT[:, 4 + ch, 0:N], go[:, ch, 0:N], acc[:, 0:N], ALU.mult),
                 reads=[accb, nb["go"]], writes=[bf("hT%d" % (4 + ch))])

        def ev_o(key, oc, ps, pb):
            ch = key[1] * 4 + oc
            c.op("dve", lambda e: e.tensor_tensor(rT[:, ch, 0:N], rT[:, ch, 0:N], ps, ALU.add),
                 reads=[pb, bf("rT%d" % ch)], writes=[bf("rT%d" % ch)])
            norm_acc(ch, N)
        dump(5, "h", N)
        ph("L1wout")
        linear([("od_out", j) for j in range(2)], h_rhs(N), 8, N, ev_o, keep=1)
        dump(6, "r", N)
        ffn_ple(1, N, pTs, d_psT[1] if sample else d_pT[1, :, (t - 4) * TT:(t - 3) * TT])

    def final_out(N, o_ap, col0):
        def out_fn(ch, r_ap, r_b):
            ridx = (rr["scr"] - 1) % 3 if ch == 0 else out_fn.ridx
            out_fn.ridx = ridx
            yi = (ridx + 1 + (ch % 2)) % 3
            y, yb = scr[:, yi, :], bf("scr%d" % yi)
            c.op("dve", lambda e: e.scalar_tensor_tensor(y[:, 0:N], rT[:, ch, 0:N], vecs[:, V_NFIN + ch:V_NFIN + ch + 1],
                                                         r_ap[:, 0:N], ALU.mult, ALU.mult),
                 reads=[bf("rT%d" % ch), r_b, CONST], writes=[yb])
            c.dma("sp", o_ap[128 * ch:128 * ch + 128, col0:col0 + N], y[:, 0:N], reads=[yb], is_output=True)
            x_prefetch_next(ch)
        ph("final")
        rmsnorm(N, V_NFIN, out_fn=out_fn, pre=True)

    def layer0_sample():
        N = NS
        QA = aview(A_QA, [128, 3, 4, TT])
        QB = aview(A_QB, [128, 3, 4, TT])
        VT = aview(A_VT, [128, 4, TT])
        Pb = aview(A_P, [128, 3, TT])
        rden = aview(A_RD, [128, TT], F32)
        ab = {n: bf("ar_" + n) for n in ["QA", "QB", "VT", "P0", "P1", "P2", "rden"]}
        c.handoff(ARENA_BUFS[0], list(ab.values()))
        ARENA_BUFS[0] = list(ab.values())
        kcb = [kview(0, [128, 9, 4, 128]), kview(4608, [128, 9, 4, 128])]
        vcb = [kview(9216, [128, 9, 512]), kview(13824, [128, 9, 512])]
        KTs = kview(18432, [128, 3, 4, NS])
        VsT = kview(19200, [128, 3, 512])
        Pn = kview(20736, [128, 512])
        a_s = kvr[:, 21248:21248 + 2 * 4 * 16 * 19].bitcast(F32).rearrange("p (a b c) -> p a b c", a=4, b=16)
        p1 = kvr[:, 23680:23680 + 2 * 16 * 19].bitcast(F32).rearrange("p (b c) -> p b c", b=16)
        p2 = kvr[:, 24288:24288 + 2 * 16 * 19].bitcast(F32).rearrange("p (b c) -> p b c", b=16)
        plds = kview(24896, [128, 4, NS])
        sb_ = {n: bf("sk_" + n) for n in ["kc0", "kc1", "vc0", "vc1", "KTs", "VsT", "Pn", "a_s", "p1", "p2", "pld"]}
        c.handoff(KV_BUFS, list(sb_.values()))
        c.op("dve", lambda e: e.memset(arena[64:128, 0:6144], 0.0), writes=[ab["QA"]])
        c.op("dve", lambda e: e.memset(arena[0:64, 6144:12288], 0.0), writes=[ab["QB"]])
        c.op("pool", lambda e: e.memset(VsT[:, :, :], 0.0), writes=[sb_["VsT"]])
        c.op("pool", lambda e: e.memset(Pn[:, :], 0.0), writes=[sb_["Pn"]])
        stg = scr[:, 0:2, :].rearrange("p a b -> p (a b)")
        c.dma("sp", stg[:, 0:960], d_spool, writes=[bf("scr0"), bf("scr1")])
        c.op("dve", lambda e: e.tensor_copy(a_s[:, :, :, 0:15], stg[:, 0:960].rearrange("p (a b c) -> p a b c", a=4, b=16)),
             reads=[bf("scr0"), bf("scr1")], writes=[sb_["a_s"]])
        ph("S L0norm")
        rmsnorm(N, V_NMIX[0])
        dump(0, "h", N)
        ph("S L0proj")

        def ev_a(key, oc, ps, pb):
            c.op("act", lambda e: e.copy(a_s[:, oc, :, 15:19], ps.rearrange("p (n i) -> p n i", i=4)),
                 reads=[pb], writes=[sb_["a_s"]])
        linear([("ev_in", "a")], h_rhs(N), 8, N, ev_a)
        for g in range(3):
            def ev_q(key, oc, ps, pb, g=g):
                rope(ps, pb, N, out_halves=(QA[:, g, oc, 0:N], QB[:, g, oc, 0:N]), out_bufs=(ab["QA"], ab["QB"]))
            linear([("ev_in", g, "q")], h_rhs(N), 8, N, ev_q)

            def ev_k(key, oc, ps, pb, g=g):
                rope(ps, pb, N, out_full=KTs[:, g, oc, :], out_bufs=(sb_["KTs"],))
            linear([("ev_in", g, "k")], h_rhs(N), 8, N, ev_k)

            def ev_v(key, oc, ps, pb):
                copy_op(ev_eng(), VT[:, oc, 0:N], ps, [pb], [ab["VT"]])
            linear([("ev_in", g, "v")], h_rhs(N), 8, N, ev_v)
            flush_deferred(everything=True)
            c.dma("pool", o_kvsT[g, 0].rearrange("(ch p) n -> p ch n", p=128), KTs[:, g, :, :],
                  reads=[sb_["KTs"]], is_output=True)
            c.dma("pool", o_kvsT[g, 1].rearrange("(ch p) n -> p ch n", p=128), VT[:, :, 0:N],
                  reads=[ab["VT"]], is_output=True)
            th, thb = trp_half()
            for ch in range(4):
                c.op("pe", lambda e, th=th, ch=ch: e.transpose(th[0:N, 128 * ch:128 * ch + 128], VT[:, ch, 0:N], ident[:]),
                     reads=[ab["VT"], CONST], writes=[thb], inc=(ch == 3))
            copy_op(ev_eng(), VsT[0:N, g, :], th[0:N, :], [thb], [sb_["VsT"]])
        ph("S att new")
        pbufs = [ab["P0"], ab["P1"], ab["P2"]]
        zero_acc(256)
        for g in range(3):
            sp_, sbk = s_bank()
            for h in range(8):
                ch, hh = h // 2, h % 2
                Q = QA if hh == 0 else QB
                mm(sp_[0:N, 64 * h:64 * h + 64], KTs[:, g, ch, :], Q[:, g, ch, 0:N], True, True,
                   [sb_["KTs"], ab["QA"], ab["QB"]], [sbk], h == 7)
            c.op("act", lambda e, sp_=sp_: e.activation(Pn[0:N, :], sp_[0:N, :], AF.Exp, scale=0.125),
                 reads=[sbk], writes=[sb_["Pn"]])
            c.op("dve", lambda e, g=g: e.tensor_tensor(Pn[0:N, :], Pn[0:N, :], maskSn[0:N, g, :], ALU.mult),
                 reads=[sb_["Pn"], CONST], writes=[sb_["Pn"]])
            for h in range(8):
                ch, hh = h // 2, h % 2
                po = 64 * hh
                mm(nump[po:po + 64, 64 * ch:64 * ch + 64], VsT[:, g, 64 * h:64 * h + 64], Pn[:, 64 * h:64 * h + 64],
                   False, False, [sb_["VsT"], sb_["Pn"]], [bf("num")], False)
                mm(denp[po:po + 64, 64 * ch:64 * ch + 64], ones[:, 0:64], Pn[:, 64 * h:64 * h + 64],
                   False, False, [CONST, sb_["Pn"]], [bf("den")], h == 7)
        ph("S att cache")
        for n in range(16):
            kb_ap, vb_ap = kcb[n % 2], vcb[n % 2]
            kbb, vbb = sb_["kc%d" % (n % 2)], sb_["vc%d" % (n % 2)]
            c.dma("pool", kb_ap.rearrange("p a b c -> p (a b c)"), d_kc[n], writes=[kbb])
            c.dma("pool", vb_ap.rearrange("p a b -> p (a b)"), d_vc[n], writes=[vbb])
            sp_, sbk = s_bank()
            sets = [(0, 0, 4 * n, 4, 0)] + [(1, 1 + i, 4 * n + i, 1, 4 + i) for i in range(4)] + \
                   [(2, 5 + i, 4 * n + i, 1, 8 + i) for i in range(4)]
            for h in range(8):
                ch, hh = h // 2, h % 2
                Q = QA if hh == 0 else QB
                for si, (g, s, q0, nq, k0) in enumerate(sets):
                    mm(sp_[:, 12 * h + k0:12 * h + k0 + nq], kb_ap[:, s, ch, :], Q[:, g, ch, q0:q0 + nq], True, True,
                       [kbb, ab["QA"], ab["QB"]], [sbk], h == 7 and si == 8)
            pi = rr["P"] % 3
            rr["P"] += 1
            P = Pb[:, pi, 0:96]
            c.op("act", lambda e, P=P, sp_=sp_: e.activation(P, sp_[:, 0:96], AF.Exp, scale=0.125),
                 reads=[sbk], writes=[pbufs[pi]])
            c.op("dve", lambda e, P=P: e.tensor_tensor(P, P, maskSc[:, :], ALU.mult),
                 reads=[pbufs[pi], CONST], writes=[pbufs[pi]])
            for h in range(8):
                ch, hh = h // 2, h % 2
                po = 64 * hh
                for si, (g, s, q0, nq, k0) in enumerate(sets):
                    last = (n == 15)
                    mm(nump[po:po + 64, 64 * ch + q0:64 * ch + q0 + nq], vb_ap[:, s, 64 * h:64 * h + 64],
                       P[:, 12 * h + k0:12 * h + k0 + nq], False, last, [vbb, pbufs[pi]], [bf("num")], False)
                    mm(denp[po:po + 64, 64 * ch + q0:64 * ch + q0 + nq], ones[:, 0:64],
                       P[:, 12 * h + k0:12 * h + k0 + nq], False, last, [CONST, pbufs[pi]], [bf("den")],
                       h == 7 and si == 8)
        c.op("dve", lambda e: e.reciprocal(rden[:, 0:256], denp[:, 0:256]), reads=[bf("den")], writes=[ab["rden"]])
        for ch in range(4):
            c.op("dve", lambda e, ch=ch: e.tensor_tensor(hT[:, 4 + ch, 0:N], nump[:, 64 * ch:64 * ch + 64],
                                                         rden[:, 64 * ch:64 * ch + 64], ALU.mult),
                 reads=[bf("num"), ab["rden"]], writes=[bf("hT%d" % (4 + ch))])
        ph("S pool")
        for g in range(4):
            w = 2 << g
            src, srcb = a_s[:, g, :, :], sb_["a_s"]
            off = 0
            bufs = [(p1, sb_["p1"]), (p2, sb_["p2"])]
            for k in range(g + 1):
                sh = 1 << k
                dst, dstb = bufs[k % 2]
                c.op("dve", lambda e, dst=dst, src=src, off=off, sh=sh: e.tensor_tensor(
                    dst[:, :, off + sh:19], src[:, :, off + sh:19], src[:, :, off:19 - sh], ALU.add),
                    reads=[srcb], writes=[dstb])
                src, srcb = dst, dstb
                off += sh
            c.op("dve", lambda e, src=src, g=g, w=w: e.scalar_tensor_tensor(
                plds[:, g, :].rearrange("p (n i) -> p n i", i=4), src[:, :, 15:19], 1.0 / w, a_s[:, g, :, 15:19],
                ALU.mult, ALU.subtract), reads=[srcb, sb_["a_s"]], writes=[sb_["pld"]])
            ps, pb2 = mm_bank()
            mm(ps[:, 0:N], poolw[:, g, :], plds[:, g, :], True, True, [CONST, sb_["pld"]], [pb2], True)
            c.op("act", lambda e, g=g, ps=ps: e.activation(hT[:, g, 0:N], ps[:, 0:N], AF.Identity,
                                                           scale=vecs[:, V_PSC + g:V_PSC + g + 1]),
                 reads=[pb2, CONST], writes=[bf("hT%d" % g)])
        stg = scr[:, 0:2, :].rearrange("p a b -> p (a b)")
        c.op("dve", lambda e: e.tensor_copy(stg[:, 0:960].rearrange("p (a b c) -> p a b c", a=4, b=16), a_s[:, :, :, 4:19]),
             reads=[sb_["a_s"]], writes=[bf("scr0"), bf("scr1")])
        c.dma("sp", o_poolsT, stg[:, 0:960], reads=[bf("scr0"), bf("scr1")], is_output=True)

        def ev_o(key, oc, ps, pb):
            ch = key[1] * 4 + oc
            c.op("dve", lambda e: e.tensor_tensor(rT[:, ch, 0:N], rT[:, ch, 0:N], ps, ALU.add),
                 reads=[pb, bf("rT%d" % ch)], writes=[bf("rT%d" % ch)])
            norm_acc(ch, N)
        ph("S wout")
        dump(1, "h", N)
        linear([("ev_out", j) for j in range(2)], h_rhs(N), 8, N, ev_o, keep=1)
        dump(2, "r", N)
        ffn_ple(0, N, pTs, d_psT[0])

    def _main_loop():
        for ti_, (kind, t) in enumerate(tiles):
            cur["i"] = ti_
            for ch_ in range(8):
                x_load_chunk(ti_, ch_)
            if kind != "sample":
                c.dma("sp", cs[:, :, :], d_cs[:, :, t * TT:(t + 1) * TT], writes=[bf("cs")])
                if kind == "full":
                    c.dma("pool", msk[:, :], d_mask[t - 4], writes=[bf("msk")])
                layer0_prompt(kind, t)
            else:
                c.dma("sp", cs[:, :, 0:NS], d_cs[:, :, NT * TT:NT * TT + NS], writes=[bf("cs")])
                layer0_sample()
            if kind == "kv":
                continue
            if kind == "full" and t == 4:
                layer1("halo", t, 2)
                continue
            if kind == "full":
                layer1("full", t, TT)
                final_out(TT, o_yT, (t - 5) * TT)
            else:
                layer1("sample", t, NS)
                final_out(NS, o_ysT, 0)

    try:
        _main_loop()
    except _Stop:
        pass
    ph("end")
    c.finish()
    c.close()
    return nc, c


_CACHE = {}


def _vec_pc(v):
    return np.ascontiguousarray(v.reshape(-1, 128).T)


def _shared_inputs(inp):
    f = np.float32
    sh = {}
    WB = _wblocks()
    wall = np.zeros((len(WB), 128, 4096), f)
    for i, (key, nm, l, c0, ncols, kcn) in enumerate(WB):
        W = np.asarray(inp[nm][l], dtype=f)[:, c0:c0 + ncols]
        wall[i, :, 0:kcn * ncols] = W.reshape(kcn, 128, ncols).transpose(1, 0, 2).reshape(128, kcn * ncols)
    sh["wall"] = wall
    sh["pool_w"] = np.ascontiguousarray(inp["ev_pool_w"][0].transpose(1, 0, 2), dtype=f)
    ws = np.asarray(inp["od_ws"][0], dtype=f)
    bs = np.asarray(inp["od_bs"][0], dtype=f)
    sh["od_wsT"] = np.ascontiguousarray(ws.transpose(2, 0, 1))
    vec = np.zeros((128, 80), f)
    vec[:, 0:8] = _vec_pc(inp["norm_mix"][0])
    vec[:, 8:16] = _vec_pc(inp["norm_mix"][1])
    vec[:, 16:24] = _vec_pc(inp["norm_ffn"][0])
    vec[:, 24:32] = _vec_pc(inp["norm_ffn"][1])
    vec[:, 32:40] = _vec_pc(inp["norm_ple"][0])
    vec[:, 40:48] = _vec_pc(inp["norm_ple"][1])
    vec[:, 48:56] = _vec_pc(inp["norm_final"])
    vec[:, 56:60] = _vec_pc(inp["ev_pool_scale"][0])
    vec[:, 60:64] = _vec_pc(inp["od_ln_g"][0])
    vec[:, 64:68] = _vec_pc(inp["od_ln_b"][0])
    cw = np.asarray(inp["od_conv_w"][0], dtype=f)
    for ch in range(4):
        for j in range(3):
            vec[:, 68 + 3 * ch + j] = cw[j, ch * 128:(ch + 1) * 128]
    sh["vecs"] = vec
    sh["bsT"] = np.ascontiguousarray(np.broadcast_to(np.tile(bs, (1, 4))[None], (128, 4, 512)), dtype=f)
    sh["bsS"] = np.ascontiguousarray(np.broadcast_to(np.tile(bs[:, 0:4], (1, 16))[None], (128, 4, 64)), dtype=f)
    sh["wsS"] = np.ascontiguousarray(np.broadcast_to(ws[:, 0:4, 0:4].reshape(4, 16)[None], (128, 4, 16)), dtype=f)
    sh["ident"] = np.eye(128, dtype=f)
    rot = np.zeros((128, 128), f)
    for m in range(128):
        k = m + 32 if (m % 64) < 32 else m - 32
        rot[k, m] = 1.0
    sh["rotm"] = rot
    sidx = np.arange(128)
    sh["trimask"] = (sidx[:, None] <= sidx[None, :]).astype(f)
    msc = np.ones((128, 96), f)
    for h in range(8):
        for k in range(4):
            msc[:, 12 * h + k] = (sidx >= k).astype(f)
    sh["maskSc"] = msc
    msn = np.zeros((128, 3, 512), f)
    row = np.arange(64)
    col = np.arange(64)
    same = (row[:, None] // 4) == (col[None, :] // 4)
    m0 = same & ((row[:, None] % 4) <= (col[None, :] % 4))
    m1 = same & ((row[:, None] % 4) == (col[None, :] % 4))
    for h in range(8):
        msn[0:64, 0, 64 * h:64 * h + 64] = m0
        msn[0:64, 1, 64 * h:64 * h + 64] = m1
        msn[0:64, 2, 64 * h:64 * h + 64] = m1
    sh["maskSn"] = msn
    return sh


def _rope_tables(pos):
    f = np.float32
    half = 32
    inv = np.power(f(10000.0), -np.arange(half, dtype=f) / f(half)).astype(f)
    ang = pos.astype(f)[None, :] * inv[:, None]
    cos = np.cos(ang).astype(f)
    sin = np.sin(ang).astype(f)
    p = np.arange(128)
    fi = p % 32
    sign = np.where((p % 64) < 32, -1.0, 1.0).astype(f)
    out = np.empty((128, 2, pos.shape[0]), f)
    out[:, 0, :] = cos[fi]
    out[:, 1, :] = sin[fi] * sign[:, None]
    return out


def _core_masks(s):
    f = np.float32
    k = np.arange(128)[:, None]
    q128 = np.arange(128)[None, :]
    prev_m = (k >= q128).astype(f)
    cur_m = (k <= q128).astype(f)
    out = np.zeros((5, 128, MASKW), f)
    for ti in range(5):
        tau = 4 + ti
        valid = lambda tp: 1.0 if (s - 2560 + 512 * tp) >= 0 else 0.0
        m = out[ti]
        for qb in range(4):
            m[:, (qb * 2) * 128:(qb * 2 + 1) * 128] = prev_m * (valid(tau - 1) if qb == 0 else 1.0)
            m[:, (qb * 2 + 1) * 128:(qb * 2 + 2) * 128] = cur_m
        for r4 in range(4):
            m[:, 1024 + (r4 * 2) * 128:1024 + (r4 * 2 + 1) * 128] = prev_m * valid(tau - 1)
            m[:, 1024 + (r4 * 2 + 1) * 128:1024 + (r4 * 2 + 2) * 128] = cur_m
        i32 = np.arange(32)[None, :]
        blk = np.zeros((128, 32), f)
        for slot in range(4):
            tp = [x for x in range(tau - 4, tau) if x % 4 == slot][0]
            u = np.arange(32)[:, None]
            rel = 32 * (tau - tp) + i32 - u
            blk[32 * slot:32 * slot + 32, :] = ((rel <= 128) & (rel >= 0)).astype(f) * valid(tp)
        for r in range(16):
            m[:, 2048 + 32 * r:2048 + 32 * r + 32] = blk
        kk = np.arange(128)[:, None]
        for r4 in range(4):
            for rho in range(4):
                mb = ((kk % 4) == rho) & ((kk // 4) <= i32)
                c0 = 2560 + (4 * r4 + rho) * 32
                m[:, c0:c0 + 32] = mb.astype(f)
    return out


def _core_icnt(s):
    f = np.float32
    out = np.ones((5, 128, 4, TT), f)
    for ti in range(5):
        pos = s - 2560 + 512 * (4 + ti) + np.arange(TT)
        for g in range(4):
            w = 2 << g
            cnt = np.minimum(w, np.maximum(pos, 0) + 1).astype(f)
            out[ti, :, g, :] = (f(1.0) / cnt)[None, :]
    return out


def _gather_cache(inp, n):
    f = np.float32
    kc = np.empty((128, 9, 4, 128), f)
    vc = np.empty((128, 9, 512), f)
    sets = [(inp["cache_kv_w128"], np.arange(128))]
    for i in range(4):
        sets.append((inp["cache_kv_w512"], i + 4 * np.arange(128)))
    for i in range(4):
        sets.append((inp["cache_kv_w2048"], i + 16 * np.arange(128)))
    for si, (arr, rows) in enumerate(sets):
        blk = np.asarray(arr[0, n, rows], dtype=f)
        kk = blk[:, 0].reshape(128, 512)
        vv = blk[:, 1].reshape(128, 512)
        kc[:, si, :, :] = kk.T.reshape(4, 128, 128).transpose(1, 0, 2)
        vc[:, si, :] = vv
    return kc.reshape(128, 9 * 4 * 128), vc.reshape(128, 9 * 512)


def kernel(**inp):
    f = np.float32
    inp = {k: np.asarray(v) for k, v in inp.items()}
    if "prog" not in _CACHE:
        _CACHE["prog"] = build_program()
    nc, _ctx = _CACHE["prog"]
    sh = _shared_inputs(inp)
    xp = np.asarray(inp["x_prompt"], dtype=f)
    pp = np.asarray(inp["p_prompt"], dtype=f)
    xs = np.asarray(inp["x_sample"], dtype=f)
    ps_ = np.asarray(inp["p_sample"], dtype=f)
    in_maps = []
    for core in (_DBG.get("cores") or range(NCORES)):
        b, s = core // 4, (core % 4) * SEG
        m = dict(sh)
        tok = s - 2560 + np.arange(NT * TT)
        xT = np.zeros((1024, NT * TT), f)
        ok = tok >= 0
        xT[:, ok] = xp[b, tok[ok]].T
        m["xT"] = xT
        tokp = s - 512 + np.arange(5 * TT)
        pT = np.zeros((2, 256, 5 * TT), f)
        okp = tokp >= 0
        pT[:, :, okp] = pp[:, b, tokp[okp]].transpose(0, 2, 1)
        m["pT"] = pT
        pos = np.concatenate([np.maximum(tok, 0), 2048 + (np.arange(NS) % 4)])
        m["cs"] = _rope_tables(pos)
        m["mask"] = _core_masks(s)
        m["icnt"] = _core_icnt(s)
        n0 = 16 * core
        m["xsT"] = np.ascontiguousarray(xs[n0:n0 + 16].reshape(NS, 1024).T)
        m["psT"] = np.ascontiguousarray(ps_[:, n0:n0 + 16].reshape(2, NS, 256).transpose(0, 2, 1))
        kcs, vcs = [], []
        for j in range(16):
            a, b_ = _gather_cache(inp, n0 + j)
            kcs.append(a)
            vcs.append(b_)
        m["kc"] = np.stack(kcs)
        m["vc"] = np.stack(vcs)
        sp = np.asarray(inp["state_pool"][0, n0:n0 + 16], dtype=f)
        m["spool"] = np.ascontiguousarray(sp.reshape(16, 15, 4, 128).transpose(3, 2, 0, 1)).reshape(128, 960)
        sc = np.asarray(inp["state_conv"][0, n0:n0 + 16], dtype=f)
        m["sconv"] = np.ascontiguousarray(sc.reshape(16, 2, 4, 128).transpose(3, 2, 0, 1)).reshape(128, 128)
        in_maps.append(m)
    if _DBG.get("cores"):
        res = run_bass_kernel_spmd(nc, in_maps, core_ids=list(range(len(in_maps))))
        return res.results
    res = run_bass_kernel_spmd(nc, in_maps, core_ids=list(range(NCORES)))
    R = res.results
    y_prompt = np.empty((2, SEQ, 1024), f)
    y_sample = np.empty((128, 4, 1024), f)
    kvp = [np.empty((1, 2, w, 2, 8, 64), f) for w in (128, 512, 2048)]
    kvs = [np.empty((1, 128, 4, 2, 8, 64), f) for _ in range(3)]
    pool_p = np.empty((1, 2, 15, 512), f)
    pool_s = np.empty((1, 128, 15, 512), f)
    conv_p = np.empty((1, 2, 2, 512), f)
    conv_s = np.empty((1, 128, 2, 512), f)
    cv_s = np.empty((1, 128, 4, 512), f)
    for core in range(NCORES):
        r = R[core]
        b, s = core // 4, (core % 4) * SEG
        n0 = 16 * core
        y_prompt[b, s:s + SEG] = r["o_yT"].T
        y_sample[n0:n0 + 16] = r["o_ysT"].T.reshape(16, 4, 1024)
        for g in range(3):
            t = r["o_kvsT"][g]
            kvs[g][0, n0:n0 + 16] = t.transpose(2, 0, 1).reshape(16, 4, 2, 8, 64)
        pool_s[0, n0:n0 + 16] = r["o_poolsT"].reshape(128, 4, 16, 15).transpose(2, 3, 1, 0).reshape(16, 15, 512)
        conv_s[0, n0:n0 + 16] = r["o_convsT"].reshape(128, 4, 16, 2).transpose(2, 3, 1, 0).reshape(16, 2, 512)
        cv_s[0, n0:n0 + 16] = r["o_cvsT"].T.reshape(16, 4, 512)
        if core % 4 == 3:
            for g, w in enumerate((128, 512, 2048)):
                t = r["o_kvT"][g][:, :, SEG - w:]
                kvp[g][0, b] = t.transpose(2, 0, 1).reshape(w, 2, 8, 64)
            pool_p[0, b] = r["o_poolT"].transpose(2, 1, 0).reshape(15, 512)
            conv_p[0, b] = r["o_convT"].transpose(2, 1, 0).reshape(2, 512)
    return (y_prompt, y_sample, kvp[0], kvp[1], kvp[2], kvs[0], kvs[1], kvs[2],
            pool_p, pool_s, conv_p, conv_s, cv_s)
```
